# Optimizing a Trainium2 kernel written in Bass

```python
import jax, jax.numpy as jnp
from jax import lax
import numpy as np

D_MODEL = 1024
BATCH = 4
SEQ = 4096
DEPTH = 2

GRID_W = 64
CTX_LEN = 256
N_MIXERS = 2
D_FF = 2816
NORM_EPS = 1e-6
CONV_W = 4
CONV_LEFT = CONV_W // 2
D_RNN = 1024
LRU_HEADS = 4
LRU_BW = D_RNN // LRU_HEADS
LRU_C = 8.0
DN_HEADS = 8
DN_DK = 128
DN_DV = 128
DN_CHUNK = 64
DN_QK = DN_HEADS * DN_DK
DN_VW = DN_HEADS * DN_DV
DN_QKV = 2 * DN_QK + DN_VW
DN_PROJ = DN_QKV + DN_VW + 4 * DN_HEADS

kernel_name = "hybrid_rglru_gdn_prefix_trunk"

F32 = jnp.float32


def rms_norm(x, g):
    xf = x.astype(F32)
    y = xf * lax.rsqrt(jnp.mean(xf * xf, axis=-1, keepdims=True) + NORM_EPS)
    return (y * g.astype(F32)).astype(x.dtype)


def l2norm(x):
    xf = x.astype(F32)
    return xf * lax.rsqrt(jnp.sum(xf * xf, axis=-1, keepdims=True) + NORM_EPS)


def adaln(x, g, m):
    return rms_norm(x, g) * (1.0 + m[:, 1]) + m[:, 0]


def swiglu(h, w_in, w_out):
    gu = h @ w_in
    return (jax.nn.silu(gu[..., :D_FF]) * gu[..., D_FF:]) @ w_out


def centred_conv(u, w, b=None):
    t = u.shape[1]
    up = jnp.pad(u, ((0, 0), (CONV_LEFT, CONV_W - 1 - CONV_LEFT), (0, 0)))
    out = up[:, 0:t] * w[0]
    for j in range(1, CONV_W):
        out = out + up[:, j:j + t] * w[j]
    if b is not None:
        out = out + b
    return out


def to_col_major(h, rows):
    b, s, d = h.shape
    return h.reshape(b, rows, GRID_W, d).transpose(0, 2, 1, 3).reshape(b, s, d)


def from_col_major(h, rows):
    b, s, d = h.shape
    return h.reshape(b, GRID_W, rows, d).transpose(0, 2, 1, 3).reshape(b, s, d)


def linear_scan(a, b, h0):
    b = b.at[:, 0].add(a[:, 0] * h0)

    def combine(l, r):
        return (l[0] * r[0], r[0] * l[1] + r[1])

    _, h = lax.associative_scan(combine, (a, b), axis=1)
    return h


def lru_dir_gates(u, w_gate, b_gate, lam):
    bsz, t, _ = u.shape
    gates = jnp.einsum('bthi,ghij->gbthj', u.reshape(bsz, t, LRU_HEADS, LRU_BW), w_gate)
    gates = jax.nn.sigmoid((gates.reshape(2, bsz, t, D_RNN) + b_gate[:, None, None, :]).astype(F32))
    log_a = -LRU_C * gates[0] * jax.nn.softplus(-lam.astype(F32))
    a = jnp.exp(log_a)
    bterm = jnp.sqrt(-jnp.expm1(2.0 * log_a)) * gates[1] * u.astype(F32)
    return a, bterm


def lru_bidir(u, w_gate, b_gate, lam, h0_f, h0_b):
    a_f, b_f = lru_dir_gates(u, w_gate[0], b_gate[0], lam[0])
    h_f = linear_scan(a_f, b_f, h0_f)
    a_b, b_b = lru_dir_gates(jnp.flip(u, 1), w_gate[1], b_gate[1], lam[1])
    h_b_rev = linear_scan(a_b, b_b, h0_b)
    return h_f + jnp.flip(h_b_rev, 1), h_f[:, -1], h_b_rev[:, -1]


def rglru_mixer(h_lat, h_ctx, w_in, conv_w, conv_b, w_gate, b_gate, lam, w_out, ctx_out):
    def branch_in(h):
        p = h @ w_in
        return p[..., :D_RNN], centred_conv(p[..., D_RNN:], conv_w, conv_b)

    def readout(y, r):
        return (jax.nn.gelu(y.astype(F32)) * r) @ w_out

    y_c, u_c = branch_in(h_ctx)
    h0 = jnp.zeros((h_ctx.shape[0], D_RNN), F32)
    r_c, s_f, s_b = lru_bidir(u_c, w_gate, b_gate, lam, h0, h0)
    y_l, u_l = branch_in(h_lat)
    r_l, _, _ = lru_bidir(u_l, w_gate, b_gate, lam, s_f, s_b)
    out = readout(y_l, r_l).astype(h_lat.dtype)
    out_c = readout(y_c, r_c).astype(h_ctx.dtype) if ctx_out else None
    return out, out_c


def chunk_gated_delta(q, k, v, g, beta, s0):
    bsz, t, h, _ = q.shape
    dv = v.shape[-1]
    c = DN_CHUNK
    n = t // c

    def chunks(a):
        a = a.reshape((bsz, n, c, h) + a.shape[3:])
        return jnp.moveaxis(jnp.moveaxis(a, 1, 0), 3, 2)

    q, k, v = (chunks(a.astype(F32)) for a in (q, k, v))
    g = chunks(g.astype(F32))
    beta = chunks(beta.astype(F32))
    g_cum = jnp.cumsum(g, axis=-1)
    idx = jnp.arange(c)
    incl = idx[:, None] >= idx[None, :]
    strict = idx[:, None] > idx[None, :]
    decay = jnp.exp(jnp.where(incl, g_cum[..., :, None] - g_cum[..., None, :], -jnp.inf))
    kb = k * beta[..., None]
    a_mat = jnp.where(strict, jnp.einsum('nbhcd,nbhed->nbhce', kb, k) * decay, 0.0) + jnp.eye(c, dtype=F32)
    rhs = jnp.concatenate([v * beta[..., None], kb * jnp.exp(g_cum)[..., None]], axis=-1)
    sol = lax.linalg.triangular_solve(a_mat, rhs, left_side=True, lower=True, unit_diagonal=True)
    u, w = sol[..., :dv], sol[..., dv:]
    qk = jnp.einsum('nbhcd,nbhed->nbhce', q, k) * decay
    q_dec = q * jnp.exp(g_cum)[..., None]
    k_dec = k * jnp.exp(g_cum[..., -1:] - g_cum)[..., None]
    g_tot = jnp.exp(g_cum[..., -1])

    def step(s, inp):
        qk_c, qd_c, kd_c, u_c, w_c, gt_c = inp
        v_new = u_c - jnp.einsum('bhcd,bhde->bhce', w_c, s)
        o = jnp.einsum('bhcd,bhde->bhce', qd_c, s) + jnp.einsum('bhce,bhef->bhcf', qk_c, v_new)
        s = s * gt_c[..., None, None] + jnp.einsum('bhcd,bhce->bhde', kd_c, v_new)
        return s, o

    s_fin, o = lax.scan(step, s0, (qk, q_dec, k_dec, u, w, g_tot))
    o = o.transpose(1, 0, 3, 2, 4).reshape(bsz, t, h, dv)
    return o, s_fin


def dn_inputs(h, w_in, conv_w, a_log, dt_bias):
    bsz, t, _ = h.shape
    proj = h @ w_in
    qkv = jax.nn.silu(centred_conv(proj[..., :DN_QKV], conv_w))
    q = l2norm(qkv[..., :DN_QK].reshape(bsz, t, DN_HEADS, DN_DK)) * (DN_DK ** -0.5)
    k = l2norm(qkv[..., DN_QK:2 * DN_QK].reshape(bsz, t, DN_HEADS, DN_DK))
    v = qkv[..., 2 * DN_QK:].reshape(bsz, t, DN_HEADS, DN_DV).astype(F32)
    z = proj[..., DN_QKV:DN_QKV + DN_VW].reshape(bsz, t, DN_HEADS, DN_DV)
    ab = proj[..., DN_QKV + DN_VW:].astype(F32).reshape(bsz, t, 2, 2, DN_HEADS)
    g = -jnp.exp(a_log.astype(F32)) * jax.nn.softplus(ab[:, :, 0] + dt_bias.astype(F32))
    beta = jax.nn.sigmoid(ab[:, :, 1])
    return q, k, v, z, g, beta


def dn_bidir(q, k, v, g, beta, s0_f, s0_b):
    o_f, s_f = chunk_gated_delta(q, k, v, g[:, :, 0], beta[:, :, 0], s0_f)
    fl = lambda a: jnp.flip(a, 1)
    o_b, s_b = chunk_gated_delta(fl(q), fl(k), fl(v), fl(g[:, :, 1]), fl(beta[:, :, 1]), s0_b)
    return o_f + fl(o_b), s_f, s_b


def deltanet_mixer(h_lat, h_ctx, w_in, conv_w, a_log, dt_bias, g_norm, w_out, ctx_out):
    def readout(o, z):
        y = rms_norm(o, g_norm) * jax.nn.silu(z.astype(F32))
        return y.reshape(o.shape[0], o.shape[1], DN_VW) @ w_out

    qc, kc, vc, zc, gc, bc = dn_inputs(h_ctx, w_in, conv_w, a_log, dt_bias)
    s0 = jnp.zeros((h_ctx.shape[0], DN_HEADS, DN_DK, DN_DV), F32)
    oc, s_f, s_b = dn_bidir(qc, kc, vc, gc, bc, s0, s0)
    q, k, v, z, g, beta = dn_inputs(h_lat, w_in, conv_w, a_log, dt_bias)
    o, _, _ = dn_bidir(q, k, v, g, beta, s_f, s_b)
    out = readout(o, z).astype(h_lat.dtype)
    out_c = readout(oc, zc).astype(h_ctx.dtype) if ctx_out else None
    return out, out_c


def setup_inputs(seed: int = 0) -> dict:
    key = jax.random.key(seed)
    ks = jax.random.split(key, 24)
    n_a = (DEPTH + 1) // 2
    n_b = DEPTH // 2
    nrm = lambda k, shape, fan_in, gain=1.0: gain * jax.random.normal(k, shape, F32) * (fan_in ** -0.5)
    a_base = jax.random.uniform(ks[13], (n_a, 2, D_RNN), F32, 0.9, 0.999) ** (1.0 / LRU_C)
    dt = jnp.exp(jax.random.uniform(ks[17], (n_b, 2, DN_HEADS), F32, np.log(1e-3), np.log(1e-1)))
    return {
        "x": jax.random.normal(ks[0], (BATCH, SEQ, D_MODEL), F32),
        "c": jax.random.normal(ks[1], (BATCH, D_MODEL), F32),
        "ctx": jax.random.normal(ks[2], (BATCH, CTX_LEN, D_MODEL), F32),
        "c_ctx": jax.random.normal(ks[3], (D_MODEL,), F32),
        "w_ada": nrm(ks[4], (DEPTH, D_MODEL, 9 * D_MODEL), D_MODEL, 0.5),
        "b_ada": 0.02 * jax.random.normal(ks[5], (DEPTH, 9 * D_MODEL), F32),
        "g_sub": 1.0 + 0.05 * jax.random.normal(ks[6], (DEPTH, 3, D_MODEL), F32),
        "ffn_w_in": nrm(ks[7], (DEPTH, 2, D_MODEL, 2 * D_FF), D_MODEL),
        "ffn_w_out": nrm(ks[8], (DEPTH, 2, D_FF, D_MODEL), D_FF),
        "lru_w_in": nrm(ks[9], (n_a, D_MODEL, 2 * D_RNN), D_MODEL),
        "lru_conv_w": nrm(ks[10], (n_a, CONV_W, D_RNN), CONV_W),
        "lru_conv_b": 0.02 * jax.random.normal(ks[11], (n_a, D_RNN), F32),
        "lru_w_gate": nrm(ks[12], (n_a, 2, 2, LRU_HEADS, LRU_BW, LRU_BW), LRU_BW),
        "lru_b_gate": 0.02 * jax.random.normal(ks[14], (n_a, 2, 2, D_RNN), F32),
        "lru_lambda": jnp.log(a_base) - jnp.log1p(-a_base),
        "lru_w_out": nrm(ks[15], (n_a, D_RNN, D_MODEL), D_RNN),
        "dn_w_in": nrm(ks[16], (n_b, D_MODEL, DN_PROJ), D_MODEL),
        "dn_conv_w": nrm(ks[18], (n_b, CONV_W, DN_QKV), CONV_W),
        "dn_a_log": jnp.log(jax.random.uniform(ks[19], (n_b, 2, DN_HEADS), F32, 1.0, 16.0)),
        "dn_dt_bias": dt + jnp.log(-jnp.expm1(-dt)),
        "dn_g_norm": 1.0 + 0.05 * jax.random.normal(ks[20], (n_b, DN_DV), F32),
        "dn_w_out": nrm(ks[21], (n_b, DN_VW, D_MODEL), DN_VW),
        "g_final": 1.0 + 0.05 * jax.random.normal(ks[22], (D_MODEL,), F32),
    }


def reference(x, c, ctx, c_ctx, w_ada, b_ada, g_sub, ffn_w_in, ffn_w_out,
              lru_w_in, lru_conv_w, lru_conv_b, lru_w_gate, lru_b_gate, lru_lambda, lru_w_out,
              dn_w_in, dn_conv_w, dn_a_log, dn_dt_bias, dn_g_norm, dn_w_out, g_final):
    bsz, seq, d = x.shape
    rows = seq // GRID_W
    c_silu = jax.nn.silu(c)
    cc_silu = jax.nn.silu(c_ctx)[None]
    xc = ctx
    for i in range(DEPTH):
        last = i == DEPTH - 1
        j = i // N_MIXERS
        m = (c_silu @ w_ada[i] + b_ada[i]).reshape(bsz, 3, 3, 1, d)
        mc = (cc_silu @ w_ada[i] + b_ada[i]).reshape(1, 3, 3, 1, d)
        x = x + 0.5 * m[:, 0, 2] * swiglu(adaln(x, g_sub[i, 0], m[:, 0]), ffn_w_in[i, 0], ffn_w_out[i, 0])
        xc = xc + 0.5 * mc[:, 0, 2] * swiglu(adaln(xc, g_sub[i, 0], mc[:, 0]), ffn_w_in[i, 0], ffn_w_out[i, 0])
        h = adaln(x, g_sub[i, 1], m[:, 1])
        hc = adaln(xc, g_sub[i, 1], mc[:, 1])
        if i % N_MIXERS == 0:
            o, oc = rglru_mixer(h, hc, lru_w_in[j], lru_conv_w[j], lru_conv_b[j], lru_w_gate[j],
                                lru_b_gate[j], lru_lambda[j], lru_w_out[j], not last)
        else:
            o, oc = deltanet_mixer(to_col_major(h, rows), hc, dn_w_in[j], dn_conv_w[j], dn_a_log[j],
                                   dn_dt_bias[j], dn_g_norm[j], dn_w_out[j], not last)
            o = from_col_major(o, rows)
        x = x + m[:, 1, 2] * o
        x = x + 0.5 * m[:, 2, 2] * swiglu(adaln(x, g_sub[i, 2], m[:, 2]), ffn_w_in[i, 1], ffn_w_out[i, 1])
        if not last:
            xc = xc + mc[:, 1, 2] * oc
            xc = xc + 0.5 * mc[:, 2, 2] * swiglu(adaln(xc, g_sub[i, 2], mc[:, 2]), ffn_w_in[i, 1], ffn_w_out[i, 1])
    return rms_norm(x, g_final)
```

```python
import contextlib
import math
import numpy as np
import concourse.bass as bass
import concourse.mybir as mybir
from concourse.bass_utils import run_bass_kernel_spmd

F32 = mybir.dt.float32
BF16 = mybir.dt.bfloat16
F32R = mybir.dt.float32r
AF = mybir.ActivationFunctionType
ALU = mybir.AluOpType

D = 1024
SEQ = 4096
CTX = 256
T = SEQ + CTX
DFF = 2816
NF = DFF // 128
EPS = 1e-6
CTX0 = 2
LAT0 = CTX0 + CTX + 3
TP = LAT0 + SEQ + 1
NEG = -30000.0

C_C, C_CC, C_BADA, C_GSUB, C_LCW, C_LCB, C_LBG, C_LAM, C_DCW, C_GN = 0, 8, 16, 160, 208, 240, 248, 280, 296, 392
NCOLS = 512


class K:
    def __init__(self, nc, es):
        self.nc = nc
        self.es = es
        self.eng = {"pe": nc.tensor, "dve": nc.vector, "act": nc.scalar, "pool": nc.gpsimd, "sp": nc.sync}
        self.ce = ["pe", "dve", "act", "pool"]
        self.EPOCH = 30000
        self.cnt = {e: 0 for e in self.ce}
        self.ep = {e: 0 for e in self.ce}
        self.sems = {e: [es.enter_context(nc.semaphore(f"s_{e}_0"))] for e in self.ce}
        self.KD = 8
        self.dq = ["sp", "pool"]
        self.dsem = {q: [es.enter_context(nc.semaphore(f"d_{q}_{i}")) for i in range(self.KD)] for q in self.dq}
        self.dn = {q: 0 for q in self.dq}
        self.seen = {e: {} for e in self.eng}
        self.st = {}

    def _wait(self, e, tok):
        if tok is None:
            return
        if tok[0] == "c":
            _, f, ep, idx = tok
            if f == e and e == "pe":
                return
            key = (f, ep)
            if self.seen[e].get(key, 0) >= idx:
                return
            self.eng[e].wait_ge(self.sems[f][ep], idx)
            self.seen[e][key] = idx
        else:
            _, q, si, cnt = tok
            key = ("d", q, si)
            if self.seen[e].get(key, 0) >= cnt:
                return
            self.eng[e].wait_ge(self.dsem[q][si], cnt)
            self.seen[e][key] = cnt

    def _deps(self, e, r, w):
        for k in r:
            s = self.st.get(k)
            if s:
                self._wait(e, s[0])
        for k in w:
            s = self.st.get(k)
            if s:
                self._wait(e, s[0])
                for t in s[1].values():
                    self._wait(e, t)

    def _record(self, tok, r, w, rid):
        for k in w:
            self.st[k] = [tok, {}]
        for k in r:
            s = self.st.setdefault(k, [None, {}])
            s[1][rid] = tok

    def op(self, e, fn, r=(), w=()):
        self._deps(e, r, w)
        inst = fn()
        if self.cnt[e] >= self.EPOCH:
            self.ep[e] += 1
            self.cnt[e] = 0
            self.sems[e].append(self.es.enter_context(self.nc.semaphore(f"s_{e}_{self.ep[e]}")))
        self.cnt[e] += 1
        inst.then_inc(self.sems[e][self.ep[e]], 1)
        tok = ("c", e, self.ep[e], self.cnt[e])
        self._record(tok, r, w, e)
        return tok

    def dma(self, q, out, in_, r=(), w=()):
        n = self.dn[q]
        si = n % self.KD
        cnt = 16 * (n // self.KD + 1)
        if cnt > 16:
            self._wait(q, ("d", q, si, cnt - 16))
        self._deps(q, r, w)
        self.eng[q].dma_start(out=out, in_=in_).then_inc(self.dsem[q][si], 16)
        self.dn[q] = n + 1
        tok = ("d", q, si, cnt)
        self._record(tok, r, w, ("d", q, si))
        return tok

    def barrier(self):
        toks = []
        for f in self.ce:
            if self.cnt[f] > 0:
                toks.append(("c", f, self.ep[f], self.cnt[f]))
        for q in self.dq:
            n = self.dn[q]
            for si in range(self.KD):
                uses = (n - si + self.KD - 1) // self.KD if n > si else 0
                if uses > 0:
                    toks.append(("d", q, si, 16 * uses))
        for e in self.eng:
            for t in toks:
                if t[0] == "c" and t[1] == e:
                    continue
                self._wait(e, t)
        self.st = {}


def build(stage=99, dbg=False):
    nc = bass.Bass("TRN2", target_bir_lowering=False)
    dt = lambda name, shape, dtype=F32, kind="ExternalInput": nc.dram_tensor(name, shape, dtype, kind=kind).ap()
    x_in = dt("x", [SEQ, D])
    ctx_in = dt("ctx", [CTX, D])
    cols_src = dt("cols_src", [NCOLS, 128])
    rows_src = dt("rows_src", [1, 1056])
    w_ada = dt("w_ada", [2, D, 9 * D])
    ffn_w_in = dt("ffn_w_in", [4, D, 2 * DFF])
    ffn_w_out = dt("ffn_w_out", [4, DFF, D])
    lru_w_in = dt("lru_w_in", [D, 2048])
    lru_w_gate = dt("lru_w_gate", [16, 256, 256])
    lru_w_out = dt("lru_w_out", [D, D])
    dn_w_in = dt("dn_w_in", [D, 4128])
    dn_w_out = dt("dn_w_out", [D, D])
    out = dt("out", [SEQ, D], kind="ExternalOutput")
    X = dt("Xs", [T, D], kind="Internal")
    PT = dt("PTs", [16, 128, TP], kind="Internal")
    ZT = dt("ZTs", [8, 128, T], BF16, kind="Internal")
    QP = dt("QPs", [24, 128, TP], kind="Internal")
    ZD = dt("ZDs", [8, 128, T], kind="Internal")
    GD = dt("GDs", [T, 16], kind="Internal")
    BD = dt("BDs", [T, 16], kind="Internal")
    QT = dt("QTs", [8, 128, T], kind="Internal")
    KT = dt("KTs", [8, 128, T], kind="Internal")
    KTOK = dt("KTOKs", [8, T, 128], kind="Internal")
    VTOK = dt("VTOKs", [8, T, 128], kind="Internal")
    OFB = [dt("OFs", [8, 128, SEQ], kind="Internal"), dt("OBs", [8, 128, SEQ], kind="Internal")]
    Xlat_cm = X[CTX:T, :].rearrange("(r c) d -> c r d", c=64)

    with contextlib.ExitStack() as es:
        k = K(nc, es)
        op, dma = k.op, k.dma
        V, A, PE, PO = nc.vector, nc.scalar, nc.tensor, nc.gpsimd

        uid = [0]

        def sb(st, name, shape, dtype=F32):
            uid[0] += 1
            return st.enter_context(nc.sbuf_tensor(f"{name}_u{uid[0]}", shape, dtype))

        PS = [es.enter_context(nc.psum_tensor(f"ps{i}", [128, 512], F32)) for i in range(8)]
        pk = [("ps", i) for i in range(8)]

        ident = sb(es, "ident", [128, 128])
        ones = sb(es, "ones", [128, 128])
        iot = sb(es, "iot", [128, 128])
        cols = sb(es, "cols", [128, NCOLS])
        mcol = sb(es, "mcol", [128, 2, 72, 2])
        gsall = sb(es, "gsall", [128, 2, 3, 2, 8])
        rowsb = sb(es, "rowsb", [128, 1056])
        op("pool", lambda: PO.iota(iot[:], pattern=[[1, 128]], base=0, channel_multiplier=-1,
                                   allow_small_or_imprecise_dtypes=True), w=["iot"])
        op("dve", lambda: V.tensor_scalar(out=ident[:], in0=iot[:], scalar1=0.0, scalar2=None, op0=ALU.is_equal),
           r=["iot"], w=["ident"])
        op("dve", lambda: V.memset(ones[:], 1.0), w=["ones"])

        with contextlib.ExitStack() as ph:
            stg = sb(ph, "stg", [128, 4, 128])
            rows1 = sb(ph, "rows1", [1, 1056])
            sc = sb(ph, "sc", [128, 8, 2])
            was = [sb(ph, f"wa{i}", [128, 8, 512]) for i in range(2)]
            dma("sp", stg[:, :, :], cols_src.rearrange("(g p) f -> p g f", p=128), w=["stg"])
            dma("sp", rows1[:, :], rows_src[:, :], w=["rows1"])
            for g in range(4):
                op("pe", lambda g=g: PE.transpose(out=PS[0][:, g * 128:(g + 1) * 128], in_=stg[:, g, :], identity=ident[:]),
                   r=["stg", "ident"], w=[pk[0]])
            op("dve", lambda: V.tensor_copy(out=cols[:], in_=PS[0][:, :]), r=[pk[0]], w=["cols"])
            for i, (a, b) in enumerate([(0, 512), (512, 1024), (1024, 1056)]):
                op("pe", lambda a=a, b=b, i=i: PE.matmul(PS[1 + i][:, 0:b - a], lhsT=ones[0:1, :], rhs=rows1[0:1, a:b],
                                                         start=True, stop=True), r=["ones", "rows1"], w=[pk[1 + i]])
                op("act", lambda a=a, b=b, i=i: A.copy(out=rowsb[:, a:b], in_=PS[1 + i][:, 0:b - a]), r=[pk[1 + i]], w=["rowsb"])
            op("act", lambda: A.activation(out=sc[:, :, 0], in_=cols[:, C_C:C_C + 8], func=AF.Silu), r=["cols"], w=["sc"])
            op("act", lambda: A.activation(out=sc[:, :, 1], in_=cols[:, C_CC:C_CC + 8], func=AF.Silu), r=["cols"], w=["sc"])
            it = 0
            for l in range(2):
                for cg in range(18):
                    wa = was[it % 2]
                    wk = ("wa", it % 2)
                    pz = 4 + (it % 2)
                    dma("sp", wa[:, :, :], w_ada[l][:, cg * 512:(cg + 1) * 512].rearrange("(k p) c -> p k c", p=128), w=[wk])
                    for cc in range(4):
                        for kk in range(8):
                            op("pe", lambda cc=cc, kk=kk, wa=wa, pz=pz: PE.matmul(
                                PS[pz][:, cc * 2:cc * 2 + 2], lhsT=wa[:, kk, cc * 128:(cc + 1) * 128], rhs=sc[:, kk, :],
                                start=(kk == 0), stop=(kk == 7)), r=[wk, "sc"], w=[pk[pz]])
                    op("dve", lambda l=l, cg=cg, pz=pz: V.tensor_tensor(
                        out=mcol[:, l, cg * 4:(cg + 1) * 4, :], in0=PS[pz][:, 0:8].rearrange("p (c v) -> p c v", v=2),
                        in1=cols[:, C_BADA + l * 72 + cg * 4:C_BADA + l * 72 + cg * 4 + 4].unsqueeze(2).broadcast_to([128, 4, 2]),
                        op=ALU.add), r=[pk[pz], "cols"], w=["mcol"])
                    it += 1
            for l in range(2):
                for s in range(3):
                    for v in range(2):
                        op("dve", lambda l=l, s=s, v=v: V.scalar_tensor_tensor(
                            out=gsall[:, l, s, v, :], in0=mcol[:, l, (s * 3 + 1) * 8:(s * 3 + 2) * 8, v], scalar=1.0,
                            in1=cols[:, C_GSUB + (l * 3 + s) * 8:C_GSUB + (l * 3 + s) * 8 + 8], op0=ALU.add, op1=ALU.mult),
                           r=["mcol", "cols"], w=["gsall"])
            k.barrier()

        def sh_ap(l, s, v):
            return mcol[:, l, (s * 3) * 8:(s * 3) * 8 + 8, v]

        def gate_ap(l, s, v):
            return mcol[:, l, (s * 3 + 2) * 8:(s * 3 + 2) * 8 + 8, v]

        def gs_ap(l, s, v):
            return gsall[:, l, s, v, :]

        def bcast_row(dst, dkey, col_ap, factor, dg):
            for j in range(8):
                op("dve", lambda j=j: V.tensor_scalar(out=dg[:, :], in0=ident[:], scalar1=col_ap[:, j:j + 1], scalar2=float(factor),
                                                      op0=ALU.mult, op1=ALU.mult), r=["ident", "mcol"], w=["dg"])
                op("pe", lambda j=j: PE.matmul(PS[7][:, 0:128], lhsT=ones[:, :], rhs=dg[:, :], start=True, stop=True),
                   r=["dg", "ones"], w=[pk[7]])
                op("act", lambda j=j: A.copy(out=dst[:, j * 128:(j + 1) * 128], in_=PS[7][:, 0:128]), r=[pk[7]], w=[dkey])

        def make_ln(ph):
            S = {"junk": [sb(ph, f"lnj{i}", [128, 1024]) for i in range(2)],
                 "xs": [sb(ph, f"lnx{i}", [128, 1024]) for i in range(2)],
                 "st": [sb(ph, f"lns{i}", [128, 4]) for i in range(2)], "n": 0}

            def ln_A(xap, xkey):
                i = S["n"] % 2
                S["n"] += 1
                junk, xs, st = S["junk"][i], S["xs"][i], S["st"][i]
                kj, kx, ks = ("lnj", i), ("lnx", i), ("lns", i)
                op("act", lambda: A.activation(out=junk[:], in_=xap, func=AF.Square, accum_out=st[:, 0:1]), r=[xkey], w=[kj, ks])
                op("act", lambda: A.activation(out=st[:, 1:2], in_=st[:, 0:1], func=AF.Ln, bias=EPS, scale=1.0 / D), r=[ks], w=[ks])
                op("act", lambda: A.activation(out=st[:, 2:3], in_=st[:, 1:2], func=AF.Exp, scale=-0.5), r=[ks], w=[ks])
                op("dve", lambda: V.tensor_scalar(out=xs[:], in0=xap, scalar1=st[:, 2:3], scalar2=None, op0=ALU.mult),
                   r=[xkey, ks], w=[kx])
                return i

            def ln_B(i, gs, sh, hdst, hkey, pbanks):
                xs = S["xs"][i]
                kx = ("lnx", i)
                for half in range(2):
                    pb = pbanks[half]
                    for j4 in range(4):
                        j = half * 4 + j4
                        op("pe", lambda j=j, j4=j4, pb=pb: PE.transpose(out=PS[pb][:, j4 * 128:(j4 + 1) * 128],
                                                                        in_=xs[:, j * 128:(j + 1) * 128], identity=ident[:]),
                           r=[kx, "ident"], w=[pk[pb]])
                    for j4 in range(4):
                        j = half * 4 + j4
                        op("act", lambda j=j, j4=j4, pb=pb: A.activation(out=hdst(j), in_=PS[pb][:, j4 * 128:(j4 + 1) * 128],
                                                                         func=AF.Identity, scale=gs[:, j:j + 1], bias=sh[:, j:j + 1]),
                           r=[pk[pb], "mcol", "gsall"], w=[hkey])

            def ln_T(xap, xkey, gs, sh, hdst, hkey, pbanks):
                ln_B(ln_A(xap, xkey), gs, sh, hdst, hkey, pbanks)
            ln_T.A = ln_A
            ln_T.B = ln_B
            return ln_T

        def tiles_for(first, with_ctx):
            tl = []
            if with_ctx:
                tl.append((1, ctx_in[:, :] if first else X[0:CTX, :], X[0:CTX, :]))
            for i in range(SEQ // 256):
                src = x_in[i * 256:(i + 1) * 256, :] if first else X[CTX + i * 256:CTX + (i + 1) * 256, :]
                tl.append((0, src, X[CTX + i * 256:CTX + (i + 1) * 256, :]))
            return tl

        def ffn_phase(l, s, fi, first, with_ctx):
            with contextlib.ExitStack() as ph:
                w1 = sb(ph, "w1", [128, 8, 2 * DFF], BF16)
                w2 = sb(ph, "w2", [128, NF, D], BF16)
                xts = [sb(ph, f"xt{i}", [128, 2, D]) for i in range(2)]
                hT = sb(ph, "hT", [128, 8, 256], BF16)
                actT = sb(ph, "actT", [128, NF, 256], BF16)
                sg = [sb(ph, f"sg{i}", [128, 256]) for i in range(2)]
                tmp = [sb(ph, f"tmp{i}", [128, 512]) for i in range(2)]
                gb = [sb(ph, f"gb{i}", [128, D]) for i in range(2)]
                dg = sb(ph, "dg", [128, 128])
                ln_T = make_ln(ph)
                for fb in range(NF // 2):
                    for which in range(2):
                        c0 = which * DFF + fb * 256
                        dma("pool", w1[:, :, c0:c0 + 256], ffn_w_in[fi][:, c0:c0 + 256].rearrange("(k p) c -> p k c", p=128),
                            w=[("w1", which, fb)])
                for f in range(NF):
                    dma("pool", w2[:, f, :], ffn_w_out[fi][f * 128:(f + 1) * 128, :], w=[("w2", f)])
                for v in range(2):
                    if v == 1 and not with_ctx:
                        continue
                    bcast_row(gb[v], ("gb", v), gate_ap(l, s, v), 0.5, dg)
                n = 0
                tl = tiles_for(first, with_ctx)

                def load(ti):
                    dma("sp", xts[ti % 2][:, :, :], tl[ti][1].rearrange("(b p) d -> p b d", p=128), w=[("xt", ti % 2)])

                def lnA(ti):
                    return [ln_T.A(xts[ti % 2][:, b, :], ("xt", ti % 2)) for b in range(2)]

                load(0)
                hA = lnA(0)
                for ti, (v, src, dst) in enumerate(tl):
                    xt = xts[ti % 2]
                    xk = ("xt", ti % 2)
                    if ti + 1 < len(tl):
                        load(ti + 1)
                    for b in range(2):
                        ln_T.B(hA[b], gs_ap(l, s, v), sh_ap(l, s, v), lambda j, b=b: hT[:, j, b * 128:(b + 1) * 128], "hT", (0, 1))
                    for f in range(NF):
                        pg, pu = 2 + 2 * (f % 2), 3 + 2 * (f % 2)
                        for which, pb in ((0, pg), (1, pu)):
                            c0 = which * DFF + f * 128
                            for kk in range(8):
                                op("pe", lambda kk=kk, pb=pb, c0=c0: PE.matmul(PS[pb][:, 0:256], lhsT=w1[:, kk, c0:c0 + 128],
                                                                               rhs=hT[:, kk, :], start=(kk == 0), stop=(kk == 7)),
                                   r=[("w1", which, f // 2), "hT"], w=[pk[pb]])
                        sgi = sg[f % 2]
                        op("act", lambda pg=pg, sgi=sgi: A.activation(out=sgi[:], in_=PS[pg][:, 0:256], func=AF.Silu),
                           r=[pk[pg]], w=[("sg", f % 2)])
                        op("dve", lambda pu=pu, sgi=sgi, f=f: V.tensor_tensor(out=actT[:, f, :], in0=PS[pu][:, 0:256], in1=sgi[:],
                                                                              op=ALU.mult), r=[pk[pu], ("sg", f % 2)], w=[("actT", f)])
                    if ti + 1 < len(tl):
                        hA = lnA(ti + 1)
                    for tb in range(2):
                        for dh in range(2):
                            po = n % 2
                            tm = tmp[n % 2]
                            n += 1
                            for f in range(NF):
                                op("pe", lambda f=f, po=po, tb=tb, dh=dh: PE.matmul(
                                    PS[po][:, :], lhsT=actT[:, f, tb * 128:(tb + 1) * 128], rhs=w2[:, f, dh * 512:(dh + 1) * 512],
                                    start=(f == 0), stop=(f == NF - 1)), r=[("actT", f), ("w2", f)], w=[pk[po]])
                            op("dve", lambda po=po, tm=tm, dh=dh, v=v: V.tensor_tensor(
                                out=tm[:], in0=PS[po][:, :], in1=gb[v][:, dh * 512:(dh + 1) * 512], op=ALU.mult),
                               r=[pk[po], ("gb", v)], w=[("tmp", po)])
                            op("pool", lambda tm=tm, tb=tb, dh=dh, xt=xt: PO.tensor_tensor(
                                out=xt[:, tb, dh * 512:(dh + 1) * 512], in0=tm[:], in1=xt[:, tb, dh * 512:(dh + 1) * 512], op=ALU.add),
                               r=[("tmp", po), xk], w=[xk])
                    dma("sp", dst.rearrange("(b p) d -> p b d", p=128), xt[:, :, :], r=[xk])
                k.barrier()

        def lru_phase():
            l, s = 0, 1
            cw = lambda j, ch: cols[:, C_LCW + j * 8 + ch:C_LCW + j * 8 + ch + 1]
            cb = lambda ch: cols[:, C_LCB + ch:C_LCB + ch + 1]
            bg = lambda d, g, ch: cols[:, C_LBG + (d * 2 + g) * 8 + ch:C_LBG + (d * 2 + g) * 8 + ch + 1]
            tl = tiles_for(False, True)
            toff = lambda ti: 0 if ti == 0 else CTX + (ti - 1) * 256
            poff = lambda ti: CTX0 if ti == 0 else LAT0 + (ti - 1) * 256
            with contextlib.ExitStack() as ph:
                wi = sb(ph, "lwi", [128, 8, 2048], BF16)
                xts = [sb(ph, f"xt{i}", [128, 2, D]) for i in range(2)]
                hT = sb(ph, "hT", [128, 8, 256], BF16)
                pts = [sb(ph, f"pt{i}", [128, 16, 256]) for i in range(2)]
                ln_T = make_ln(ph)
                for kk in range(8):
                    dma("pool", wi[:, kk, :], lru_w_in[kk * 128:(kk + 1) * 128, :], w=[("lwi", kk)])
                for ti, (v, src, dst) in enumerate(tl):
                    xt = xts[ti % 2]
                    xk = ("xt", ti % 2)
                    pt = pts[ti % 2]
                    ptk = ("pt", ti % 2)
                    dma("sp", xt[:, :, :], src.rearrange("(b p) d -> p b d", p=128), w=[xk])
                    for b in range(2):
                        ln_T(xt[:, b, :], xk, gs_ap(l, s, v), sh_ap(l, s, v),
                             lambda j, b=b: hT[:, j, b * 128:(b + 1) * 128], "hT", (0, 1))
                    for ch in range(16):
                        pb = 2 + ch % 4
                        for kk in range(8):
                            op("pe", lambda kk=kk, pb=pb, ch=ch: PE.matmul(PS[pb][:, 0:256], lhsT=wi[:, kk, ch * 128:(ch + 1) * 128],
                                                                           rhs=hT[:, kk, :], start=(kk == 0), stop=(kk == 7)),
                               r=[("lwi", kk), "hT"], w=[pk[pb]])
                        if ch % 2 == 0:
                            op("act", lambda pb=pb, ch=ch, pt=pt: A.copy(out=pt[:, ch, :], in_=PS[pb][:, 0:256]), r=[pk[pb]], w=[ptk])
                        else:
                            op("dve", lambda pb=pb, ch=ch, pt=pt: V.tensor_copy(out=pt[:, ch, :], in_=PS[pb][:, 0:256]), r=[pk[pb]], w=[ptk])
                    dma("sp", PT[:, :, poff(ti):poff(ti) + 256].rearrange("c p t -> p c t"), pt[:, :, :], r=[ptk])
                k.barrier()
            with contextlib.ExitStack() as ph:
                wg = sb(ph, "wg", [128, 32, 256], BF16)
                coef = sb(ph, "coef", [128, 16])
                coef2 = sb(ph, "coef2", [128, 16])
                ee = sb(ph, "ee", [128, 16])
                pp = sb(ph, "pp", [128, 16])
                upre = sb(ph, "upre", [128, 2, TP])
                u = sb(ph, "u", [128, 2, T])
                ub = sb(ph, "ub", [128, 2, T], BF16)
                hsum = sb(ph, "hsum", [128, 2, T])
                tr = [[sb(ph, f"lt{n}_{i}", [128, 256]) for i in range(5)] for n in range(4)]
                sts = [[sb(ph, f"lst{d}{oc}", [128, 1]) for oc in range(2)] for d in range(2)]
                for q in range(16):
                    dma("pool", wg[:, q * 2:(q + 1) * 2, :], lru_w_gate[q].rearrange("(kc p) j -> p kc j", p=128), w=[("wg", q)])
                op("act", lambda: A.activation(out=ee[:], in_=cols[:, C_LAM:C_LAM + 16], func=AF.Exp, scale=-1.0), r=["cols"], w=["ee"])
                op("dve", lambda: V.memset(pp[:], 1.0 / 12), w=["pp"])
                for n in range(11, 0, -1):
                    op("dve", lambda: V.tensor_tensor(out=pp[:], in0=pp[:], in1=ee[:], op=ALU.mult), r=["pp", "ee"], w=["pp"])
                    op("dve", lambda n=n: V.tensor_scalar(out=pp[:], in0=pp[:], scalar1=-1.0, scalar2=1.0 / n, op0=ALU.mult, op1=ALU.add),
                       r=["pp"], w=["pp"])
                op("dve", lambda: V.scalar_tensor_tensor(out=coef[:], in0=pp[:], scalar=-8.0, in1=ee[:], op0=ALU.mult, op1=ALU.mult),
                   r=["pp", "ee"], w=["coef"])
                op("dve", lambda: V.tensor_scalar(out=coef2[:], in0=coef[:], scalar1=2.0, scalar2=None, op0=ALU.mult), r=["coef"], w=["coef2"])
                for hd in range(4):
                    dma("sp", upre[:, :, :], PT[8 + 2 * hd:8 + 2 * hd + 2, :, :].rearrange("c p t -> p c t"), w=["upre"])
                    for (a, b) in ((0, 2), (CTX0 + CTX, LAT0), (TP - 1, TP)):
                        op("pool", lambda a=a, b=b: PO.memset(upre[:, :, a:b], 0.0), r=["upre"], w=["upre"])
                    for oc in range(2):
                        ch = 2 * hd + oc
                        for (o0, n, base) in ((0, CTX, CTX0), (CTX, SEQ, LAT0)):
                            op("dve", lambda oc=oc, ch=ch, o0=o0, n=n, base=base: V.tensor_scalar(
                                out=u[:, oc, o0:o0 + n], in0=upre[:, oc, base - 2:base - 2 + n], scalar1=cw(0, ch), scalar2=cb(ch),
                                op0=ALU.mult, op1=ALU.add), r=["upre", "cols"], w=[("u", oc)])
                            for j in range(1, 4):
                                op("dve", lambda oc=oc, ch=ch, o0=o0, n=n, base=base, j=j: V.scalar_tensor_tensor(
                                    out=u[:, oc, o0:o0 + n], in0=upre[:, oc, base - 2 + j:base - 2 + j + n], scalar=cw(j, ch),
                                    in1=u[:, oc, o0:o0 + n], op0=ALU.mult, op1=ALU.add), r=["upre", "cols", ("u", oc)], w=[("u", oc)])
                        op("act", lambda oc=oc: A.copy(out=ub[:, oc, :], in_=u[:, oc, :]), r=[("u", oc)], w=["ub"])
                    for oc in range(2):
                        op("pool", lambda oc=oc: PO.memset(hsum[:, oc, :], 0.0), w=[("hsum", oc)])
                    for d in range(2):
                        for oc in range(2):
                            op("dve", lambda d=d, oc=oc: V.memset(sts[d][oc][:], 0.0), w=[("lst", d, oc)])
                    order = [list(range(17)), [0] + list(range(16, 0, -1))]
                    for step in range(17):
                        chains = [(d, oc) for d in range(2) for oc in range(2)]
                        for ci, (d, oc) in enumerate(chains):
                            ti = order[d][step]
                            t0 = toff(ti)
                            ch = 2 * hd + oc
                            R_, I_ = tr[ci][0], tr[ci][1]
                            pr, pi = 2 * ci, 2 * ci + 1
                            for g, pb in ((0, pr), (1, pi)):
                                for kc in range(2):
                                    op("pe", lambda g=g, pb=pb, kc=kc, d=d, oc=oc, t0=t0: PE.matmul(
                                        PS[pb][:, 0:256], lhsT=wg[:, ((d * 2 + g) * 4 + hd) * 2 + kc, oc * 128:(oc + 1) * 128],
                                        rhs=ub[:, kc, t0:t0 + 256], start=(kc == 0), stop=(kc == 1)),
                                       r=[("wg", (d * 2 + g) * 4 + hd), "ub"], w=[pk[pb]])
                            op("act", lambda: A.activation(out=R_[:], in_=PS[pr][:, 0:256], func=AF.Sigmoid, bias=bg(d, 0, ch)),
                               r=[pk[pr], "cols"], w=[("ltR", ci)])
                            op("act", lambda: A.activation(out=I_[:], in_=PS[pi][:, 0:256], func=AF.Sigmoid, bias=bg(d, 1, ch)),
                               r=[pk[pi], "cols"], w=[("ltI", ci)])
                        for ci, (d, oc) in enumerate(chains):
                            ch = 2 * hd + oc
                            cidx = d * 8 + ch
                            R_, A_, SQ_ = tr[ci][0], tr[ci][2], tr[ci][3]
                            op("act", lambda: A.activation(out=A_[:], in_=R_[:], func=AF.Exp, scale=coef[:, cidx:cidx + 1]),
                               r=[("ltR", ci), "coef"], w=[("ltA", ci)])
                            op("pool", lambda: PO.tensor_tensor(out=SQ_[:], in0=A_[:], in1=A_[:], op=ALU.mult),
                               r=[("ltA", ci)], w=[("ltS", ci)])
                            op("act", lambda: A.activation(out=SQ_[:], in_=SQ_[:], func=AF.Ln, scale=-1.0, bias=1.0),
                               r=[("ltS", ci)], w=[("ltS", ci)])
                            op("act", lambda: A.activation(out=SQ_[:], in_=SQ_[:], func=AF.Exp, scale=0.5),
                               r=[("ltS", ci)], w=[("ltS", ci)])
                        for ci, (d, oc) in enumerate(chains):
                            ti = order[d][step]
                            t0 = toff(ti)
                            I_, A_, SQ_, HB_ = tr[ci][1], tr[ci][2], tr[ci][3], tr[ci][4]
                            stt = sts[d][oc]
                            sk = ("lst", d, oc)
                            hk = ("hsum", oc)
                            op("dve", lambda: V.tensor_tensor(out=I_[:], in0=I_[:], in1=u[:, oc, t0:t0 + 256], op=ALU.mult),
                               r=[("ltI", ci), ("u", oc)], w=[("ltI", ci)])
                            op("dve", lambda: V.tensor_tensor(out=I_[:], in0=I_[:], in1=SQ_[:], op=ALU.mult),
                               r=[("ltI", ci), ("ltS", ci)], w=[("ltI", ci)])
                            if d == 0:
                                op("dve", lambda: V.tensor_tensor_scan(out=HB_[:], data0=A_[:], data1=I_[:],
                                                                       initial=stt[:, 0:1], op0=ALU.mult, op1=ALU.add),
                                   r=[("ltA", ci), ("ltI", ci), sk], w=[("ltH", ci)])
                                op("dve", lambda: V.tensor_copy(out=stt[:, 0:1], in_=HB_[:, 255:256]), r=[("ltH", ci)], w=[sk])
                            else:
                                op("dve", lambda: V.tensor_tensor_scan(out=HB_[:, ::-1], data0=A_[:, ::-1], data1=I_[:, ::-1],
                                                                       initial=stt[:, 0:1], op0=ALU.mult, op1=ALU.add),
                                   r=[("ltA", ci), ("ltI", ci), sk], w=[("ltH", ci)])
                                op("dve", lambda: V.tensor_copy(out=stt[:, 0:1], in_=HB_[:, 0:1]), r=[("ltH", ci)], w=[sk])
                            op("pool", lambda: PO.tensor_tensor(out=hsum[:, oc, t0:t0 + 256], in0=hsum[:, oc, t0:t0 + 256],
                                                                in1=HB_[:], op=ALU.add), r=[("ltH", ci), hk], w=[hk])
                    dma("sp", upre[:, :, 0:CTX], PT[2 * hd:2 * hd + 2, :, CTX0:CTX0 + CTX].rearrange("c p t -> p c t"), w=["upre"])
                    dma("sp", upre[:, :, CTX:T], PT[2 * hd:2 * hd + 2, :, LAT0:LAT0 + SEQ].rearrange("c p t -> p c t"), w=["upre"])
                    for oc in range(2):
                        op("act", lambda oc=oc: A.activation(out=upre[:, oc, 0:T], in_=upre[:, oc, 0:T], func=AF.Gelu_apprx_tanh),
                           r=["upre"], w=["upre"])
                        op("dve", lambda oc=oc: V.tensor_tensor(out=ub[:, oc, :], in0=upre[:, oc, 0:T], in1=hsum[:, oc, :], op=ALU.mult),
                           r=["upre", ("hsum", oc)], w=["ub"])
                    dma("sp", ZT[2 * hd:2 * hd + 2, :, :].rearrange("c p t -> p c t"), ub[:, :, :], r=["ub"])
                k.barrier()
            with contextlib.ExitStack() as ph:
                wo = sb(ph, "lwo", [128, 8, D], BF16)
                gb = [sb(ph, f"gb{i}", [128, D]) for i in range(2)]
                dg = sb(ph, "dg", [128, 128])
                xts = [sb(ph, f"xt{i}", [128, 2, D]) for i in range(2)]
                zts = [sb(ph, f"zt{i}", [128, 8, 256], BF16) for i in range(2)]
                tmp = [sb(ph, f"tmp{i}", [128, 512]) for i in range(2)]
                for c in range(8):
                    dma("pool", wo[:, c, :], lru_w_out[c * 128:(c + 1) * 128, :], w=[("lwo", c)])
                for v in range(2):
                    bcast_row(gb[v], ("gb", v), gate_ap(l, s, v), 1.0, dg)
                n = 0
                for ti, (v, src, dst) in enumerate(tl):
                    xt, xk = xts[ti % 2], ("xt", ti % 2)
                    zt, zk = zts[ti % 2], ("zt", ti % 2)
                    dma("sp", xt[:, :, :], src.rearrange("(b p) d -> p b d", p=128), w=[xk])
                    dma("sp", zt[:, :, :], ZT[:, :, toff(ti):toff(ti) + 256].rearrange("c p t -> p c t"), w=[zk])
                    for tb in range(2):
                        for dh in range(2):
                            po = n % 2
                            tm = tmp[n % 2]
                            n += 1
                            for c in range(8):
                                op("pe", lambda c=c, po=po, tb=tb, dh=dh, zt=zt: PE.matmul(
                                    PS[po][:, :], lhsT=zt[:, c, tb * 128:(tb + 1) * 128], rhs=wo[:, c, dh * 512:(dh + 1) * 512],
                                    start=(c == 0), stop=(c == 7)), r=[zk, ("lwo", c)], w=[pk[po]])
                            op("dve", lambda po=po, tm=tm, dh=dh, v=v: V.tensor_tensor(
                                out=tm[:], in0=PS[po][:, :], in1=gb[v][:, dh * 512:(dh + 1) * 512], op=ALU.mult),
                               r=[pk[po], ("gb", v)], w=[("tmp", po)])
                            op("pool", lambda tm=tm, tb=tb, dh=dh, xt=xt: PO.tensor_tensor(
                                out=xt[:, tb, dh * 512:(dh + 1) * 512], in0=tm[:], in1=xt[:, tb, dh * 512:(dh + 1) * 512], op=ALU.add),
                               r=[("tmp", po), xk], w=[xk])
                    dma("sp", dst.rearrange("(b p) d -> p b d", p=128), xt[:, :, :], r=[xk])
                k.barrier()

        def dn_phase():
            l, s = 1, 1
            toffs = [0] + [CTX + i * 256 for i in range(16)]
            poffs = [CTX0] + [LAT0 + i * 256 for i in range(16)]
            with contextlib.ExitStack() as ph:
                wi = sb(ph, "dwi", [128, 8, 4128], BF16)
                xts = [sb(ph, f"xt{i}", [128, 2, D]) for i in range(2)]
                hT = sb(ph, "hT", [128, 8, 256], BF16)
                pq = sb(ph, "pq", [128, 24, 256])
                pzs = [sb(ph, f"pz{i}", [128, 8, 256]) for i in range(2)]
                gbt = [sb(ph, f"gbt{i}", [128, 2, 32]) for i in range(2)]
                t1 = [sb(ph, f"t1{i}", [128, 16]) for i in range(2)]
                negA = sb(ph, "negA", [128, 16])
                ln_T = make_ln(ph)
                for kk in range(8):
                    dma("pool", wi[:, kk, :], dn_w_in[kk * 128:(kk + 1) * 128, :], w=[("dwi", kk)])
                op("act", lambda: A.activation(out=negA[:], in_=rowsb[:, 0:16], func=AF.Exp), r=["rowsb"], w=["negA"])
                op("dve", lambda: V.tensor_scalar(out=negA[:], in0=negA[:], scalar1=-1.0, scalar2=None, op0=ALU.mult), r=["negA"], w=["negA"])
                nb = 0
                for ti in range(17):
                    v = 1 if ti == 0 else 0
                    xt, xk = xts[ti % 2], ("xt", ti % 2)
                    pz, pzk = pzs[ti % 2], ("pz", ti % 2)
                    if ti == 0:
                        dma("sp", xt[:, :, :], X[0:CTX, :].rearrange("(b p) d -> p b d", p=128), w=[xk])
                    else:
                        for b in range(2):
                            for cl in range(2):
                                col = 4 * (ti - 1) + 2 * b + cl
                                dma("sp", xt[cl * 64:(cl + 1) * 64, b, :], Xlat_cm[col], w=[xk])
                    for b in range(2):
                        ln_T(xt[:, b, :], xk, gs_ap(l, s, v), sh_ap(l, s, v),
                             lambda j, b=b: hT[:, j, b * 128:(b + 1) * 128], "hT", (0, 1))
                    for ch in range(32):
                        pb = 2 + ch % 4
                        for kk in range(8):
                            op("pe", lambda kk=kk, pb=pb, ch=ch: PE.matmul(PS[pb][:, 0:256], lhsT=wi[:, kk, ch * 128:(ch + 1) * 128],
                                                                           rhs=hT[:, kk, :], start=(kk == 0), stop=(kk == 7)),
                               r=[("dwi", kk), "hT"], w=[pk[pb]])
                        if ch >= 24:
                            op("act", lambda pb=pb, ch=ch, pz=pz: A.activation(out=pz[:, ch - 24, :], in_=PS[pb][:, 0:256], func=AF.Silu),
                               r=[pk[pb]], w=[pzk])
                        elif ch % 2 == 0:
                            op("act", lambda pb=pb, ch=ch: A.copy(out=pq[:, ch, :], in_=PS[pb][:, 0:256]), r=[pk[pb]], w=["pq"])
                        else:
                            op("dve", lambda pb=pb, ch=ch: V.tensor_copy(out=pq[:, ch, :], in_=PS[pb][:, 0:256]), r=[pk[pb]], w=["pq"])
                    for b in range(2):
                        pb = 6 + b
                        g_, t_ = gbt[nb % 2], t1[nb % 2]
                        gk, tk = ("gbt", nb % 2), ("t1", nb % 2)
                        nb += 1
                        for kk in range(8):
                            op("pe", lambda kk=kk, pb=pb, b=b: PE.matmul(PS[pb][:, 0:32], lhsT=hT[:, kk, b * 128:(b + 1) * 128],
                                                                         rhs=wi[:, kk, 4096:4128], start=(kk == 0), stop=(kk == 7)),
                               r=[("dwi", kk), "hT"], w=[pk[pb]])
                        op("dve", lambda: V.tensor_tensor(out=t_[:], in0=PS[pb][:, 0:16], in1=rowsb[:, 16:32], op=ALU.add),
                           r=[pk[pb], "rowsb"], w=[tk])
                        op("act", lambda: A.activation(out=t_[:], in_=t_[:], func=AF.Exp), r=[tk], w=[tk])
                        op("act", lambda: A.activation(out=t_[:], in_=t_[:], func=AF.Ln, bias=1.0), r=[tk], w=[tk])
                        op("dve", lambda: V.tensor_tensor(out=g_[:, 0, 0:16], in0=t_[:], in1=negA[:], op=ALU.mult), r=[tk, "negA"], w=[gk])
                        op("act", lambda: A.activation(out=g_[:, 0, 16:32], in_=PS[pb][:, 16:32], func=AF.Sigmoid), r=[pk[pb]], w=[gk])
                        r0 = toffs[ti] + b * 128
                        dma("sp", GD[r0:r0 + 128, :], g_[:, 0, 0:16], r=[gk])
                        dma("sp", BD[r0:r0 + 128, :], g_[:, 0, 16:32], r=[gk])
                    dma("sp", QP[:, :, poffs[ti]:poffs[ti] + 256].rearrange("c p t -> p c t"), pq[:, :, :], r=["pq"])
                    dma("sp", ZD[:, :, toffs[ti]:toffs[ti] + 256].rearrange("c p t -> p c t"), pz[:, :, :], r=[pzk])
                k.barrier()
            with contextlib.ExitStack() as ph:
                pres = [sb(ph, f"pre{i}", [128, TP]) for i in range(2)]
                vals = [sb(ph, f"val{i}", [128, T]) for i in range(2)]
                toks = [sb(ph, f"tok{i}", [128, 34, 128]) for i in range(2)]
                sqs = [sb(ph, f"sq{i}", [128, 512]) for i in range(2)]
                rins = [sb(ph, f"rin{i}", [128, 512]) for i in range(2)]
                nt = 0
                for c in range(24):
                    pre, prk = pres[c % 2], ("pre", c % 2)
                    val, vk = vals[c % 2], ("val", c % 2)
                    dma("sp", pre[:, :], QP[c], w=[prk])
                    for (a, b) in ((0, 2), (CTX0 + CTX, LAT0), (TP - 1, TP)):
                        op("pool", lambda a=a, b=b, pre=pre: PO.memset(pre[:, a:b], 0.0), r=[prk], w=[prk])
                    cw = lambda j: cols[:, C_DCW + j * 24 + c:C_DCW + j * 24 + c + 1]
                    for (o0, n, base) in ((0, CTX, CTX0), (CTX, SEQ, LAT0)):
                        op("dve", lambda o0=o0, n=n, base=base: V.tensor_scalar(
                            out=val[:, o0:o0 + n], in0=pre[:, base - 2:base - 2 + n], scalar1=cw(0), scalar2=None, op0=ALU.mult),
                           r=[prk, "cols"], w=[vk])
                        for j in range(1, 4):
                            op("dve", lambda o0=o0, n=n, base=base, j=j: V.scalar_tensor_tensor(
                                out=val[:, o0:o0 + n], in0=pre[:, base - 2 + j:base - 2 + j + n], scalar=cw(j),
                                in1=val[:, o0:o0 + n], op0=ALU.mult, op1=ALU.add), r=[prk, "cols", vk], w=[vk])
                    op("act", lambda: A.activation(out=val[:, :], in_=val[:, :], func=AF.Silu), r=[vk], w=[vk])
                    if c < 16:
                        for a in range(0, T, 512):
                            n = min(512, T - a)
                            sq, sk = sqs[nt % 2], ("sq", nt % 2)
                            rin, rk = rins[nt % 2], ("rin", nt % 2)
                            pb = nt % 2
                            nt += 1
                            op("act", lambda: A.activation(out=sq[:, 0:n], in_=val[:, a:a + n], func=AF.Square), r=[vk], w=[sk])
                            op("pe", lambda: PE.matmul(PS[pb][:, 0:n], lhsT=ones[:, :], rhs=sq[:, 0:n], start=True, stop=True),
                               r=[sk, "ones"], w=[pk[pb]])
                            op("act", lambda: A.activation(out=rin[:, 0:n], in_=PS[pb][:, 0:n], func=AF.Ln, bias=EPS), r=[pk[pb]], w=[rk])
                            op("act", lambda: A.activation(out=rin[:, 0:n], in_=rin[:, 0:n], func=AF.Exp, scale=-0.5,
                                                           bias=(math.log(128.0 ** -0.5) if c < 8 else 0.0)), r=[rk], w=[rk])
                            op("dve", lambda: V.tensor_tensor(out=val[:, a:a + n], in0=val[:, a:a + n], in1=rin[:, 0:n], op=ALU.mult),
                               r=[vk, rk], w=[vk])
                    if c < 8:
                        dma("sp", QT[c], val[:, :], r=[vk])
                    elif c < 16:
                        dma("sp", KT[c - 8], val[:, :], r=[vk])
                    if c >= 8:
                        tok, tkk = toks[c % 2], ("tok", c % 2)
                        for b0 in range(0, 34, 4):
                            nbk = min(4, 34 - b0)
                            pb = 2 + (b0 // 4) % 4
                            for bb in range(nbk):
                                blk = b0 + bb
                                op("pe", lambda bb=bb, blk=blk, pb=pb: PE.transpose(out=PS[pb][:, bb * 128:(bb + 1) * 128],
                                                                                     in_=val[:, blk * 128:(blk + 1) * 128], identity=ident[:]),
                                   r=[vk, "ident"], w=[pk[pb]])
                            src_ap = PS[pb][:, 0:nbk * 128].rearrange("p (b f) -> p b f", f=128)
                            if (b0 // 4) % 2 == 0:
                                op("act", lambda: A.copy(out=tok[:, b0:b0 + nbk, :], in_=src_ap), r=[pk[pb]], w=[tkk])
                            else:
                                op("dve", lambda: V.tensor_copy(out=tok[:, b0:b0 + nbk, :], in_=src_ap), r=[pk[pb]], w=[tkk])
                        dst = (KTOK[c - 8] if c < 16 else VTOK[c - 16]).rearrange("(b p) f -> p b f", p=128)
                        dma("sp", dst, tok[:, :, :], r=[tkk])
                k.barrier()
            with contextlib.ExitStack() as ph:
                W3 = [64, 8, 64]
                offd_t = sb(ph, "offd", [128, 64])
                mi_t = sb(ph, "mi", [128, 64]); ntm_t = sb(ph, "ntm", [128, 64]); ncm_t = sb(ph, "ncm", [128, 64])
                MI, NT_, NC_, OFFD, I64 = [], [], [], [], []
                for d in range(2):
                    po = d * 64
                    v64 = iot[po:po + 64, po:po + 64]
                    mi, ntm, ncm, offd = mi_t[po:po + 64, :], ntm_t[po:po + 64, :], ncm_t[po:po + 64, :], offd_t[po:po + 64, :]
                    op("dve", lambda: V.tensor_scalar(out=offd, in0=v64, scalar1=0.0, scalar2=None, op0=ALU.not_equal), r=["iot"], w=["msk"])
                    op("dve", lambda: V.tensor_scalar(out=mi, in0=v64, scalar1=0.0, scalar2=None,
                                                      op0=(ALU.is_ge if d == 0 else ALU.is_le)), r=["iot"], w=["msk"])
                    op("dve", lambda: V.tensor_scalar(out=ntm, in0=mi, scalar1=-1.0, scalar2=-NEG, op0=ALU.add, op1=ALU.mult),
                       r=["msk"], w=["msk"])
                    op("dve", lambda: V.tensor_scalar(out=ncm, in0=v64, scalar1=0.0, scalar2=None,
                                                      op0=(ALU.is_lt if d == 0 else ALU.is_gt)), r=["iot"], w=["msk"])
                    op("dve", lambda: V.tensor_scalar(out=ncm, in0=ncm, scalar1=-1.0, scalar2=-NEG, op0=ALU.add, op1=ALU.mult),
                       r=["msk"], w=["msk"])
                    MI.append(mi); NT_.append(ntm); NC_.append(ncm); OFFD.append(offd); I64.append(ident[po:po + 64, po:po + 64])
                bcm = lambda m: m.unsqueeze(1).broadcast_to(W3)
                bc = lambda ap2, n: ap2.unsqueeze(2).broadcast_to([ap2.shape[0], 8, n])
                TL = []
                big = {}
                for nm in ("ktok", "vtok", "kbg", "vb", "kd", "vn"):
                    big[nm] = sb(ph, f"{nm}B", [128, 8, 128])
                for nm in ("gM", "bdg", "E1", "E2", "DTi", "DCs", "DTs", "Q0", "P0", "Q1", "P1", "QKm", "R", "Rr", "tmpw"):
                    big[nm] = sb(ph, f"{nm}B", [128, 8, 64])
                for nm in ("g16", "b16"):
                    big[nm] = sb(ph, f"{nm}B", [128, 16])
                for nm in ("gcc", "egc", "bco"):
                    big[nm] = sb(ph, f"{nm}B", [128, 8])
                for d in range(2):
                    t = {}
                    for nm in ("kT", "qT", "qdT", "wTn", "osb"):
                        t[nm] = sb(ph, f"{nm}{d}", [128, 8, 64])
                    for nm in big:
                        t[nm] = big[nm][d * 64:(d + 1) * 64]
                    t["egr"] = sb(ph, f"egr{d}", [128, 8, 64])
                    t["S"] = sb(ph, f"S{d}", [128, 8, 128])
                    TL.append(t)
                    op("pool", lambda t=t: PO.memset(t["egr"][:, :, :], 0.0), w=[("egr", d)])
                    for h in range(8):
                        op("dve", lambda t=t, h=h: V.tensor_copy(out=t["S"][:, h, :].bitcast(F32R), in_=t["egr"][:, 0:2, :].rearrange("p a b -> p (a b)")),
                           r=[("egr", d)], w=[("S", d, h)])
                w3 = lambda ps_, po: ps_[po:po + 64, :].rearrange("p (h j) -> p h j", h=8)
                fl = lambda tl: tl[:, :, :].rearrange("p h j -> p (h j)")

                def dn_chunk(d, t0, is_lat):
                    t = TL[d]
                    po = d * 64
                    i64, offd = I64[d], OFFD[d]
                    rr = lambda ap: ap.bitcast(F32R)
                    rq = rr if d == 0 else (lambda ap: ap)
                    K_ = lambda nm: (nm, d)
                    T1a, T1b, T2a, T2b = PS[d * 4], PS[d * 4 + 1], PS[d * 4 + 2], PS[d * 4 + 3]
                    k1a, k1b, k2a, k2b = pk[d * 4], pk[d * 4 + 1], pk[d * 4 + 2], pk[d * 4 + 3]
                    last = 63 if d == 0 else 0
                    gd = t["g16"][:, d * 8:(d + 1) * 8]
                    bd = t["b16"][:, d * 8:(d + 1) * 8]
                    dma("sp", t["kT"][:, :, :], KT[:, :, t0:t0 + 64].rearrange("h p t -> p h t"), w=[K_("kT")])
                    dma("sp", t["qT"][:, :, :], QT[:, :, t0:t0 + 64].rearrange("h p t -> p h t"), w=[K_("qT")])
                    dma("sp", t["ktok"][:, :, :], KTOK[:, t0:t0 + 64, :].rearrange("h t f -> t h f"), w=[K_("ktok")])
                    dma("sp", t["vtok"][:, :, :], VTOK[:, t0:t0 + 64, :].rearrange("h t f -> t h f"), w=[K_("vtok")])
                    dma("sp", t["g16"][:, :], GD[t0:t0 + 64, :], w=[K_("g16")])
                    dma("sp", t["b16"][:, :], BD[t0:t0 + 64, :], w=[K_("b16")])
                    yield
                    op("dve", lambda: V.tensor_tensor(out=t["gM"][:, :, :], in0=bcm(MI[d]), in1=bc(gd, 64), op=ALU.mult),
                       r=["msk", K_("g16")], w=[K_("gM")])
                    op("pe", lambda: PE.matmul(T1a[:, :], lhsT=ones[po:po + 64, :], rhs=fl(t["gM"]), start=True, stop=True),
                       r=["ones", K_("gM")], w=[k1a])
                    op("pe", lambda: PE.matmul(T1b[po:po + 64, 0:8], lhsT=MI[d], rhs=gd, start=True, stop=True),
                       r=["msk", K_("g16")], w=[k1b])
                    op("dve", lambda: V.tensor_tensor(out=t["bdg"][:, :, :], in0=bcm(i64), in1=bc(bd, 64), op=ALU.mult),
                       r=["ident", K_("b16")], w=[K_("bdg")])
                    op("pe", lambda: PE.matmul(T2a[po:po + 64, :], lhsT=ones[po:po + 64, 0:64], rhs=fl(t["bdg"]), start=True, stop=True),
                       r=["ones", K_("bdg")], w=[k2a])
                    op("act", lambda: A.copy(out=t["gcc"][:, :], in_=T1b[po:po + 64, 0:8]), r=[k1b], w=[K_("gcc")])
                    op("dve", lambda: V.tensor_tensor(out=t["E1"][:, :, :], in0=w3(T1a, po), in1=bc(t["gcc"][:, :], 64), op=ALU.subtract),
                       r=[k1a, K_("gcc")], w=[K_("E1")])
                    op("dve", lambda: V.tensor_tensor(out=t["E2"][:, :, :], in0=bc(t["gcc"][:, :], 64), in1=w3(T1a, po), op=ALU.subtract),
                       r=[k1a, K_("gcc")], w=[K_("E2")])
                    op("dve", lambda: V.scalar_tensor_tensor(out=t["E1"][:, :, :], in0=t["E1"][:, :, :], scalar=0.0, in1=bcm(NT_[d]),
                                                             op0=ALU.min, op1=ALU.add), r=[K_("E1"), "msk"], w=[K_("E1")])
                    op("dve", lambda: V.scalar_tensor_tensor(out=t["E2"][:, :, :], in0=t["E2"][:, :, :], scalar=0.0, in1=bcm(NC_[d]),
                                                             op0=ALU.min, op1=ALU.add), r=[K_("E2"), "msk"], w=[K_("E2")])
                    op("act", lambda: A.activation(out=t["DTi"][:, :, :], in_=t["E1"][:, :, :], func=AF.Exp), r=[K_("E1")], w=[K_("DTi")])
                    op("act", lambda: A.activation(out=t["DCs"][:, :, :], in_=t["E2"][:, :, :], func=AF.Exp), r=[K_("E2")], w=[K_("DCs")])
                    op("act", lambda: A.activation(out=fl(t["egr"]), in_=T1a[:, :], func=AF.Exp), r=[k1a], w=[K_("egr")])
                    op("act", lambda: A.activation(out=t["egc"][:, :], in_=t["gcc"][:, :], func=AF.Exp), r=[K_("gcc")], w=[K_("egc")])
                    yield
                    for h in range(8):
                        op("pe", lambda h=h: PE.matmul(T2b[po:po + 64, h * 64:(h + 1) * 64], lhsT=t["kT"][:, h, :], rhs=t["kT"][:, h, :],
                                                       start=True, stop=True), r=[K_("kT")], w=[k2b])
                    for h in range(8):
                        op("pe", lambda h=h: PE.matmul(T1b[po:po + 64, h * 64:(h + 1) * 64], lhsT=t["kT"][:, h, :], rhs=t["qT"][:, h, :],
                                                       start=True, stop=True), r=[K_("kT"), K_("qT")], w=[k1b])
                    yield
                    op("pool", lambda: PO.tensor_tensor(out=t["DTs"][:, :, :], in0=t["DTi"][:, :, :], in1=bcm(offd), op=ALU.mult),
                       r=[K_("DTi"), "msk"], w=[K_("DTs")])
                    op("dve", lambda: V.tensor_tensor(out=t["tmpw"][:, :, :], in0=w3(T2b, po), in1=t["DTs"][:, :, :], op=ALU.mult),
                       r=[k2b, K_("DTs")], w=[K_("tmpw")])
                    op("dve", lambda: V.tensor_tensor(out=t["Q0"][:, :, :], in0=w3(T2a, po), in1=t["tmpw"][:, :, :], op=ALU.mult),
                       r=[k2a, K_("tmpw")], w=[K_("Q0")])
                    op("dve", lambda: V.tensor_tensor(out=t["tmpw"][:, :, :], in0=w3(T2b, po), in1=t["DCs"][:, :, :], op=ALU.mult),
                       r=[k2b, K_("DCs"), K_("tmpw")], w=[K_("tmpw")])
                    op("dve", lambda: V.tensor_tensor(out=t["P0"][:, :, :], in0=t["tmpw"][:, :, :], in1=bc(bd, 64), op=ALU.mult),
                       r=[K_("tmpw"), K_("b16")], w=[K_("P0")])
                    op("dve", lambda: V.tensor_tensor(out=rr(t["QKm"][:, :, :]), in0=w3(T1b, po), in1=t["DTi"][:, :, :], op=ALU.mult),
                       r=[k1b, K_("DTi")], w=[K_("QKm")])
                    op("dve", lambda: V.tensor_tensor(out=t["R"][:, :, :], in0=bcm(i64), in1=t["Q0"][:, :, :], op=ALU.subtract),
                       r=["ident", K_("Q0")], w=[K_("R")])
                    yield
                    Qc, Pc, Qn, Pn = "Q0", "P0", "Q1", "P1"
                    for lev in range(1, 6):
                        for h in range(8):
                            op("pe", lambda h=h: PE.matmul(T2a[po:po + 64, h * 64:(h + 1) * 64], lhsT=t[Qc][:, h, :], rhs=t[Pc][:, h, :],
                                                           start=True, stop=True), r=[K_(Qc), K_(Pc)], w=[k2a])
                        if lev < 5:
                            for h in range(8):
                                op("pe", lambda h=h: PE.matmul(T1a[po:po + 64, h * 64:(h + 1) * 64], lhsT=t[Pc][:, h, :], rhs=t[Qc][:, h, :],
                                                               start=True, stop=True), r=[K_(Qc), K_(Pc)], w=[k1a])
                        yield
                        op("act", lambda: A.copy(out=t[Pn][:, :, :], in_=w3(T2a, po)), r=[k2a], w=[K_(Pn)])
                        if lev < 5:
                            op("dve", lambda: V.tensor_copy(out=t[Qn][:, :, :], in_=w3(T1a, po)), r=[k1a], w=[K_(Qn)])
                        for h in range(8):
                            op("pe", lambda h=h: PE.matmul(T2b[po:po + 64, h * 64:(h + 1) * 64], lhsT=t[Pn][:, h, :], rhs=t["R"][:, h, :],
                                                           start=True, stop=True), r=[K_(Pn), K_("R")], w=[k2b])
                        op("dve", lambda: V.tensor_tensor(out=t["R"][:, :, :], in0=w3(T2b, po), in1=t["R"][:, :, :], op=ALU.add),
                           r=[k2b, K_("R")], w=[K_("R")])
                        Qc, Pc, Qn, Pn = Qn, Pn, Qc, Pc
                        yield
                    yield
                    op("act", lambda: A.copy(out=rr(t["Rr"][:, :, :]), in_=t["R"][:, :, :]), r=[K_("R")], w=[K_("Rr")])
                    op("dve", lambda: V.tensor_tensor(out=t["bco"][:, :], in0=bd, in1=t["egc"][:, :], op=ALU.mult),
                       r=[K_("b16"), K_("egc")], w=[K_("bco")])
                    op("dve", lambda: V.tensor_tensor(out=rr(t["kbg"][:, :, :]), in0=t["ktok"][:, :, :], in1=bc(t["bco"][:, :], 128), op=ALU.mult),
                       r=[K_("ktok"), K_("bco")], w=[K_("kbg")])
                    op("dve", lambda: V.tensor_tensor(out=rr(t["vb"][:, :, :]), in0=t["vtok"][:, :, :], in1=bc(bd, 128), op=ALU.mult),
                       r=[K_("vtok"), K_("b16")], w=[K_("vb")])
                    op("dve", lambda: V.tensor_tensor(out=rr(t["kd"][:, :, :]), in0=t["ktok"][:, :, :],
                                                        in1=bc(t["DTi"][:, :, last], 128), op=ALU.mult),
                       r=[K_("ktok"), K_("DTi")], w=[K_("kd")])
                    op("dve", lambda: V.tensor_tensor(out=rr(t["qdT"][:, :, :]), in0=t["qT"][:, :, :], in1=t["egr"][:, :, :], op=ALU.mult),
                       r=[K_("qT"), K_("egr")], w=[K_("qdT")])
                    yield
                    for h in range(8):
                        op("pe", lambda h=h: PE.matmul(T1b[:, h * 64:(h + 1) * 64], lhsT=rr(t["kbg"][:, h, :]), rhs=rr(t["Rr"][:, h, :]),
                                                       start=True, stop=True), r=[K_("kbg"), K_("Rr")], w=[k1b])
                    yield
                    op("act", lambda: A.activation(out=rr(fl(t["wTn"])), in_=T1b[:, :], func=AF.Copy, scale=-1.0), r=[k1b], w=[K_("wTn")])
                    for h in range(8):
                        Tb, kb_ = (T1a, k1a) if h < 4 else (T1b, k1b)
                        o_ = Tb[po:po + 64, (h % 4) * 128:(h % 4 + 1) * 128]
                        op("pe", lambda h=h, o_=o_: PE.matmul(o_, lhsT=rq(t["Rr"][:, h, :]), rhs=rq(t["vb"][:, h, :]), start=True, stop=False),
                           r=[K_("Rr"), K_("vb")], w=[kb_])
                        op("pe", lambda h=h, o_=o_: PE.matmul(o_, lhsT=rq(t["wTn"][:, h, :]), rhs=rq(t["S"][:, h, :]), start=False, stop=True),
                           r=[K_("wTn"), ("S", d, h)], w=[kb_])
                    yield
                    op("act", lambda: A.copy(out=rr(t["vn"][:, 0:4, :]), in_=T1a[po:po + 64, :].rearrange("p (h f) -> p h f", h=4)),
                       r=[k1a], w=[K_("vn")])
                    op("dve", lambda: V.tensor_copy(out=rr(t["vn"][:, 4:8, :]), in_=T1b[po:po + 64, :].rearrange("p (h f) -> p h f", h=4)),
                       r=[k1b], w=[K_("vn")])
                    yield
                    if is_lat:
                        for h in range(8):
                            o_ = T2a[:, h * 64:(h + 1) * 64]
                            op("pe", lambda h=h, o_=o_: PE.matmul(o_, lhsT=rr(t["S"][:, h, :]), rhs=rr(t["qdT"][:, h, :]), start=True, stop=False),
                               r=[("S", d, h), K_("qdT")], w=[k2a])
                            op("pe", lambda h=h, o_=o_: PE.matmul(o_, lhsT=rr(t["vn"][:, h, :]), rhs=rr(t["QKm"][:, h, :]), start=False, stop=True),
                               r=[K_("vn"), K_("QKm")], w=[k2a])
                        op("act", lambda: A.copy(out=fl(t["osb"]), in_=T2a[:, :]), r=[k2a], w=[K_("osb")])
                        s0 = t0 - CTX
                        dma("sp", OFB[d][:, :, s0:s0 + 64].rearrange("h p t -> p h t"), t["osb"][:, :, :], r=[K_("osb")])
                    yield
                    for h in range(8):
                        Tb, kb_ = (T1a, k1a) if h < 4 else (T1b, k1b)
                        o_ = Tb[:, (h % 4) * 128:(h % 4 + 1) * 128]
                        op("pe", lambda h=h, o_=o_: PE.matmul(o_, lhsT=rr(t["kd"][:, h, :]), rhs=rr(t["vn"][:, h, :]), start=True, stop=True),
                           r=[K_("kd"), K_("vn")], w=[kb_])
                    for h in range(8):
                        Tb, kb_ = (T1a, k1a) if h < 4 else (T1b, k1b)
                        o_ = Tb[:, (h % 4) * 128:(h % 4 + 1) * 128]
                        op("dve", lambda h=h, o_=o_: V.scalar_tensor_tensor(out=rr(t["S"][:, h, :]), in0=t["S"][:, h, :],
                                                                            scalar=t["egr"][:, h, last:last + 1], in1=o_,
                                                                            op0=ALU.mult, op1=ALU.add),
                           r=[("S", d, h), K_("egr"), kb_], w=[("S", d, h)])

                fw = [(64 * j, False) for j in range(4)] + [(CTX + 64 * j, True) for j in range(64)]
                bw = [(64 * j, False) for j in range(3, -1, -1)] + [(CTX + 64 * j, True) for j in range(63, -1, -1)]
                for step in range(68):
                    gens = [dn_chunk(0, *fw[step]), dn_chunk(1, *bw[step])]
                    while gens:
                        for g in list(gens):
                            try:
                                next(g)
                            except StopIteration:
                                gens.remove(g)
                k.barrier()
            with contextlib.ExitStack() as ph:
                wo = sb(ph, "dwo", [128, 8, D], BF16)
                gb0 = sb(ph, "gb0", [128, D])
                dg = sb(ph, "dg", [128, 128])
                xts = [sb(ph, f"xt{i}", [128, 2, D]) for i in range(2)]
                ofs = [sb(ph, f"of{i}", [128, 8, 256]) for i in range(2)]
                obs = [sb(ph, f"ob{i}", [128, 8, 256]) for i in range(2)]
                zss = [sb(ph, f"zs{i}", [128, 8, 256]) for i in range(2)]
                sq = sb(ph, "rsq", [128, 2048])
                rr = sb(ph, "rr", [128, 2048])
                yb = sb(ph, "yb", [128, 8, 256], BF16)
                tmp = [sb(ph, f"tmp{i}", [128, 512]) for i in range(2)]
                for c in range(8):
                    dma("pool", wo[:, c, :], dn_w_out[c * 128:(c + 1) * 128, :], w=[("dwo", c)])
                bcast_row(gb0, "gb0", gate_ap(l, s, 0), 1.0, dg)
                gn = cols[:, C_GN:C_GN + 1]
                n = 0
                for ti in range(16):
                    xt, xk = xts[ti % 2], ("xt", ti % 2)
                    of, ofk = ofs[ti % 2], ("of", ti % 2)
                    ob, obk = obs[ti % 2], ("ob", ti % 2)
                    zs, zk = zss[ti % 2], ("zs", ti % 2)
                    s0 = ti * 256
                    for b in range(2):
                        for cl in range(2):
                            dma("sp", xt[cl * 64:(cl + 1) * 64, b, :], Xlat_cm[4 * ti + 2 * b + cl], w=[xk])
                    dma("sp", of[:, :, :], OFB[0][:, :, s0:s0 + 256].rearrange("h p t -> p h t"), w=[ofk])
                    dma("sp", ob[:, :, :], OFB[1][:, :, s0:s0 + 256].rearrange("h p t -> p h t"), w=[obk])
                    dma("sp", zs[:, :, :], ZD[:, :, CTX + s0:CTX + s0 + 256].rearrange("c p t -> p c t"), w=[zk])
                    off = of[:, :, :].rearrange("p h t -> p (h t)")
                    op("dve", lambda: V.tensor_tensor(out=off, in0=off, in1=ob[:, :, :].rearrange("p h t -> p (h t)"), op=ALU.add),
                       r=[ofk, obk], w=[ofk])
                    op("act", lambda: A.activation(out=sq[:, :], in_=off, func=AF.Square), r=[ofk], w=["rsq"])
                    for q in range(4):
                        pb = 2 + q
                        op("pe", lambda q=q, pb=pb: PE.matmul(PS[pb][:, :], lhsT=ones[:, :], rhs=sq[:, q * 512:(q + 1) * 512],
                                                              start=True, stop=True), r=["rsq", "ones"], w=[pk[pb]])
                        op("act", lambda q=q, pb=pb: A.activation(out=rr[:, q * 512:(q + 1) * 512], in_=PS[pb][:, :], func=AF.Ln,
                                                                  bias=EPS, scale=1.0 / 128), r=[pk[pb]], w=["rr"])
                    op("act", lambda: A.activation(out=rr[:, :], in_=rr[:, :], func=AF.Exp, scale=-0.5), r=["rr"], w=["rr"])
                    op("dve", lambda: V.tensor_tensor(out=off, in0=off, in1=rr[:, :], op=ALU.mult), r=[ofk, "rr"], w=[ofk])
                    op("dve", lambda: V.scalar_tensor_tensor(out=yb[:, :, :].rearrange("p h t -> p (h t)"), in0=off, scalar=gn,
                                                             in1=zs[:, :, :].rearrange("p h t -> p (h t)"), op0=ALU.mult, op1=ALU.mult),
                       r=[ofk, zk, "cols"], w=["yb"])
                    for tb in range(2):
                        for dh in range(2):
                            po = n % 2
                            tm = tmp[n % 2]
                            n += 1
                            for c in range(8):
                                op("pe", lambda c=c, po=po, tb=tb, dh=dh: PE.matmul(
                                    PS[po][:, :], lhsT=yb[:, c, tb * 128:(tb + 1) * 128], rhs=wo[:, c, dh * 512:(dh + 1) * 512],
                                    start=(c == 0), stop=(c == 7)), r=["yb", ("dwo", c)], w=[pk[po]])
                            op("dve", lambda po=po, tm=tm, dh=dh: V.tensor_tensor(
                                out=tm[:], in0=PS[po][:, :], in1=gb0[:, dh * 512:(dh + 1) * 512], op=ALU.mult),
                               r=[pk[po], "gb0"], w=[("tmp", po)])
                            op("pool", lambda tm=tm, tb=tb, dh=dh, xt=xt: PO.tensor_tensor(
                                out=xt[:, tb, dh * 512:(dh + 1) * 512], in0=tm[:], in1=xt[:, tb, dh * 512:(dh + 1) * 512], op=ALU.add),
                               r=[("tmp", po), xk], w=[xk])
                    for b in range(2):
                        for cl in range(2):
                            dma("sp", Xlat_cm[4 * ti + 2 * b + cl], xt[cl * 64:(cl + 1) * 64, b, :], r=[xk])
                k.barrier()

        def final_phase(raw):
            with contextlib.ExitStack() as ph:
                xts = [sb(ph, f"fx{i}", [128, 2, D]) for i in range(2)]
                junk = sb(ph, "fj", [128, D])
                sts = [sb(ph, f"fs{i}", [128, 4]) for i in range(2)]
                for ti in range(SEQ // 256):
                    xt = xts[ti % 2]
                    xk = ("fx", ti % 2)
                    dma("sp", xt[:, :, :], X[CTX + ti * 256:CTX + (ti + 1) * 256, :].rearrange("(b p) d -> p b d", p=128), w=[xk])
                    if not raw:
                        for b in range(2):
                            st = sts[b]
                            ks = ("fs", b)
                            op("act", lambda b=b, st=st: A.activation(out=junk[:], in_=xt[:, b, :], func=AF.Square, accum_out=st[:, 0:1]),
                               r=[xk], w=["fj", ks])
                            op("act", lambda st=st: A.activation(out=st[:, 1:2], in_=st[:, 0:1], func=AF.Ln, bias=EPS, scale=1.0 / D),
                               r=[ks], w=[ks])
                            op("act", lambda st=st: A.activation(out=st[:, 2:3], in_=st[:, 1:2], func=AF.Exp, scale=-0.5), r=[ks], w=[ks])
                            op("dve", lambda b=b, st=st: V.scalar_tensor_tensor(out=xt[:, b, :], in0=xt[:, b, :], scalar=st[:, 2:3],
                                                                               in1=rowsb[:, 32:1056], op0=ALU.mult, op1=ALU.mult),
                               r=[xk, ks, "rowsb"], w=[xk])
                    dma("sp", out[ti * 256:(ti + 1) * 256, :].rearrange("(b p) d -> p b d", p=128), xt[:, :, :], r=[xk])
                k.barrier()

        stages = []
        stages.append(lambda: ffn_phase(0, 0, 0, True, True))
        stages.append(lru_phase)
        stages.append(lambda: ffn_phase(0, 2, 1, False, True))
        stages.append(lambda: ffn_phase(1, 0, 2, False, True))
        stages.append(dn_phase)
        stages.append(lambda: ffn_phase(1, 2, 3, False, False))
        from_lru = len(stages)
        nst = 0
        for f in stages:
            if nst >= stage:
                break
            f()
            nst += 1
        final_phase(raw=dbg)
    return nc


def host_inputs(inputs, b):
    f = lambda a: np.ascontiguousarray(np.asarray(a, dtype=np.float32))
    cols = np.zeros((NCOLS, 128), np.float32)
    cols[C_C:C_C + 8] = f(inputs["c"])[b].reshape(8, 128)
    cols[C_CC:C_CC + 8] = f(inputs["c_ctx"]).reshape(8, 128)
    cols[C_BADA:C_BADA + 144] = f(inputs["b_ada"]).reshape(144, 128)
    cols[C_GSUB:C_GSUB + 48] = f(inputs["g_sub"]).reshape(48, 128)
    cols[C_LCW:C_LCW + 32] = f(inputs["lru_conv_w"])[0].reshape(32, 128)
    cols[C_LCB:C_LCB + 8] = f(inputs["lru_conv_b"])[0].reshape(8, 128)
    cols[C_LBG:C_LBG + 32] = f(inputs["lru_b_gate"])[0].reshape(32, 128)
    cols[C_LAM:C_LAM + 16] = f(inputs["lru_lambda"])[0].reshape(16, 128)
    cols[C_DCW:C_DCW + 96] = f(inputs["dn_conv_w"])[0].reshape(96, 128)
    cols[C_GN:C_GN + 1] = f(inputs["dn_g_norm"])[0].reshape(1, 128)
    rows = np.concatenate([f(inputs["dn_a_log"])[0].reshape(16), f(inputs["dn_dt_bias"])[0].reshape(16),
                           f(inputs["g_final"]).reshape(1024)]).reshape(1, 1056)
    return {
        "x": f(inputs["x"])[b], "ctx": f(inputs["ctx"])[b], "cols_src": cols, "rows_src": rows,
        "w_ada": f(inputs["w_ada"]), "ffn_w_in": f(inputs["ffn_w_in"]).reshape(4, D, 2 * DFF),
        "ffn_w_out": f(inputs["ffn_w_out"]).reshape(4, DFF, D), "lru_w_in": f(inputs["lru_w_in"])[0],
        "lru_w_gate": f(inputs["lru_w_gate"])[0].reshape(16, 256, 256), "lru_w_out": f(inputs["lru_w_out"])[0],
        "dn_w_in": f(inputs["dn_w_in"])[0], "dn_w_out": f(inputs["dn_w_out"])[0],
    }


def kernel(**inputs):
    nc = build()
    in_maps = [host_inputs(inputs, c % 4) for c in range(8)]
    res = run_bass_kernel_spmd(nc, in_maps, core_ids=list(range(8)))
    return np.stack([np.asarray(res.results[b]["out"], dtype=np.float32) for b in range(4)], axis=0)
```

```python
import contextlib
import math
import numpy as np
import concourse.bass as bass
import concourse.mybir as mybir
from concourse.bass_utils import run_bass_kernel_spmd

F32 = mybir.dt.float32
BF16 = mybir.dt.bfloat16
F32R = mybir.dt.float32r
AF = mybir.ActivationFunctionType
ALU = mybir.AluOpType

D = 1024
SEQ = 4096
CTX = 256
T = SEQ + CTX
DFF = 2816
NF = DFF // 128
EPS = 1e-6
CTX0 = 2
LAT0 = CTX0 + CTX + 3
TP = LAT0 + SEQ + 1
NEG = -30000.0

C_C, C_CC, C_BADA, C_GSUB, C_LCW, C_LCB, C_LBG, C_LAM, C_DCW, C_GN = 0, 8, 16, 160, 208, 240, 248, 280, 296, 392
NCOLS = 512


class K:
    def __init__(self, nc, es):
        self.nc = nc
        self.es = es
        self.eng = {"pe": nc.tensor, "dve": nc.vector, "act": nc.scalar, "pool": nc.gpsimd, "sp": nc.sync}
        self.ce = ["pe", "dve", "act", "pool"]
        self.EPOCH = 30000
        self.cnt = {e: 0 for e in self.ce}
        self.ep = {e: 0 for e in self.ce}
        self.sems = {e: [es.enter_context(nc.semaphore(f"s_{e}_0"))] for e in self.ce}
        self.KD = 8
        self.dq = ["sp", "pool"]
        self.dsem = {q: [es.enter_context(nc.semaphore(f"d_{q}_{i}")) for i in range(self.KD)] for q in self.dq}
        self.dn = {q: 0 for q in self.dq}
        self.seen = {e: {} for e in self.eng}
        self.st = {}

    def _wait(self, e, tok):
        if tok is None:
            return
        if tok[0] == "c":
            _, f, ep, idx = tok
            if f == e and e == "pe":
                return
            key = (f, ep)
            if self.seen[e].get(key, 0) >= idx:
                return
            self.eng[e].wait_ge(self.sems[f][ep], idx)
            self.seen[e][key] = idx
        else:
            _, q, si, cnt = tok
            key = ("d", q, si)
            if self.seen[e].get(key, 0) >= cnt:
                return
            self.eng[e].wait_ge(self.dsem[q][si], cnt)
            self.seen[e][key] = cnt

    def _deps(self, e, r, w):
        for k in r:
            s = self.st.get(k)
            if s:
                self._wait(e, s[0])
        for k in w:
            s = self.st.get(k)
            if s:
                self._wait(e, s[0])
                for t in s[1].values():
                    self._wait(e, t)

    def _record(self, tok, r, w, rid):
        for k in w:
            self.st[k] = [tok, {}]
        for k in r:
            s = self.st.setdefault(k, [None, {}])
            s[1][rid] = tok

    def op(self, e, fn, r=(), w=()):
        self._deps(e, r, w)
        inst = fn()
        if self.cnt[e] >= self.EPOCH:
            self.ep[e] += 1
            self.cnt[e] = 0
            self.sems[e].append(self.es.enter_context(self.nc.semaphore(f"s_{e}_{self.ep[e]}")))
        self.cnt[e] += 1
        inst.then_inc(self.sems[e][self.ep[e]], 1)
        tok = ("c", e, self.ep[e], self.cnt[e])
        self._record(tok, r, w, e)
        return tok

    def dma(self, q, out, in_, r=(), w=()):
        n = self.dn[q]
        si = n % self.KD
        cnt = 16 * (n // self.KD + 1)
        if cnt > 16:
            self._wait(q, ("d", q, si, cnt - 16))
        self._deps(q, r, w)
        self.eng[q].dma_start(out=out, in_=in_).then_inc(self.dsem[q][si], 16)
        self.dn[q] = n + 1
        tok = ("d", q, si, cnt)
        self._record(tok, r, w, ("d", q, si))
        return tok

    def barrier(self):
        toks = []
        for f in self.ce:
            if self.cnt[f] > 0:
                toks.append(("c", f, self.ep[f], self.cnt[f]))
        for q in self.dq:
            n = self.dn[q]
            for si in range(self.KD):
                uses = (n - si + self.KD - 1) // self.KD if n > si else 0
                if uses > 0:
                    toks.append(("d", q, si, 16 * uses))
        for e in self.eng:
            for t in toks:
                if t[0] == "c" and t[1] == e:
                    continue
                self._wait(e, t)
        self.st = {}


def build(stage=99, dbg=False):
    nc = bass.Bass("TRN2", target_bir_lowering=False)
    dt = lambda name, shape, dtype=F32, kind="ExternalInput": nc.dram_tensor(name, shape, dtype, kind=kind).ap()
    x_in = dt("x", [SEQ, D])
    ctx_in = dt("ctx", [CTX, D])
    cols_src = dt("cols_src", [NCOLS, 128])
    rows_src = dt("rows_src", [1, 1056])
    w_ada = dt("w_ada", [2, D, 9 * D])
    ffn_w_in = dt("ffn_w_in", [4, D, 2 * DFF])
    ffn_w_out = dt("ffn_w_out", [4, DFF, D])
    lru_w_in = dt("lru_w_in", [D, 2048])
    lru_w_gate = dt("lru_w_gate", [16, 256, 256])
    lru_w_out = dt("lru_w_out", [D, D])
    dn_w_in = dt("dn_w_in", [D, 4128])
    dn_w_out = dt("dn_w_out", [D, D])
    out = dt("out", [SEQ, D], kind="ExternalOutput")
    X = dt("Xs", [T, D], kind="Internal")
    PT = dt("PTs", [16, 128, TP], kind="Internal")
    ZT = dt("ZTs", [8, 128, T], BF16, kind="Internal")
    QP = dt("QPs", [24, 128, TP], kind="Internal")
    ZD = dt("ZDs", [8, 128, T], kind="Internal")
    GD = dt("GDs", [T, 16], kind="Internal")
    BD = dt("BDs", [T, 16], kind="Internal")
    QT = dt("QTs", [8, 128, T], kind="Internal")
    KT = dt("KTs", [8, 128, T], kind="Internal")
    KTOK = dt("KTOKs", [8, T, 128], kind="Internal")
    VTOK = dt("VTOKs", [8, T, 128], kind="Internal")
    OFB = [dt("OFs", [8, 128, SEQ], kind="Internal"), dt("OBs", [8, 128, SEQ], kind="Internal")]
    Xlat_cm = X[CTX:T, :].rearrange("(r c) d -> c r d", c=64)

    with contextlib.ExitStack() as es:
        k = K(nc, es)
        op, dma = k.op, k.dma
        V, A, PE, PO = nc.vector, nc.scalar, nc.tensor, nc.gpsimd

        uid = [0]

        def sb(st, name, shape, dtype=F32):
            uid[0] += 1
            return st.enter_context(nc.sbuf_tensor(f"{name}_u{uid[0]}", shape, dtype))

        PS = [es.enter_context(nc.psum_tensor(f"ps{i}", [128, 512], F32)) for i in range(8)]
        pk = [("ps", i) for i in range(8)]

        ident = sb(es, "ident", [128, 128])
        ones = sb(es, "ones", [128, 128])
        iot = sb(es, "iot", [128, 128])
        cols = sb(es, "cols", [128, NCOLS])
        mcol = sb(es, "mcol", [128, 2, 72, 2])
        gsall = sb(es, "gsall", [128, 2, 3, 2, 8])
        rowsb = sb(es, "rowsb", [128, 1056])
        op("pool", lambda: PO.iota(iot[:], pattern=[[1, 128]], base=0, channel_multiplier=-1,
                                   allow_small_or_imprecise_dtypes=True), w=["iot"])
        op("dve", lambda: V.tensor_scalar(out=ident[:], in0=iot[:], scalar1=0.0, scalar2=None, op0=ALU.is_equal),
           r=["iot"], w=["ident"])
        op("dve", lambda: V.memset(ones[:], 1.0), w=["ones"])

        with contextlib.ExitStack() as ph:
            stg = sb(ph, "stg", [128, 4, 128])
            rows1 = sb(ph, "rows1", [1, 1056])
            sc = sb(ph, "sc", [128, 8, 2])
            was = [sb(ph, f"wa{i}", [128, 8, 512]) for i in range(2)]
            dma("sp", stg[:, :, :], cols_src.rearrange("(g p) f -> p g f", p=128), w=["stg"])
            dma("sp", rows1[:, :], rows_src[:, :], w=["rows1"])
            for g in range(4):
                op("pe", lambda g=g: PE.transpose(out=PS[0][:, g * 128:(g + 1) * 128], in_=stg[:, g, :], identity=ident[:]),
                   r=["stg", "ident"], w=[pk[0]])
            op("dve", lambda: V.tensor_copy(out=cols[:], in_=PS[0][:, :]), r=[pk[0]], w=["cols"])
            for i, (a, b) in enumerate([(0, 512), (512, 1024), (1024, 1056)]):
                op("pe", lambda a=a, b=b, i=i: PE.matmul(PS[1 + i][:, 0:b - a], lhsT=ones[0:1, :], rhs=rows1[0:1, a:b],
                                                         start=True, stop=True), r=["ones", "rows1"], w=[pk[1 + i]])
                op("act", lambda a=a, b=b, i=i: A.copy(out=rowsb[:, a:b], in_=PS[1 + i][:, 0:b - a]), r=[pk[1 + i]], w=["rowsb"])
            op("act", lambda: A.activation(out=sc[:, :, 0], in_=cols[:, C_C:C_C + 8], func=AF.Silu), r=["cols"], w=["sc"])
            op("act", lambda: A.activation(out=sc[:, :, 1], in_=cols[:, C_CC:C_CC + 8], func=AF.Silu), r=["cols"], w=["sc"])
            it = 0
            for l in range(2):
                for cg in range(18):
                    wa = was[it % 2]
                    wk = ("wa", it % 2)
                    pz = 4 + (it % 2)
                    dma("sp", wa[:, :, :], w_ada[l][:, cg * 512:(cg + 1) * 512].rearrange("(k p) c -> p k c", p=128), w=[wk])
                    for cc in range(4):
                        for kk in range(8):
                            op("pe", lambda cc=cc, kk=kk, wa=wa, pz=pz: PE.matmul(
                                PS[pz][:, cc * 2:cc * 2 + 2], lhsT=wa[:, kk, cc * 128:(cc + 1) * 128], rhs=sc[:, kk, :],
                                start=(kk == 0), stop=(kk == 7)), r=[wk, "sc"], w=[pk[pz]])
                    op("dve", lambda l=l, cg=cg, pz=pz: V.tensor_tensor(
                        out=mcol[:, l, cg * 4:(cg + 1) * 4, :], in0=PS[pz][:, 0:8].rearrange("p (c v) -> p c v", v=2),
                        in1=cols[:, C_BADA + l * 72 + cg * 4:C_BADA + l * 72 + cg * 4 + 4].unsqueeze(2).broadcast_to([128, 4, 2]),
                        op=ALU.add), r=[pk[pz], "cols"], w=["mcol"])
                    it += 1
            for l in range(2):
                for s in range(3):
                    for v in range(2):
                        op("dve", lambda l=l, s=s, v=v: V.scalar_tensor_tensor(
                            out=gsall[:, l, s, v, :], in0=mcol[:, l, (s * 3 + 1) * 8:(s * 3 + 2) * 8, v], scalar=1.0,
                            in1=cols[:, C_GSUB + (l * 3 + s) * 8:C_GSUB + (l * 3 + s) * 8 + 8], op0=ALU.add, op1=ALU.mult),
                           r=["mcol", "cols"], w=["gsall"])
            k.barrier()

        def sh_ap(l, s, v):
            return mcol[:, l, (s * 3) * 8:(s * 3) * 8 + 8, v]

        def gate_ap(l, s, v):
            return mcol[:, l, (s * 3 + 2) * 8:(s * 3 + 2) * 8 + 8, v]

        def gs_ap(l, s, v):
            return gsall[:, l, s, v, :]

        def bcast_row(dst, dkey, col_ap, factor, dg):
            for j in range(8):
                op("dve", lambda j=j: V.tensor_scalar(out=dg[:, :], in0=ident[:], scalar1=col_ap[:, j:j + 1], scalar2=float(factor),
                                                      op0=ALU.mult, op1=ALU.mult), r=["ident", "mcol"], w=["dg"])
                op("pe", lambda j=j: PE.matmul(PS[7][:, 0:128], lhsT=ones[:, :], rhs=dg[:, :], start=True, stop=True),
                   r=["dg", "ones"], w=[pk[7]])
                op("act", lambda j=j: A.copy(out=dst[:, j * 128:(j + 1) * 128], in_=PS[7][:, 0:128]), r=[pk[7]], w=[dkey])

        def make_ln(ph):
            S = {"junk": [sb(ph, f"lnj{i}", [128, 1024]) for i in range(2)],
                 "xs": [sb(ph, f"lnx{i}", [128, 1024]) for i in range(2)],
                 "st": [sb(ph, f"lns{i}", [128, 4]) for i in range(2)], "n": 0}

            def ln_A(xap, xkey):
                i = S["n"] % 2
                S["n"] += 1
                junk, xs, st = S["junk"][i], S["xs"][i], S["st"][i]
                kj, kx, ks = ("lnj", i), ("lnx", i), ("lns", i)
                op("act", lambda: A.activation(out=junk[:], in_=xap, func=AF.Square, accum_out=st[:, 0:1]), r=[xkey], w=[kj, ks])
                op("act", lambda: A.activation(out=st[:, 1:2], in_=st[:, 0:1], func=AF.Ln, bias=EPS, scale=1.0 / D), r=[ks], w=[ks])
                op("act", lambda: A.activation(out=st[:, 2:3], in_=st[:, 1:2], func=AF.Exp, scale=-0.5), r=[ks], w=[ks])
                op("dve", lambda: V.tensor_scalar(out=xs[:], in0=xap, scalar1=st[:, 2:3], scalar2=None, op0=ALU.mult),
                   r=[xkey, ks], w=[kx])
                return i

            def ln_B(i, gs, sh, hdst, hkey, pbanks):
                xs = S["xs"][i]
                kx = ("lnx", i)
                for half in range(2):
                    pb = pbanks[half]
                    for j4 in range(4):
                        j = half * 4 + j4
                        op("pe", lambda j=j, j4=j4, pb=pb: PE.transpose(out=PS[pb][:, j4 * 128:(j4 + 1) * 128],
                                                                        in_=xs[:, j * 128:(j + 1) * 128], identity=ident[:]),
                           r=[kx, "ident"], w=[pk[pb]])
                    for j4 in range(4):
                        j = half * 4 + j4
                        op("act", lambda j=j, j4=j4, pb=pb: A.activation(out=hdst(j), in_=PS[pb][:, j4 * 128:(j4 + 1) * 128],
                                                                         func=AF.Identity, scale=gs[:, j:j + 1], bias=sh[:, j:j + 1]),
                           r=[pk[pb], "mcol", "gsall"], w=[hkey])

            def ln_T(xap, xkey, gs, sh, hdst, hkey, pbanks):
                ln_B(ln_A(xap, xkey), gs, sh, hdst, hkey, pbanks)
            ln_T.A = ln_A
            ln_T.B = ln_B
            return ln_T

        def tiles_for(first, with_ctx):
            tl = []
            if with_ctx:
                tl.append((1, ctx_in[:, :] if first else X[0:CTX, :], X[0:CTX, :]))
            for i in range(SEQ // 256):
                src = x_in[i * 256:(i + 1) * 256, :] if first else X[CTX + i * 256:CTX + (i + 1) * 256, :]
                tl.append((0, src, X[CTX + i * 256:CTX + (i + 1) * 256, :]))
            return tl

        def ffn_phase(l, s, fi, first, with_ctx):
            with contextlib.ExitStack() as ph:
                w1 = sb(ph, "w1", [128, 8, 2 * DFF], BF16)
                w2 = sb(ph, "w2", [128, NF, D], BF16)
                xts = [sb(ph, f"xt{i}", [128, 2, D]) for i in range(2)]
                hT = sb(ph, "hT", [128, 8, 256], BF16)
                actT = sb(ph, "actT", [128, NF, 256], BF16)
                sg = [sb(ph, f"sg{i}", [128, 256]) for i in range(2)]
                tmp = [sb(ph, f"tmp{i}", [128, 512]) for i in range(2)]
                gb = [sb(ph, f"gb{i}", [128, D]) for i in range(2)]
                dg = sb(ph, "dg", [128, 128])
                ln_T = make_ln(ph)
                for fb in range(NF // 2):
                    for which in range(2):
                        c0 = which * DFF + fb * 256
                        dma("pool", w1[:, :, c0:c0 + 256], ffn_w_in[fi][:, c0:c0 + 256].rearrange("(k p) c -> p k c", p=128),
                            w=[("w1", which, fb)])
                for f in range(NF):
                    dma("pool", w2[:, f, :], ffn_w_out[fi][f * 128:(f + 1) * 128, :], w=[("w2", f)])
                for v in range(2):
                    if v == 1 and not with_ctx:
                        continue
                    bcast_row(gb[v], ("gb", v), gate_ap(l, s, v), 0.5, dg)
                n = 0
                tl = tiles_for(first, with_ctx)

                def load(ti):
                    dma("sp", xts[ti % 2][:, :, :], tl[ti][1].rearrange("(b p) d -> p b d", p=128), w=[("xt", ti % 2)])

                def lnA(ti):
                    return [ln_T.A(xts[ti % 2][:, b, :], ("xt", ti % 2)) for b in range(2)]

                load(0)
                hA = lnA(0)
                for ti, (v, src, dst) in enumerate(tl):
                    xt = xts[ti % 2]
                    xk = ("xt", ti % 2)
                    if ti + 1 < len(tl):
                        load(ti + 1)
                    for b in range(2):
                        ln_T.B(hA[b], gs_ap(l, s, v), sh_ap(l, s, v), lambda j, b=b: hT[:, j, b * 128:(b + 1) * 128], "hT", (0, 1))
                    for f in range(NF):
                        pg, pu = 2 + 2 * (f % 2), 3 + 2 * (f % 2)
                        for which, pb in ((0, pg), (1, pu)):
                            c0 = which * DFF + f * 128
                            for kk in range(8):
                                op("pe", lambda kk=kk, pb=pb, c0=c0: PE.matmul(PS[pb][:, 0:256], lhsT=w1[:, kk, c0:c0 + 128],
                                                                               rhs=hT[:, kk, :], start=(kk == 0), stop=(kk == 7)),
                                   r=[("w1", which, f // 2), "hT"], w=[pk[pb]])
                        sgi = sg[f % 2]
                        op("act", lambda pg=pg, sgi=sgi: A.activation(out=sgi[:], in_=PS[pg][:, 0:256], func=AF.Silu),
                           r=[pk[pg]], w=[("sg", f % 2)])
                        op("dve", lambda pu=pu, sgi=sgi, f=f: V.tensor_tensor(out=actT[:, f, :], in0=PS[pu][:, 0:256], in1=sgi[:],
                                                                              op=ALU.mult), r=[pk[pu], ("sg", f % 2)], w=[("actT", f)])
                    if ti + 1 < len(tl):
                        hA = lnA(ti + 1)
                    for tb in range(2):
                        for dh in range(2):
                            po = n % 2
                            tm = tmp[n % 2]
                            n += 1
                            for f in range(NF):
                                op("pe", lambda f=f, po=po, tb=tb, dh=dh: PE.matmul(
                                    PS[po][:, :], lhsT=actT[:, f, tb * 128:(tb + 1) * 128], rhs=w2[:, f, dh * 512:(dh + 1) * 512],
                                    start=(f == 0), stop=(f == NF - 1)), r=[("actT", f), ("w2", f)], w=[pk[po]])
                            op("dve", lambda po=po, tm=tm, dh=dh, v=v: V.tensor_tensor(
                                out=tm[:], in0=PS[po][:, :], in1=gb[v][:, dh * 512:(dh + 1) * 512], op=ALU.mult),
                               r=[pk[po], ("gb", v)], w=[("tmp", po)])
                            op("pool", lambda tm=tm, tb=tb, dh=dh, xt=xt: PO.tensor_tensor(
                                out=xt[:, tb, dh * 512:(dh + 1) * 512], in0=tm[:], in1=xt[:, tb, dh * 512:(dh + 1) * 512], op=ALU.add),
                               r=[("tmp", po), xk], w=[xk])
                    dma("sp", dst.rearrange("(b p) d -> p b d", p=128), xt[:, :, :], r=[xk])
                k.barrier()

        def lru_phase():
            l, s = 0, 1
            cw = lambda j, ch: cols[:, C_LCW + j * 8 + ch:C_LCW + j * 8 + ch + 1]
            cb = lambda ch: cols[:, C_LCB + ch:C_LCB + ch + 1]
            bg = lambda d, g, ch: cols[:, C_LBG + (d * 2 + g) * 8 + ch:C_LBG + (d * 2 + g) * 8 + ch + 1]
            tl = tiles_for(False, True)
            toff = lambda ti: 0 if ti == 0 else CTX + (ti - 1) * 256
            poff = lambda ti: CTX0 if ti == 0 else LAT0 + (ti - 1) * 256
            with contextlib.ExitStack() as ph:
                wi = sb(ph, "lwi", [128, 8, 2048], BF16)
                xts = [sb(ph, f"xt{i}", [128, 2, D]) for i in range(2)]
                hT = sb(ph, "hT", [128, 8, 256], BF16)
                pts = [sb(ph, f"pt{i}", [128, 16, 256]) for i in range(2)]
                ln_T = make_ln(ph)
                for kk in range(8):
                    dma("pool", wi[:, kk, :], lru_w_in[kk * 128:(kk + 1) * 128, :], w=[("lwi", kk)])
                for ti, (v, src, dst) in enumerate(tl):
                    xt = xts[ti % 2]
                    xk = ("xt", ti % 2)
                    pt = pts[ti % 2]
                    ptk = ("pt", ti % 2)
                    dma("sp", xt[:, :, :], src.rearrange("(b p) d -> p b d", p=128), w=[xk])
                    for b in range(2):
                        ln_T(xt[:, b, :], xk, gs_ap(l, s, v), sh_ap(l, s, v),
                             lambda j, b=b: hT[:, j, b * 128:(b + 1) * 128], "hT", (0, 1))
                    for ch in range(16):
                        pb = 2 + ch % 4
                        for kk in range(8):
                            op("pe", lambda kk=kk, pb=pb, ch=ch: PE.matmul(PS[pb][:, 0:256], lhsT=wi[:, kk, ch * 128:(ch + 1) * 128],
                                                                           rhs=hT[:, kk, :], start=(kk == 0), stop=(kk == 7)),
                               r=[("lwi", kk), "hT"], w=[pk[pb]])
                        if ch % 2 == 0:
                            op("act", lambda pb=pb, ch=ch, pt=pt: A.copy(out=pt[:, ch, :], in_=PS[pb][:, 0:256]), r=[pk[pb]], w=[ptk])
                        else:
                            op("dve", lambda pb=pb, ch=ch, pt=pt: V.tensor_copy(out=pt[:, ch, :], in_=PS[pb][:, 0:256]), r=[pk[pb]], w=[ptk])
                    dma("sp", PT[:, :, poff(ti):poff(ti) + 256].rearrange("c p t -> p c t"), pt[:, :, :], r=[ptk])
                k.barrier()
            with contextlib.ExitStack() as ph:
                wg = sb(ph, "wg", [128, 32, 256], BF16)
                coef = sb(ph, "coef", [128, 16])
                coef2 = sb(ph, "coef2", [128, 16])
                ee = sb(ph, "ee", [128, 16])
                pp = sb(ph, "pp", [128, 16])
                upre = sb(ph, "upre", [128, 2, TP])
                u = sb(ph, "u", [128, 2, T])
                ub = sb(ph, "ub", [128, 2, T], BF16)
                hsum = sb(ph, "hsum", [128, 2, T])
                tr = [[sb(ph, f"lt{n}_{i}", [128, 256]) for i in range(5)] for n in range(4)]
                sts = [[sb(ph, f"lst{d}{oc}", [128, 1]) for oc in range(2)] for d in range(2)]
                for q in range(16):
                    dma("pool", wg[:, q * 2:(q + 1) * 2, :], lru_w_gate[q].rearrange("(kc p) j -> p kc j", p=128), w=[("wg", q)])
                op("act", lambda: A.activation(out=ee[:], in_=cols[:, C_LAM:C_LAM + 16], func=AF.Exp, scale=-1.0), r=["cols"], w=["ee"])
                op("dve", lambda: V.memset(pp[:], 1.0 / 12), w=["pp"])
                for n in range(11, 0, -1):
                    op("dve", lambda: V.tensor_tensor(out=pp[:], in0=pp[:], in1=ee[:], op=ALU.mult), r=["pp", "ee"], w=["pp"])
                    op("dve", lambda n=n: V.tensor_scalar(out=pp[:], in0=pp[:], scalar1=-1.0, scalar2=1.0 / n, op0=ALU.mult, op1=ALU.add),
                       r=["pp"], w=["pp"])
                op("dve", lambda: V.scalar_tensor_tensor(out=coef[:], in0=pp[:], scalar=-8.0, in1=ee[:], op0=ALU.mult, op1=ALU.mult),
                   r=["pp", "ee"], w=["coef"])
                op("dve", lambda: V.tensor_scalar(out=coef2[:], in0=coef[:], scalar1=2.0, scalar2=None, op0=ALU.mult), r=["coef"], w=["coef2"])
                for hd in range(4):
                    dma("sp", upre[:, :, :], PT[8 + 2 * hd:8 + 2 * hd + 2, :, :].rearrange("c p t -> p c t"), w=["upre"])
                    for (a, b) in ((0, 2), (CTX0 + CTX, LAT0), (TP - 1, TP)):
                        op("pool", lambda a=a, b=b: PO.memset(upre[:, :, a:b], 0.0), r=["upre"], w=["upre"])
                    for oc in range(2):
                        ch = 2 * hd + oc
                        for (o0, n, base) in ((0, CTX, CTX0), (CTX, SEQ, LAT0)):
                            op("dve", lambda oc=oc, ch=ch, o0=o0, n=n, base=base: V.tensor_scalar(
                                out=u[:, oc, o0:o0 + n], in0=upre[:, oc, base - 2:base - 2 + n], scalar1=cw(0, ch), scalar2=cb(ch),
                                op0=ALU.mult, op1=ALU.add), r=["upre", "cols"], w=[("u", oc)])
                            for j in range(1, 4):
                                op("dve", lambda oc=oc, ch=ch, o0=o0, n=n, base=base, j=j: V.scalar_tensor_tensor(
                                    out=u[:, oc, o0:o0 + n], in0=upre[:, oc, base - 2 + j:base - 2 + j + n], scalar=cw(j, ch),
                                    in1=u[:, oc, o0:o0 + n], op0=ALU.mult, op1=ALU.add), r=["upre", "cols", ("u", oc)], w=[("u", oc)])
                        op("act", lambda oc=oc: A.copy(out=ub[:, oc, :], in_=u[:, oc, :]), r=[("u", oc)], w=["ub"])
                    for oc in range(2):
                        op("pool", lambda oc=oc: PO.memset(hsum[:, oc, :], 0.0), w=[("hsum", oc)])
                    for d in range(2):
                        for oc in range(2):
                            op("dve", lambda d=d, oc=oc: V.memset(sts[d][oc][:], 0.0), w=[("lst", d, oc)])
                    order = [list(range(17)), [0] + list(range(16, 0, -1))]
                    for step in range(17):
                        chains = [(d, oc) for d in range(2) for oc in range(2)]
                        for ci, (d, oc) in enumerate(chains):
                            ti = order[d][step]
                            t0 = toff(ti)
                            ch = 2 * hd + oc
                            R_, I_ = tr[ci][0], tr[ci][1]
                            pr, pi = 2 * ci, 2 * ci + 1
                            for g, pb in ((0, pr), (1, pi)):
                                for kc in range(2):
                                    op("pe", lambda g=g, pb=pb, kc=kc, d=d, oc=oc, t0=t0: PE.matmul(
                                        PS[pb][:, 0:256], lhsT=wg[:, ((d * 2 + g) * 4 + hd) * 2 + kc, oc * 128:(oc + 1) * 128],
                                        rhs=ub[:, kc, t0:t0 + 256], start=(kc == 0), stop=(kc == 1)),
                                       r=[("wg", (d * 2 + g) * 4 + hd), "ub"], w=[pk[pb]])
                            op("act", lambda: A.activation(out=R_[:], in_=PS[pr][:, 0:256], func=AF.Sigmoid, bias=bg(d, 0, ch)),
                               r=[pk[pr], "cols"], w=[("ltR", ci)])
                            op("act", lambda: A.activation(out=I_[:], in_=PS[pi][:, 0:256], func=AF.Sigmoid, bias=bg(d, 1, ch)),
                               r=[pk[pi], "cols"], w=[("ltI", ci)])
                        for ci, (d, oc) in enumerate(chains):
                            ch = 2 * hd + oc
                            cidx = d * 8 + ch
                            R_, A_, SQ_ = tr[ci][0], tr[ci][2], tr[ci][3]
                            op("act", lambda: A.activation(out=A_[:], in_=R_[:], func=AF.Exp, scale=coef[:, cidx:cidx + 1]),
                               r=[("ltR", ci), "coef"], w=[("ltA", ci)])
                            op("pool", lambda: PO.tensor_tensor(out=SQ_[:], in0=A_[:], in1=A_[:], op=ALU.mult),
                               r=[("ltA", ci)], w=[("ltS", ci)])
                            op("act", lambda: A.activation(out=SQ_[:], in_=SQ_[:], func=AF.Ln, scale=-1.0, bias=1.0),
                               r=[("ltS", ci)], w=[("ltS", ci)])
                            op("act", lambda: A.activation(out=SQ_[:], in_=SQ_[:], func=AF.Exp, scale=0.5),
                               r=[("ltS", ci)], w=[("ltS", ci)])
                        for ci, (d, oc) in enumerate(chains):
                            ti = order[d][step]
                            t0 = toff(ti)
                            I_, A_, SQ_, HB_ = tr[ci][1], tr[ci][2], tr[ci][3], tr[ci][4]
                            stt = sts[d][oc]
                            sk = ("lst", d, oc)
                            hk = ("hsum", oc)
                            op("dve", lambda: V.tensor_tensor(out=I_[:], in0=I_[:], in1=u[:, oc, t0:t0 + 256], op=ALU.mult),
                               r=[("ltI", ci), ("u", oc)], w=[("ltI", ci)])
                            op("dve", lambda: V.tensor_tensor(out=I_[:], in0=I_[:], in1=SQ_[:], op=ALU.mult),
                               r=[("ltI", ci), ("ltS", ci)], w=[("ltI", ci)])
                            if d == 0:
                                op("dve", lambda: V.tensor_tensor_scan(out=HB_[:], data0=A_[:], data1=I_[:],
                                                                       initial=stt[:, 0:1], op0=ALU.mult, op1=ALU.add),
                                   r=[("ltA", ci), ("ltI", ci), sk], w=[("ltH", ci)])
                                op("dve", lambda: V.tensor_copy(out=stt[:, 0:1], in_=HB_[:, 255:256]), r=[("ltH", ci)], w=[sk])
                            else:
                                op("dve", lambda: V.tensor_tensor_scan(out=HB_[:, ::-1], data0=A_[:, ::-1], data1=I_[:, ::-1],
                                                                       initial=stt[:, 0:1], op0=ALU.mult, op1=ALU.add),
                                   r=[("ltA", ci), ("ltI", ci), sk], w=[("ltH", ci)])
                                op("dve", lambda: V.tensor_copy(out=stt[:, 0:1], in_=HB_[:, 0:1]), r=[("ltH", ci)], w=[sk])
                            op("pool", lambda: PO.tensor_tensor(out=hsum[:, oc, t0:t0 + 256], in0=hsum[:, oc, t0:t0 + 256],
                                                                in1=HB_[:], op=ALU.add), r=[("ltH", ci), hk], w=[hk])
                    dma("sp", upre[:, :, 0:CTX], PT[2 * hd:2 * hd + 2, :, CTX0:CTX0 + CTX].rearrange("c p t -> p c t"), w=["upre"])
                    dma("sp", upre[:, :, CTX:T], PT[2 * hd:2 * hd + 2, :, LAT0:LAT0 + SEQ].rearrange("c p t -> p c t"), w=["upre"])
                    for oc in range(2):
                        op("act", lambda oc=oc: A.activation(out=upre[:, oc, 0:T], in_=upre[:, oc, 0:T], func=AF.Gelu_apprx_tanh),
                           r=["upre"], w=["upre"])
                        op("dve", lambda oc=oc: V.tensor_tensor(out=ub[:, oc, :], in0=upre[:, oc, 0:T], in1=hsum[:, oc, :], op=ALU.mult),
                           r=["upre", ("hsum", oc)], w=["ub"])
                    dma("sp", ZT[2 * hd:2 * hd + 2, :, :].rearrange("c p t -> p c t"), ub[:, :, :], r=["ub"])
                k.barrier()
            with contextlib.ExitStack() as ph:
                wo = sb(ph, "lwo", [128, 8, D], BF16)
                gb = [sb(ph, f"gb{i}", [128, D]) for i in range(2)]
                dg = sb(ph, "dg", [128, 128])
                xts = [sb(ph, f"xt{i}", [128, 2, D]) for i in range(2)]
                zts = [sb(ph, f"zt{i}", [128, 8, 256], BF16) for i in range(2)]
                tmp = [sb(ph, f"tmp{i}", [128, 512]) for i in range(2)]
                for c in range(8):
                    dma("pool", wo[:, c, :], lru_w_out[c * 128:(c + 1) * 128, :], w=[("lwo", c)])
                for v in range(2):
                    bcast_row(gb[v], ("gb", v), gate_ap(l, s, v), 1.0, dg)
                n = 0
                for ti, (v, src, dst) in enumerate(tl):
                    xt, xk = xts[ti % 2], ("xt", ti % 2)
                    zt, zk = zts[ti % 2], ("zt", ti % 2)
                    dma("sp", xt[:, :, :], src.rearrange("(b p) d -> p b d", p=128), w=[xk])
                    dma("sp", zt[:, :, :], ZT[:, :, toff(ti):toff(ti) + 256].rearrange("c p t -> p c t"), w=[zk])
                    for tb in range(2):
                        for dh in range(2):
                            po = n % 2
                            tm = tmp[n % 2]
                            n += 1
                            for c in range(8):
                                op("pe", lambda c=c, po=po, tb=tb, dh=dh, zt=zt: PE.matmul(
                                    PS[po][:, :], lhsT=zt[:, c, tb * 128:(tb + 1) * 128], rhs=wo[:, c, dh * 512:(dh + 1) * 512],
                                    start=(c == 0), stop=(c == 7)), r=[zk, ("lwo", c)], w=[pk[po]])
                            op("dve", lambda po=po, tm=tm, dh=dh, v=v: V.tensor_tensor(
                                out=tm[:], in0=PS[po][:, :], in1=gb[v][:, dh * 512:(dh + 1) * 512], op=ALU.mult),
                               r=[pk[po], ("gb", v)], w=[("tmp", po)])
                            op("pool", lambda tm=tm, tb=tb, dh=dh, xt=xt: PO.tensor_tensor(
                                out=xt[:, tb, dh * 512:(dh + 1) * 512], in0=tm[:], in1=xt[:, tb, dh * 512:(dh + 1) * 512], op=ALU.add),
                               r=[("tmp", po), xk], w=[xk])
                    dma("sp", dst.rearrange("(b p) d -> p b d", p=128), xt[:, :, :], r=[xk])
                k.barrier()

        def dn_phase():
            l, s = 1, 1
            toffs = [0] + [CTX + i * 256 for i in range(16)]
            poffs = [CTX0] + [LAT0 + i * 256 for i in range(16)]
            with contextlib.ExitStack() as ph:
                wi = sb(ph, "dwi", [128, 8, 4128], BF16)
                xts = [sb(ph, f"xt{i}", [128, 2, D]) for i in range(2)]
                hT = sb(ph, "hT", [128, 8, 256], BF16)
                pq = sb(ph, "pq", [128, 24, 256])
                pzs = [sb(ph, f"pz{i}", [128, 8, 256]) for i in range(2)]
                gbt = [sb(ph, f"gbt{i}", [128, 2, 32]) for i in range(2)]
                t1 = [sb(ph, f"t1{i}", [128, 16]) for i in range(2)]
                negA = sb(ph, "negA", [128, 16])
                ln_T = make_ln(ph)
                for kk in range(8):
                    dma("pool", wi[:, kk, :], dn_w_in[kk * 128:(kk + 1) * 128, :], w=[("dwi", kk)])
                op("act", lambda: A.activation(out=negA[:], in_=rowsb[:, 0:16], func=AF.Exp), r=["rowsb"], w=["negA"])
                op("dve", lambda: V.tensor_scalar(out=negA[:], in0=negA[:], scalar1=-1.0, scalar2=None, op0=ALU.mult), r=["negA"], w=["negA"])
                nb = 0
                for ti in range(17):
                    v = 1 if ti == 0 else 0
                    xt, xk = xts[ti % 2], ("xt", ti % 2)
                    pz, pzk = pzs[ti % 2], ("pz", ti % 2)
                    if ti == 0:
                        dma("sp", xt[:, :, :], X[0:CTX, :].rearrange("(b p) d -> p b d", p=128), w=[xk])
                    else:
                        for b in range(2):
                            for cl in range(2):
                                col = 4 * (ti - 1) + 2 * b + cl
                                dma("sp", xt[cl * 64:(cl + 1) * 64, b, :], Xlat_cm[col], w=[xk])
                    for b in range(2):
                        ln_T(xt[:, b, :], xk, gs_ap(l, s, v), sh_ap(l, s, v),
                             lambda j, b=b: hT[:, j, b * 128:(b + 1) * 128], "hT", (0, 1))
                    for ch in range(32):
                        pb = 2 + ch % 4
                        for kk in range(8):
                            op("pe", lambda kk=kk, pb=pb, ch=ch: PE.matmul(PS[pb][:, 0:256], lhsT=wi[:, kk, ch * 128:(ch + 1) * 128],
                                                                           rhs=hT[:, kk, :], start=(kk == 0), stop=(kk == 7)),
                               r=[("dwi", kk), "hT"], w=[pk[pb]])
                        if ch >= 24:
                            op("act", lambda pb=pb, ch=ch, pz=pz: A.activation(out=pz[:, ch - 24, :], in_=PS[pb][:, 0:256], func=AF.Silu),
                               r=[pk[pb]], w=[pzk])
                        elif ch % 2 == 0:
                            op("act", lambda pb=pb, ch=ch: A.copy(out=pq[:, ch, :], in_=PS[pb][:, 0:256]), r=[pk[pb]], w=["pq"])
                        else:
                            op("dve", lambda pb=pb, ch=ch: V.tensor_copy(out=pq[:, ch, :], in_=PS[pb][:, 0:256]), r=[pk[pb]], w=["pq"])
                    for b in range(2):
                        pb = 6 + b
                        g_, t_ = gbt[nb % 2], t1[nb % 2]
                        gk, tk = ("gbt", nb % 2), ("t1", nb % 2)
                        nb += 1
                        for kk in range(8):
                            op("pe", lambda kk=kk, pb=pb, b=b: PE.matmul(PS[pb][:, 0:32], lhsT=hT[:, kk, b * 128:(b + 1) * 128],
                                                                         rhs=wi[:, kk, 4096:4128], start=(kk == 0), stop=(kk == 7)),
                               r=[("dwi", kk), "hT"], w=[pk[pb]])
                        op("dve", lambda: V.tensor_tensor(out=t_[:], in0=PS[pb][:, 0:16], in1=rowsb[:, 16:32], op=ALU.add),
                           r=[pk[pb], "rowsb"], w=[tk])
                        op("act", lambda: A.activation(out=t_[:], in_=t_[:], func=AF.Exp), r=[tk], w=[tk])
                        op("act", lambda: A.activation(out=t_[:], in_=t_[:], func=AF.Ln, bias=1.0), r=[tk], w=[tk])
                        op("dve", lambda: V.tensor_tensor(out=g_[:, 0, 0:16], in0=t_[:], in1=negA[:], op=ALU.mult), r=[tk, "negA"], w=[gk])
                        op("act", lambda: A.activation(out=g_[:, 0, 16:32], in_=PS[pb][:, 16:32], func=AF.Sigmoid), r=[pk[pb]], w=[gk])
                        r0 = toffs[ti] + b * 128
                        dma("sp", GD[r0:r0 + 128, :], g_[:, 0, 0:16], r=[gk])
                        dma("sp", BD[r0:r0 + 128, :], g_[:, 0, 16:32], r=[gk])
                    dma("sp", QP[:, :, poffs[ti]:poffs[ti] + 256].rearrange("c p t -> p c t"), pq[:, :, :], r=["pq"])
                    dma("sp", ZD[:, :, toffs[ti]:toffs[ti] + 256].rearrange("c p t -> p c t"), pz[:, :, :], r=[pzk])
                k.barrier()
            with contextlib.ExitStack() as ph:
                pres = [sb(ph, f"pre{i}", [128, TP]) for i in range(2)]
                vals = [sb(ph, f"val{i}", [128, T]) for i in range(2)]
                toks = [sb(ph, f"tok{i}", [128, 34, 128]) for i in range(2)]
                sqs = [sb(ph, f"sq{i}", [128, 512]) for i in range(2)]
                rins = [sb(ph, f"rin{i}", [128, 512]) for i in range(2)]
                nt = 0
                for c in range(24):
                    pre, prk = pres[c % 2], ("pre", c % 2)
                    val, vk = vals[c % 2], ("val", c % 2)
                    dma("sp", pre[:, :], QP[c], w=[prk])
                    for (a, b) in ((0, 2), (CTX0 + CTX, LAT0), (TP - 1, TP)):
                        op("pool", lambda a=a, b=b, pre=pre: PO.memset(pre[:, a:b], 0.0), r=[prk], w=[prk])
                    cw = lambda j: cols[:, C_DCW + j * 24 + c:C_DCW + j * 24 + c + 1]
                    for (o0, n, base) in ((0, CTX, CTX0), (CTX, SEQ, LAT0)):
                        op("dve", lambda o0=o0, n=n, base=base: V.tensor_scalar(
                            out=val[:, o0:o0 + n], in0=pre[:, base - 2:base - 2 + n], scalar1=cw(0), scalar2=None, op0=ALU.mult),
                           r=[prk, "cols"], w=[vk])
                        for j in range(1, 4):
                            op("dve", lambda o0=o0, n=n, base=base, j=j: V.scalar_tensor_tensor(
                                out=val[:, o0:o0 + n], in0=pre[:, base - 2 + j:base - 2 + j + n], scalar=cw(j),
                                in1=val[:, o0:o0 + n], op0=ALU.mult, op1=ALU.add), r=[prk, "cols", vk], w=[vk])
                    op("act", lambda: A.activation(out=val[:, :], in_=val[:, :], func=AF.Silu), r=[vk], w=[vk])
                    if c < 16:
                        for a in range(0, T, 512):
                            n = min(512, T - a)
                            sq, sk = sqs[nt % 2], ("sq", nt % 2)
                            rin, rk = rins[nt % 2], ("rin", nt % 2)
                            pb = nt % 2
                            nt += 1
                            op("act", lambda: A.activation(out=sq[:, 0:n], in_=val[:, a:a + n], func=AF.Square), r=[vk], w=[sk])
                            op("pe", lambda: PE.matmul(PS[pb][:, 0:n], lhsT=ones[:, :], rhs=sq[:, 0:n], start=True, stop=True),
                               r=[sk, "ones"], w=[pk[pb]])
                            op("act", lambda: A.activation(out=rin[:, 0:n], in_=PS[pb][:, 0:n], func=AF.Ln, bias=EPS), r=[pk[pb]], w=[rk])
                            op("act", lambda: A.activation(out=rin[:, 0:n], in_=rin[:, 0:n], func=AF.Exp, scale=-0.5,
                                                           bias=(math.log(128.0 ** -0.5) if c < 8 else 0.0)), r=[rk], w=[rk])
                            op("dve", lambda: V.tensor_tensor(out=val[:, a:a + n], in0=val[:, a:a + n], in1=rin[:, 0:n], op=ALU.mult),
                               r=[vk, rk], w=[vk])
                    if c < 8:
                        dma("sp", QT[c], val[:, :], r=[vk])
                    elif c < 16:
                        dma("sp", KT[c - 8], val[:, :], r=[vk])
                    if c >= 8:
                        tok, tkk = toks[c % 2], ("tok", c % 2)
                        for b0 in range(0, 34, 4):
                            nbk = min(4, 34 - b0)
                            pb = 2 + (b0 // 4) % 4
                            for bb in range(nbk):
                                blk = b0 + bb
                                op("pe", lambda bb=bb, blk=blk, pb=pb: PE.transpose(out=PS[pb][:, bb * 128:(bb + 1) * 128],
                                                                                     in_=val[:, blk * 128:(blk + 1) * 128], identity=ident[:]),
                                   r=[vk, "ident"], w=[pk[pb]])
                            src_ap = PS[pb][:, 0:nbk * 128].rearrange("p (b f) -> p b f", f=128)
                            if (b0 // 4) % 2 == 0:
                                op("act", lambda: A.copy(out=tok[:, b0:b0 + nbk, :], in_=src_ap), r=[pk[pb]], w=[tkk])
                            else:
                                op("dve", lambda: V.tensor_copy(out=tok[:, b0:b0 + nbk, :], in_=src_ap), r=[pk[pb]], w=[tkk])
                        dst = (KTOK[c - 8] if c < 16 else VTOK[c - 16]).rearrange("(b p) f -> p b f", p=128)
                        dma("sp", dst, tok[:, :, :], r=[tkk])
                k.barrier()
            with contextlib.ExitStack() as ph:
                W3 = [64, 8, 64]
                offd_t = sb(ph, "offd", [128, 64])
                mi_t = sb(ph, "mi", [128, 64]); ntm_t = sb(ph, "ntm", [128, 64]); ncm_t = sb(ph, "ncm", [128, 64])
                MI, NT_, NC_, OFFD, I64 = [], [], [], [], []
                for d in range(2):
                    po = d * 64
                    v64 = iot[po:po + 64, po:po + 64]
                    mi, ntm, ncm, offd = mi_t[po:po + 64, :], ntm_t[po:po + 64, :], ncm_t[po:po + 64, :], offd_t[po:po + 64, :]
                    op("dve", lambda: V.tensor_scalar(out=offd, in0=v64, scalar1=0.0, scalar2=None, op0=ALU.not_equal), r=["iot"], w=["msk"])
                    op("dve", lambda: V.tensor_scalar(out=mi, in0=v64, scalar1=0.0, scalar2=None,
                                                      op0=(ALU.is_ge if d == 0 else ALU.is_le)), r=["iot"], w=["msk"])
                    op("dve", lambda: V.tensor_scalar(out=ntm, in0=mi, scalar1=-1.0, scalar2=-NEG, op0=ALU.add, op1=ALU.mult),
                       r=["msk"], w=["msk"])
                    op("dve", lambda: V.tensor_scalar(out=ncm, in0=v64, scalar1=0.0, scalar2=None,
                                                      op0=(ALU.is_lt if d == 0 else ALU.is_gt)), r=["iot"], w=["msk"])
                    op("dve", lambda: V.tensor_scalar(out=ncm, in0=ncm, scalar1=-1.0, scalar2=-NEG, op0=ALU.add, op1=ALU.mult),
                       r=["msk"], w=["msk"])
                    MI.append(mi); NT_.append(ntm); NC_.append(ncm); OFFD.append(offd); I64.append(ident[po:po + 64, po:po + 64])
                bcm = lambda m: m.unsqueeze(1).broadcast_to(W3)
                bc = lambda ap2, n: ap2.unsqueeze(2).broadcast_to([ap2.shape[0], 8, n])
                TL = [[None, None], [None, None]]
                for par in range(2):
                    big = {}
                    for nm in ("ktok", "vtok", "kbg", "vb", "kd", "vn"):
                        big[nm] = sb(ph, f"{nm}B{par}", [128, 8, 128])
                    for nm in ("gM", "bdg", "E1", "E2", "DTi", "DCs", "DTs", "Q0", "P0", "Q1", "P1", "QKm", "R", "Rr", "tmpw"):
                        big[nm] = sb(ph, f"{nm}B{par}", [128, 8, 64])
                    for nm in ("g16", "b16"):
                        big[nm] = sb(ph, f"{nm}B{par}", [128, 16])
                    for nm in ("gcc", "egc", "bco"):
                        big[nm] = sb(ph, f"{nm}B{par}", [128, 8])
                    for d in range(2):
                        t = {}
                        for nm in ("kT", "qT", "qdT", "wTn", "osb", "egr"):
                            t[nm] = sb(ph, f"{nm}{d}{par}", [128, 8, 64])
                        for nm in big:
                            t[nm] = big[nm][d * 64:(d + 1) * 64]
                        TL[d][par] = t
                for d in range(2):
                    S_ = sb(ph, f"S{d}", [128, 8, 128])
                    TL[d][0]["S"] = S_
                    TL[d][1]["S"] = S_
                    t = TL[d][0]
                    op("pool", lambda t=t: PO.memset(t["egr"][:, :, :], 0.0), w=[("egr", d, 0)])
                    for h in range(8):
                        op("dve", lambda t=t, h=h: V.tensor_copy(out=t["S"][:, h, :].bitcast(F32R), in_=t["egr"][:, 0:2, :].rearrange("p a b -> p (a b)")),
                           r=[("egr", d, 0)], w=[("S", d, h)])
                w3 = lambda ps_, po: ps_[po:po + 64, :].rearrange("p (h j) -> p h j", h=8)
                fl = lambda tl: tl[:, :, :].rearrange("p h j -> p (h j)")

                def dn_A(d, t0, is_lat, par):
                    t = TL[d][par]
                    po = d * 64
                    i64, offd = I64[d], OFFD[d]
                    rr = lambda ap: ap.bitcast(F32R)
                    rq = rr if d == 0 else (lambda ap: ap)
                    K_ = lambda nm: (nm, d, par)
                    T1a, T1b, T2a, T2b = PS[d * 4], PS[d * 4 + 1], PS[d * 4 + 2], PS[d * 4 + 3]
                    k1a, k1b, k2a, k2b = pk[d * 4], pk[d * 4 + 1], pk[d * 4 + 2], pk[d * 4 + 3]
                    last = 63 if d == 0 else 0
                    gd = t["g16"][:, d * 8:(d + 1) * 8]
                    bd = t["b16"][:, d * 8:(d + 1) * 8]
                    dma("sp", t["kT"][:, :, :], KT[:, :, t0:t0 + 64].rearrange("h p t -> p h t"), w=[K_("kT")])
                    dma("sp", t["qT"][:, :, :], QT[:, :, t0:t0 + 64].rearrange("h p t -> p h t"), w=[K_("qT")])
                    dma("sp", t["ktok"][:, :, :], KTOK[:, t0:t0 + 64, :].rearrange("h t f -> t h f"), w=[K_("ktok")])
                    dma("sp", t["vtok"][:, :, :], VTOK[:, t0:t0 + 64, :].rearrange("h t f -> t h f"), w=[K_("vtok")])
                    dma("sp", t["g16"][:, :], GD[t0:t0 + 64, :], w=[K_("g16")])
                    dma("sp", t["b16"][:, :], BD[t0:t0 + 64, :], w=[K_("b16")])
                    yield
                    op("dve", lambda: V.tensor_tensor(out=t["gM"][:, :, :], in0=bcm(MI[d]), in1=bc(gd, 64), op=ALU.mult),
                       r=["msk", K_("g16")], w=[K_("gM")])
                    op("pe", lambda: PE.matmul(T1a[:, :], lhsT=ones[po:po + 64, :], rhs=fl(t["gM"]), start=True, stop=True),
                       r=["ones", K_("gM")], w=[k1a])
                    op("pe", lambda: PE.matmul(T2b[po:po + 64, 0:8], lhsT=MI[d], rhs=gd, start=True, stop=True),
                       r=["msk", K_("g16")], w=[k2b])
                    op("dve", lambda: V.tensor_tensor(out=t["bdg"][:, :, :], in0=bcm(i64), in1=bc(bd, 64), op=ALU.mult),
                       r=["ident", K_("b16")], w=[K_("bdg")])
                    op("pe", lambda: PE.matmul(T2a[po:po + 64, :], lhsT=ones[po:po + 64, 0:64], rhs=fl(t["bdg"]), start=True, stop=True),
                       r=["ones", K_("bdg")], w=[k2a])
                    op("act", lambda: A.copy(out=t["gcc"][:, :], in_=T2b[po:po + 64, 0:8]), r=[k2b], w=[K_("gcc")])
                    op("dve", lambda: V.tensor_tensor(out=t["E1"][:, :, :], in0=w3(T1a, po), in1=bc(t["gcc"][:, :], 64), op=ALU.subtract),
                       r=[k1a, K_("gcc")], w=[K_("E1")])
                    op("dve", lambda: V.tensor_tensor(out=t["E2"][:, :, :], in0=bc(t["gcc"][:, :], 64), in1=w3(T1a, po), op=ALU.subtract),
                       r=[k1a, K_("gcc")], w=[K_("E2")])
                    op("dve", lambda: V.scalar_tensor_tensor(out=t["E1"][:, :, :], in0=t["E1"][:, :, :], scalar=0.0, in1=bcm(NT_[d]),
                                                             op0=ALU.min, op1=ALU.add), r=[K_("E1"), "msk"], w=[K_("E1")])
                    op("dve", lambda: V.scalar_tensor_tensor(out=t["E2"][:, :, :], in0=t["E2"][:, :, :], scalar=0.0, in1=bcm(NC_[d]),
                                                             op0=ALU.min, op1=ALU.add), r=[K_("E2"), "msk"], w=[K_("E2")])
                    op("act", lambda: A.activation(out=t["DTi"][:, :, :], in_=t["E1"][:, :, :], func=AF.Exp), r=[K_("E1")], w=[K_("DTi")])
                    op("act", lambda: A.activation(out=t["DCs"][:, :, :], in_=t["E2"][:, :, :], func=AF.Exp), r=[K_("E2")], w=[K_("DCs")])
                    op("act", lambda: A.activation(out=fl(t["egr"]), in_=T1a[:, :], func=AF.Exp), r=[k1a], w=[K_("egr")])
                    op("act", lambda: A.activation(out=t["egc"][:, :], in_=t["gcc"][:, :], func=AF.Exp), r=[K_("gcc")], w=[K_("egc")])
                    yield
                    for h in range(8):
                        op("pe", lambda h=h: PE.matmul(T2b[po:po + 64, h * 64:(h + 1) * 64], lhsT=t["kT"][:, h, :], rhs=t["kT"][:, h, :],
                                                       start=True, stop=True), r=[K_("kT")], w=[k2b])
                    for h in range(8):
                        op("pe", lambda h=h: PE.matmul(T1a[po:po + 64, h * 64:(h + 1) * 64], lhsT=t["kT"][:, h, :], rhs=t["qT"][:, h, :],
                                                       start=True, stop=True), r=[K_("kT"), K_("qT")], w=[k1a])
                    yield
                    op("pool", lambda: PO.tensor_tensor(out=t["DTs"][:, :, :], in0=t["DTi"][:, :, :], in1=bcm(offd), op=ALU.mult),
                       r=[K_("DTi"), "msk"], w=[K_("DTs")])
                    op("dve", lambda: V.tensor_tensor(out=t["tmpw"][:, :, :], in0=w3(T2b, po), in1=t["DTs"][:, :, :], op=ALU.mult),
                       r=[k2b, K_("DTs")], w=[K_("tmpw")])
                    op("dve", lambda: V.tensor_tensor(out=t["Q0"][:, :, :], in0=w3(T2a, po), in1=t["tmpw"][:, :, :], op=ALU.mult),
                       r=[k2a, K_("tmpw")], w=[K_("Q0")])
                    op("dve", lambda: V.tensor_tensor(out=t["tmpw"][:, :, :], in0=w3(T2b, po), in1=t["DCs"][:, :, :], op=ALU.mult),
                       r=[k2b, K_("DCs"), K_("tmpw")], w=[K_("tmpw")])
                    op("dve", lambda: V.tensor_tensor(out=t["P0"][:, :, :], in0=t["tmpw"][:, :, :], in1=bc(bd, 64), op=ALU.mult),
                       r=[K_("tmpw"), K_("b16")], w=[K_("P0")])
                    op("dve", lambda: V.tensor_tensor(out=rr(t["QKm"][:, :, :]), in0=w3(T1a, po), in1=t["DTi"][:, :, :], op=ALU.mult),
                       r=[k1a, K_("DTi")], w=[K_("QKm")])
                    op("dve", lambda: V.tensor_tensor(out=t["R"][:, :, :], in0=bcm(i64), in1=t["Q0"][:, :, :], op=ALU.subtract),
                       r=["ident", K_("Q0")], w=[K_("R")])
                    yield
                    Qc, Pc, Qn, Pn = "Q0", "P0", "Q1", "P1"
                    for lev in range(1, 6):
                        for h in range(8):
                            op("pe", lambda h=h: PE.matmul(T2a[po:po + 64, h * 64:(h + 1) * 64], lhsT=t[Qc][:, h, :], rhs=t[Pc][:, h, :],
                                                           start=True, stop=True), r=[K_(Qc), K_(Pc)], w=[k2a])
                        if lev < 5:
                            for h in range(8):
                                op("pe", lambda h=h: PE.matmul(T1a[po:po + 64, h * 64:(h + 1) * 64], lhsT=t[Pc][:, h, :], rhs=t[Qc][:, h, :],
                                                               start=True, stop=True), r=[K_(Qc), K_(Pc)], w=[k1a])
                        yield
                        op("act", lambda: A.copy(out=t[Pn][:, :, :], in_=w3(T2a, po)), r=[k2a], w=[K_(Pn)])
                        if lev < 5:
                            op("dve", lambda: V.tensor_copy(out=t[Qn][:, :, :], in_=w3(T1a, po)), r=[k1a], w=[K_(Qn)])
                        for h in range(8):
                            op("pe", lambda h=h: PE.matmul(T2b[po:po + 64, h * 64:(h + 1) * 64], lhsT=t[Pn][:, h, :], rhs=t["R"][:, h, :],
                                                           start=True, stop=True), r=[K_(Pn), K_("R")], w=[k2b])
                        op("dve", lambda: V.tensor_tensor(out=t["R"][:, :, :], in0=w3(T2b, po), in1=t["R"][:, :, :], op=ALU.add),
                           r=[k2b, K_("R")], w=[K_("R")])
                        Qc, Pc, Qn, Pn = Qn, Pn, Qc, Pc
                        yield

                def dn_B(d, t0, is_lat, par):
                    t = TL[d][par]
                    po = d * 64
                    i64, offd = I64[d], OFFD[d]
                    rr = lambda ap: ap.bitcast(F32R)
                    rq = rr if d == 0 else (lambda ap: ap)
                    K_ = lambda nm: (nm, d, par)
                    T1a, T1b, T2a, T2b = PS[d * 4], PS[d * 4 + 1], PS[d * 4 + 2], PS[d * 4 + 3]
                    k1a, k1b, k2a, k2b = pk[d * 4], pk[d * 4 + 1], pk[d * 4 + 2], pk[d * 4 + 3]
                    last = 63 if d == 0 else 0
                    gd = t["g16"][:, d * 8:(d + 1) * 8]
                    bd = t["b16"][:, d * 8:(d + 1) * 8]
                    op("act", lambda: A.copy(out=rr(t["Rr"][:, :, :]), in_=t["R"][:, :, :]), r=[K_("R")], w=[K_("Rr")])
                    op("dve", lambda: V.tensor_tensor(out=t["bco"][:, :], in0=bd, in1=t["egc"][:, :], op=ALU.mult),
                       r=[K_("b16"), K_("egc")], w=[K_("bco")])
                    op("dve", lambda: V.tensor_tensor(out=rr(t["kbg"][:, :, :]), in0=t["ktok"][:, :, :], in1=bc(t["bco"][:, :], 128), op=ALU.mult),
                       r=[K_("ktok"), K_("bco")], w=[K_("kbg")])
                    op("dve", lambda: V.tensor_tensor(out=rr(t["vb"][:, :, :]), in0=t["vtok"][:, :, :], in1=bc(bd, 128), op=ALU.mult),
                       r=[K_("vtok"), K_("b16")], w=[K_("vb")])
                    op("dve", lambda: V.tensor_tensor(out=rr(t["kd"][:, :, :]), in0=t["ktok"][:, :, :],
                                                        in1=bc(t["DTi"][:, :, last], 128), op=ALU.mult),
                       r=[K_("ktok"), K_("DTi")], w=[K_("kd")])
                    op("dve", lambda: V.tensor_tensor(out=rr(t["qdT"][:, :, :]), in0=t["qT"][:, :, :], in1=t["egr"][:, :, :], op=ALU.mult),
                       r=[K_("qT"), K_("egr")], w=[K_("qdT")])
                    yield
                    for h in range(8):
                        op("pe", lambda h=h: PE.matmul(T1b[:, h * 64:(h + 1) * 64], lhsT=rr(t["kbg"][:, h, :]), rhs=rr(t["Rr"][:, h, :]),
                                                       start=True, stop=True), r=[K_("kbg"), K_("Rr")], w=[k1b])
                    yield
                    op("act", lambda: A.activation(out=rr(fl(t["wTn"])), in_=T1b[:, :], func=AF.Copy, scale=-1.0), r=[k1b], w=[K_("wTn")])
                    for half in range(2):
                        for h in range(half * 4, half * 4 + 4):
                            o_ = T1b[po:po + 64, (h % 4) * 128:(h % 4 + 1) * 128]
                            op("pe", lambda h=h, o_=o_: PE.matmul(o_, lhsT=rq(t["Rr"][:, h, :]), rhs=rq(t["vb"][:, h, :]), start=True, stop=False),
                               r=[K_("Rr"), K_("vb")], w=[k1b])
                            op("pe", lambda h=h, o_=o_: PE.matmul(o_, lhsT=rq(t["wTn"][:, h, :]), rhs=rq(t["S"][:, h, :]), start=False, stop=True),
                               r=[K_("wTn"), ("S", d, h)], w=[k1b])
                        src_ = T1b[po:po + 64, :].rearrange("p (h f) -> p h f", h=4)
                        if half == 0:
                            op("act", lambda: A.copy(out=rr(t["vn"][:, 0:4, :]), in_=src_), r=[k1b], w=[K_("vn")])
                        else:
                            op("dve", lambda: V.tensor_copy(out=rr(t["vn"][:, 4:8, :]), in_=src_), r=[k1b], w=[K_("vn")])
                        yield
                    if is_lat:
                        for h in range(8):
                            o_ = T1b[:, h * 64:(h + 1) * 64]
                            op("pe", lambda h=h, o_=o_: PE.matmul(o_, lhsT=rr(t["S"][:, h, :]), rhs=rr(t["qdT"][:, h, :]), start=True, stop=False),
                               r=[("S", d, h), K_("qdT")], w=[k1b])
                            op("pe", lambda h=h, o_=o_: PE.matmul(o_, lhsT=rr(t["vn"][:, h, :]), rhs=rr(t["QKm"][:, h, :]), start=False, stop=True),
                               r=[K_("vn"), K_("QKm")], w=[k1b])
                        op("act", lambda: A.copy(out=fl(t["osb"]), in_=T1b[:, :]), r=[k1b], w=[K_("osb")])
                        s0 = t0 - CTX
                        dma("sp", OFB[d][:, :, s0:s0 + 64].rearrange("h p t -> p h t"), t["osb"][:, :, :], r=[K_("osb")])
                        yield
                    for half in range(2):
                        for h in range(half * 4, half * 4 + 4):
                            o_ = T1b[:, (h % 4) * 128:(h % 4 + 1) * 128]
                            op("pe", lambda h=h, o_=o_: PE.matmul(o_, lhsT=rr(t["kd"][:, h, :]), rhs=rr(t["vn"][:, h, :]), start=True, stop=True),
                               r=[K_("kd"), K_("vn")], w=[k1b])
                        for h in range(half * 4, half * 4 + 4):
                            o_ = T1b[:, (h % 4) * 128:(h % 4 + 1) * 128]
                            op("dve", lambda h=h, o_=o_: V.scalar_tensor_tensor(out=rr(t["S"][:, h, :]), in0=t["S"][:, h, :],
                                                                                scalar=t["egr"][:, h, last:last + 1], in1=o_,
                                                                                op0=ALU.mult, op1=ALU.add),
                               r=[("S", d, h), K_("egr"), k1b], w=[("S", d, h)])
                        yield

                fw = [(64 * j, False) for j in range(4)] + [(CTX + 64 * j, True) for j in range(64)]
                bw = [(64 * j, False) for j in range(3, -1, -1)] + [(CTX + 64 * j, True) for j in range(63, -1, -1)]
                for step in range(69):
                    gens = []
                    if step >= 1:
                        gens += [dn_B(0, *fw[step - 1], (step - 1) % 2), dn_B(1, *bw[step - 1], (step - 1) % 2)]
                    if step < 68:
                        gens += [dn_A(0, *fw[step], step % 2), dn_A(1, *bw[step], step % 2)]
                    while gens:
                        for g in list(gens):
                            try:
                                next(g)
                            except StopIteration:
                                gens.remove(g)
                k.barrier()
            with contextlib.ExitStack() as ph:
                wo = sb(ph, "dwo", [128, 8, D], BF16)
                gb0 = sb(ph, "gb0", [128, D])
                dg = sb(ph, "dg", [128, 128])
                xts = [sb(ph, f"xt{i}", [128, 2, D]) for i in range(2)]
                ofs = [sb(ph, f"of{i}", [128, 8, 256]) for i in range(2)]
                obs = [sb(ph, f"ob{i}", [128, 8, 256]) for i in range(2)]
                zss = [sb(ph, f"zs{i}", [128, 8, 256]) for i in range(2)]
                sq = sb(ph, "rsq", [128, 2048])
                rr = sb(ph, "rr", [128, 2048])
                yb = sb(ph, "yb", [128, 8, 256], BF16)
                tmp = [sb(ph, f"tmp{i}", [128, 512]) for i in range(2)]
                for c in range(8):
                    dma("pool", wo[:, c, :], dn_w_out[c * 128:(c + 1) * 128, :], w=[("dwo", c)])
                bcast_row(gb0, "gb0", gate_ap(l, s, 0), 1.0, dg)
                gn = cols[:, C_GN:C_GN + 1]
                n = 0
                for ti in range(16):
                    xt, xk = xts[ti % 2], ("xt", ti % 2)
                    of, ofk = ofs[ti % 2], ("of", ti % 2)
                    ob, obk = obs[ti % 2], ("ob", ti % 2)
                    zs, zk = zss[ti % 2], ("zs", ti % 2)
                    s0 = ti * 256
                    for b in range(2):
                        for cl in range(2):
                            dma("sp", xt[cl * 64:(cl + 1) * 64, b, :], Xlat_cm[4 * ti + 2 * b + cl], w=[xk])
                    dma("sp", of[:, :, :], OFB[0][:, :, s0:s0 + 256].rearrange("h p t -> p h t"), w=[ofk])
                    dma("sp", ob[:, :, :], OFB[1][:, :, s0:s0 + 256].rearrange("h p t -> p h t"), w=[obk])
                    dma("sp", zs[:, :, :], ZD[:, :, CTX + s0:CTX + s0 + 256].rearrange("c p t -> p c t"), w=[zk])
                    off = of[:, :, :].rearrange("p h t -> p (h t)")
                    op("dve", lambda: V.tensor_tensor(out=off, in0=off, in1=ob[:, :, :].rearrange("p h t -> p (h t)"), op=ALU.add),
                       r=[ofk, obk], w=[ofk])
                    op("act", lambda: A.activation(out=sq[:, :], in_=off, func=AF.Square), r=[ofk], w=["rsq"])
                    for q in range(4):
                        pb = 2 + q
                        op("pe", lambda q=q, pb=pb: PE.matmul(PS[pb][:, :], lhsT=ones[:, :], rhs=sq[:, q * 512:(q + 1) * 512],
                                                              start=True, stop=True), r=["rsq", "ones"], w=[pk[pb]])
                        op("act", lambda q=q, pb=pb: A.activation(out=rr[:, q * 512:(q + 1) * 512], in_=PS[pb][:, :], func=AF.Ln,
                                                                  bias=EPS, scale=1.0 / 128), r=[pk[pb]], w=["rr"])
                    op("act", lambda: A.activation(out=rr[:, :], in_=rr[:, :], func=AF.Exp, scale=-0.5), r=["rr"], w=["rr"])
                    op("dve", lambda: V.tensor_tensor(out=off, in0=off, in1=rr[:, :], op=ALU.mult), r=[ofk, "rr"], w=[ofk])
                    op("dve", lambda: V.scalar_tensor_tensor(out=yb[:, :, :].rearrange("p h t -> p (h t)"), in0=off, scalar=gn,
                                                             in1=zs[:, :, :].rearrange("p h t -> p (h t)"), op0=ALU.mult, op1=ALU.mult),
                       r=[ofk, zk, "cols"], w=["yb"])
                    for tb in range(2):
                        for dh in range(2):
                            po = n % 2
                            tm = tmp[n % 2]
                            n += 1
                            for c in range(8):
                                op("pe", lambda c=c, po=po, tb=tb, dh=dh: PE.matmul(
                                    PS[po][:, :], lhsT=yb[:, c, tb * 128:(tb + 1) * 128], rhs=wo[:, c, dh * 512:(dh + 1) * 512],
                                    start=(c == 0), stop=(c == 7)), r=["yb", ("dwo", c)], w=[pk[po]])
                            op("dve", lambda po=po, tm=tm, dh=dh: V.tensor_tensor(
                                out=tm[:], in0=PS[po][:, :], in1=gb0[:, dh * 512:(dh + 1) * 512], op=ALU.mult),
                               r=[pk[po], "gb0"], w=[("tmp", po)])
                            op("pool", lambda tm=tm, tb=tb, dh=dh, xt=xt: PO.tensor_tensor(
                                out=xt[:, tb, dh * 512:(dh + 1) * 512], in0=tm[:], in1=xt[:, tb, dh * 512:(dh + 1) * 512], op=ALU.add),
                               r=[("tmp", po), xk], w=[xk])
                    for b in range(2):
                        for cl in range(2):
                            dma("sp", Xlat_cm[4 * ti + 2 * b + cl], xt[cl * 64:(cl + 1) * 64, b, :], r=[xk])
                k.barrier()

        def final_phase(raw):
            with contextlib.ExitStack() as ph:
                xts = [sb(ph, f"fx{i}", [128, 2, D]) for i in range(2)]
                junk = sb(ph, "fj", [128, D])
                sts = [sb(ph, f"fs{i}", [128, 4]) for i in range(2)]
                for ti in range(SEQ // 256):
                    xt = xts[ti % 2]
                    xk = ("fx", ti % 2)
                    dma("sp", xt[:, :, :], X[CTX + ti * 256:CTX + (ti + 1) * 256, :].rearrange("(b p) d -> p b d", p=128), w=[xk])
                    if not raw:
                        for b in range(2):
                            st = sts[b]
                            ks = ("fs", b)
                            op("act", lambda b=b, st=st: A.activation(out=junk[:], in_=xt[:, b, :], func=AF.Square, accum_out=st[:, 0:1]),
                               r=[xk], w=["fj", ks])
                            op("act", lambda st=st: A.activation(out=st[:, 1:2], in_=st[:, 0:1], func=AF.Ln, bias=EPS, scale=1.0 / D),
                               r=[ks], w=[ks])
                            op("act", lambda st=st: A.activation(out=st[:, 2:3], in_=st[:, 1:2], func=AF.Exp, scale=-0.5), r=[ks], w=[ks])
                            op("dve", lambda b=b, st=st: V.scalar_tensor_tensor(out=xt[:, b, :], in0=xt[:, b, :], scalar=st[:, 2:3],
                                                                               in1=rowsb[:, 32:1056], op0=ALU.mult, op1=ALU.mult),
                               r=[xk, ks, "rowsb"], w=[xk])
                    dma("sp", out[ti * 256:(ti + 1) * 256, :].rearrange("(b p) d -> p b d", p=128), xt[:, :, :], r=[xk])
                k.barrier()

        stages = []
        stages.append(lambda: ffn_phase(0, 0, 0, True, True))
        stages.append(lru_phase)
        stages.append(lambda: ffn_phase(0, 2, 1, False, True))
        stages.append(lambda: ffn_phase(1, 0, 2, False, True))
        stages.append(dn_phase)
        stages.append(lambda: ffn_phase(1, 2, 3, False, False))
        from_lru = len(stages)
        nst = 0
        for f in stages:
            if nst >= stage:
                break
            f()
            nst += 1
        final_phase(raw=dbg)
    return nc


def host_inputs(inputs, b):
    f = lambda a: np.ascontiguousarray(np.asarray(a, dtype=np.float32))
    cols = np.zeros((NCOLS, 128), np.float32)
    cols[C_C:C_C + 8] = f(inputs["c"])[b].reshape(8, 128)
    cols[C_CC:C_CC + 8] = f(inputs["c_ctx"]).reshape(8, 128)
    cols[C_BADA:C_BADA + 144] = f(inputs["b_ada"]).reshape(144, 128)
    cols[C_GSUB:C_GSUB + 48] = f(inputs["g_sub"]).reshape(48, 128)
    cols[C_LCW:C_LCW + 32] = f(inputs["lru_conv_w"])[0].reshape(32, 128)
    cols[C_LCB:C_LCB + 8] = f(inputs["lru_conv_b"])[0].reshape(8, 128)
    cols[C_LBG:C_LBG + 32] = f(inputs["lru_b_gate"])[0].reshape(32, 128)
    cols[C_LAM:C_LAM + 16] = f(inputs["lru_lambda"])[0].reshape(16, 128)
    cols[C_DCW:C_DCW + 96] = f(inputs["dn_conv_w"])[0].reshape(96, 128)
    cols[C_GN:C_GN + 1] = f(inputs["dn_g_norm"])[0].reshape(1, 128)
    rows = np.concatenate([f(inputs["dn_a_log"])[0].reshape(16), f(inputs["dn_dt_bias"])[0].reshape(16),
                           f(inputs["g_final"]).reshape(1024)]).reshape(1, 1056)
    return {
        "x": f(inputs["x"])[b], "ctx": f(inputs["ctx"])[b], "cols_src": cols, "rows_src": rows,
        "w_ada": f(inputs["w_ada"]), "ffn_w_in": f(inputs["ffn_w_in"]).reshape(4, D, 2 * DFF),
        "ffn_w_out": f(inputs["ffn_w_out"]).reshape(4, DFF, D), "lru_w_in": f(inputs["lru_w_in"])[0],
        "lru_w_gate": f(inputs["lru_w_gate"])[0].reshape(16, 256, 256), "lru_w_out": f(inputs["lru_w_out"])[0],
        "dn_w_in": f(inputs["dn_w_in"])[0], "dn_w_out": f(inputs["dn_w_out"])[0],
    }


def kernel(**inputs):
    nc = build()
    in_maps = [host_inputs(inputs, c % 4) for c in range(8)]
    res = run_bass_kernel_spmd(nc, in_maps, core_ids=list(range(8)))
    return np.stack([np.asarray(res.results[b]["out"], dtype=np.float32) for b in range(4)], axis=0)
```

```python
import contextlib
import math
import numpy as np
import concourse.bass as bass
import concourse.mybir as mybir
from concourse.bass_utils import run_bass_kernel_spmd

F32 = mybir.dt.float32
BF16 = mybir.dt.bfloat16
F32R = mybir.dt.float32r
AF = mybir.ActivationFunctionType
ALU = mybir.AluOpType

D = 1024
SEQ = 4096
CTX = 256
T = SEQ + CTX
DFF = 2816
NF = DFF // 128
EPS = 1e-6
CTX0 = 2
LAT0 = CTX0 + CTX + 3
TP = LAT0 + SEQ + 1
NEG = -30000.0

C_C, C_CC, C_BADA, C_GSUB, C_LCW, C_LCB, C_LBG, C_LAM, C_DCW, C_GN = 0, 8, 16, 160, 208, 240, 248, 280, 296, 392
NCOLS = 512


class K:
    def __init__(self, nc, es):
        self.nc = nc
        self.es = es
        self.eng = {"pe": nc.tensor, "dve": nc.vector, "act": nc.scalar, "pool": nc.gpsimd, "sp": nc.sync}
        self.ce = ["pe", "dve", "act", "pool"]
        self.EPOCH = 30000
        self.cnt = {e: 0 for e in self.ce}
        self.ep = {e: 0 for e in self.ce}
        self.sems = {e: [es.enter_context(nc.semaphore(f"s_{e}_0"))] for e in self.ce}
        self.KD = 8
        self.dq = ["sp", "pool"]
        self.dsem = {q: [es.enter_context(nc.semaphore(f"d_{q}_{i}")) for i in range(self.KD)] for q in self.dq}
        self.dn = {q: 0 for q in self.dq}
        self.seen = {e: {} for e in self.eng}
        self.st = {}

    def _wait(self, e, tok):
        if tok is None:
            return
        if tok[0] == "c":
            _, f, ep, idx = tok
            if f == e and e == "pe":
                return
            key = (f, ep)
            if self.seen[e].get(key, 0) >= idx:
                return
            self.eng[e].wait_ge(self.sems[f][ep], idx)
            self.seen[e][key] = idx
        else:
            _, q, si, cnt = tok
            key = ("d", q, si)
            if self.seen[e].get(key, 0) >= cnt:
                return
            self.eng[e].wait_ge(self.dsem[q][si], cnt)
            self.seen[e][key] = cnt

    def _deps(self, e, r, w):
        for k in r:
            s = self.st.get(k)
            if s:
                self._wait(e, s[0])
        for k in w:
            s = self.st.get(k)
            if s:
                self._wait(e, s[0])
                for t in s[1].values():
                    self._wait(e, t)

    def _record(self, tok, r, w, rid):
        for k in w:
            self.st[k] = [tok, {}]
        for k in r:
            s = self.st.setdefault(k, [None, {}])
            s[1][rid] = tok

    def op(self, e, fn, r=(), w=()):
        self._deps(e, r, w)
        inst = fn()
        if self.cnt[e] >= self.EPOCH:
            self.ep[e] += 1
            self.cnt[e] = 0
            self.sems[e].append(self.es.enter_context(self.nc.semaphore(f"s_{e}_{self.ep[e]}")))
        self.cnt[e] += 1
        inst.then_inc(self.sems[e][self.ep[e]], 1)
        tok = ("c", e, self.ep[e], self.cnt[e])
        self._record(tok, r, w, e)
        return tok

    def dma(self, q, out, in_, r=(), w=()):
        n = self.dn[q]
        si = n % self.KD
        cnt = 16 * (n // self.KD + 1)
        if cnt > 16:
            self._wait(q, ("d", q, si, cnt - 16))
        self._deps(q, r, w)
        self.eng[q].dma_start(out=out, in_=in_).then_inc(self.dsem[q][si], 16)
        self.dn[q] = n + 1
        tok = ("d", q, si, cnt)
        self._record(tok, r, w, ("d", q, si))
        return tok

    def barrier(self):
        toks = []
        for f in self.ce:
            if self.cnt[f] > 0:
                toks.append(("c", f, self.ep[f], self.cnt[f]))
        for q in self.dq:
            n = self.dn[q]
            for si in range(self.KD):
                uses = (n - si + self.KD - 1) // self.KD if n > si else 0
                if uses > 0:
                    toks.append(("d", q, si, 16 * uses))
        for e in self.eng:
            for t in toks:
                if t[0] == "c" and t[1] == e:
                    continue
                self._wait(e, t)
        self.st = {}


def build(stage=99, dbg=False):
    nc = bass.Bass("TRN2", target_bir_lowering=False)
    dt = lambda name, shape, dtype=F32, kind="ExternalInput": nc.dram_tensor(name, shape, dtype, kind=kind).ap()
    x_in = dt("x", [SEQ, D])
    ctx_in = dt("ctx", [CTX, D])
    cols_src = dt("cols_src", [NCOLS, 128])
    rows_src = dt("rows_src", [1, 1056])
    w_ada = dt("w_ada", [2, D, 9 * D])
    ffn_w_in = dt("ffn_w_in", [4, D, 2 * DFF])
    ffn_w_out = dt("ffn_w_out", [4, DFF, D])
    lru_w_in = dt("lru_w_in", [D, 2048])
    lru_w_gate = dt("lru_w_gate", [16, 256, 256])
    lru_w_out = dt("lru_w_out", [D, D])
    dn_w_in = dt("dn_w_in", [D, 4128])
    dn_w_out = dt("dn_w_out", [D, D])
    out = dt("out", [SEQ, D], kind="ExternalOutput")
    X = dt("Xs", [T, D], kind="Internal")
    PT = dt("PTs", [16, 128, TP], kind="Internal")
    ZT = dt("ZTs", [8, 128, T], BF16, kind="Internal")
    QP = dt("QPs", [24, 128, TP], BF16, kind="Internal")
    ZD = dt("ZDs", [8, 128, T], kind="Internal")
    GD = dt("GDs", [T, 16], kind="Internal")
    BD = dt("BDs", [T, 16], kind="Internal")
    QT = dt("QTs", [8, 128, T], kind="Internal")
    KT = dt("KTs", [8, 128, T], kind="Internal")
    KTOK = dt("KTOKs", [8, T, 128], kind="Internal")
    VTOK = dt("VTOKs", [8, T, 128], kind="Internal")
    OFB = [dt("OFs", [8, 128, SEQ], kind="Internal"), dt("OBs", [8, 128, SEQ], kind="Internal")]
    Xlat_cm = X[CTX:T, :].rearrange("(r c) d -> c r d", c=64)

    with contextlib.ExitStack() as es:
        k = K(nc, es)
        op, dma = k.op, k.dma
        V, A, PE, PO = nc.vector, nc.scalar, nc.tensor, nc.gpsimd

        uid = [0]

        def sb(st, name, shape, dtype=F32):
            uid[0] += 1
            return st.enter_context(nc.sbuf_tensor(f"{name}_u{uid[0]}", shape, dtype))

        PS = [es.enter_context(nc.psum_tensor(f"ps{i}", [128, 512], F32)) for i in range(8)]
        pk = [("ps", i) for i in range(8)]

        ident = sb(es, "ident", [128, 128])
        ones = sb(es, "ones", [128, 128])
        iot = sb(es, "iot", [128, 128])
        cols = sb(es, "cols", [128, NCOLS])
        mcol = sb(es, "mcol", [128, 2, 72, 2])
        gsall = sb(es, "gsall", [128, 2, 3, 2, 8])
        rowsb = sb(es, "rowsb", [128, 1056])
        op("pool", lambda: PO.iota(iot[:], pattern=[[1, 128]], base=0, channel_multiplier=-1,
                                   allow_small_or_imprecise_dtypes=True), w=["iot"])
        op("dve", lambda: V.tensor_scalar(out=ident[:], in0=iot[:], scalar1=0.0, scalar2=None, op0=ALU.is_equal),
           r=["iot"], w=["ident"])
        op("dve", lambda: V.memset(ones[:], 1.0), w=["ones"])

        def load_ffn_w(w1, w2, fi):
            for fb in range(NF // 2):
                for which in range(2):
                    c0 = which * DFF + fb * 256
                    dma("pool", w1[:, :, c0:c0 + 256], ffn_w_in[fi][:, c0:c0 + 256].rearrange("(k p) c -> p k c", p=128),
                        w=[("w1", which, fb)])
            for f in range(NF):
                dma("pool", w2[:, f, :], ffn_w_out[fi][f * 128:(f + 1) * 128, :], w=[("w2", f)])

        pre_stack = contextlib.ExitStack()
        pre_w = (sb(pre_stack, "w1p", [128, 8, 2 * DFF], BF16), sb(pre_stack, "w2p", [128, NF, D], BF16))
        if stage >= 1:
            load_ffn_w(pre_w[0], pre_w[1], 0)

        with contextlib.ExitStack() as ph:
            stg = sb(ph, "stg", [128, 4, 128])
            rows1 = sb(ph, "rows1", [1, 1056])
            sc = sb(ph, "sc", [128, 8, 2])
            was = [sb(ph, f"wa{i}", [128, 8, 512]) for i in range(2)]
            dma("sp", stg[:, :, :], cols_src.rearrange("(g p) f -> p g f", p=128), w=["stg"])
            dma("sp", rows1[:, :], rows_src[:, :], w=["rows1"])
            for g in range(4):
                op("pe", lambda g=g: PE.transpose(out=PS[0][:, g * 128:(g + 1) * 128], in_=stg[:, g, :], identity=ident[:]),
                   r=["stg", "ident"], w=[pk[0]])
            op("dve", lambda: V.tensor_copy(out=cols[:], in_=PS[0][:, :]), r=[pk[0]], w=["cols"])
            for i, (a, b) in enumerate([(0, 512), (512, 1024), (1024, 1056)]):
                op("pe", lambda a=a, b=b, i=i: PE.matmul(PS[1 + i][:, 0:b - a], lhsT=ones[0:1, :], rhs=rows1[0:1, a:b],
                                                         start=True, stop=True), r=["ones", "rows1"], w=[pk[1 + i]])
                op("act", lambda a=a, b=b, i=i: A.copy(out=rowsb[:, a:b], in_=PS[1 + i][:, 0:b - a]), r=[pk[1 + i]], w=["rowsb"])
            op("act", lambda: A.activation(out=sc[:, :, 0], in_=cols[:, C_C:C_C + 8], func=AF.Silu), r=["cols"], w=["sc"])
            op("act", lambda: A.activation(out=sc[:, :, 1], in_=cols[:, C_CC:C_CC + 8], func=AF.Silu), r=["cols"], w=["sc"])
            it = 0
            for l in range(2):
                for cg in range(18):
                    wa = was[it % 2]
                    wk = ("wa", it % 2)
                    pz = 4 + (it % 2)
                    dma("sp", wa[:, :, :], w_ada[l][:, cg * 512:(cg + 1) * 512].rearrange("(k p) c -> p k c", p=128), w=[wk])
                    for cc in range(4):
                        for kk in range(8):
                            op("pe", lambda cc=cc, kk=kk, wa=wa, pz=pz: PE.matmul(
                                PS[pz][:, cc * 2:cc * 2 + 2], lhsT=wa[:, kk, cc * 128:(cc + 1) * 128], rhs=sc[:, kk, :],
                                start=(kk == 0), stop=(kk == 7)), r=[wk, "sc"], w=[pk[pz]])
                    op("dve", lambda l=l, cg=cg, pz=pz: V.tensor_tensor(
                        out=mcol[:, l, cg * 4:(cg + 1) * 4, :], in0=PS[pz][:, 0:8].rearrange("p (c v) -> p c v", v=2),
                        in1=cols[:, C_BADA + l * 72 + cg * 4:C_BADA + l * 72 + cg * 4 + 4].unsqueeze(2).broadcast_to([128, 4, 2]),
                        op=ALU.add), r=[pk[pz], "cols"], w=["mcol"])
                    it += 1
            for l in range(2):
                for s in range(3):
                    for v in range(2):
                        op("dve", lambda l=l, s=s, v=v: V.scalar_tensor_tensor(
                            out=gsall[:, l, s, v, :], in0=mcol[:, l, (s * 3 + 1) * 8:(s * 3 + 2) * 8, v], scalar=1.0,
                            in1=cols[:, C_GSUB + (l * 3 + s) * 8:C_GSUB + (l * 3 + s) * 8 + 8], op0=ALU.add, op1=ALU.mult),
                           r=["mcol", "cols"], w=["gsall"])
            k.barrier()

        def sh_ap(l, s, v):
            return mcol[:, l, (s * 3) * 8:(s * 3) * 8 + 8, v]

        def gate_ap(l, s, v):
            return mcol[:, l, (s * 3 + 2) * 8:(s * 3 + 2) * 8 + 8, v]

        def gs_ap(l, s, v):
            return gsall[:, l, s, v, :]

        def bcast_row(dst, dkey, col_ap, factor, dg):
            for j in range(8):
                op("dve", lambda j=j: V.tensor_scalar(out=dg[:, :], in0=ident[:], scalar1=col_ap[:, j:j + 1], scalar2=float(factor),
                                                      op0=ALU.mult, op1=ALU.mult), r=["ident", "mcol"], w=["dg"])
                op("pe", lambda j=j: PE.matmul(PS[7][:, 0:128], lhsT=ones[:, :], rhs=dg[:, :], start=True, stop=True),
                   r=["dg", "ones"], w=[pk[7]])
                op("act", lambda j=j: A.copy(out=dst[:, j * 128:(j + 1) * 128], in_=PS[7][:, 0:128]), r=[pk[7]], w=[dkey])

        def make_ln(ph):
            S = {"junk": [sb(ph, f"lnj{i}", [128, 1024]) for i in range(2)],
                 "xs": [sb(ph, f"lnx{i}", [128, 1024]) for i in range(2)],
                 "st": [sb(ph, f"lns{i}", [128, 4]) for i in range(2)], "n": 0}

            def ln_A(xap, xkey):
                i = S["n"] % 2
                S["n"] += 1
                junk, xs, st = S["junk"][i], S["xs"][i], S["st"][i]
                kj, kx, ks = ("lnj", i), ("lnx", i), ("lns", i)
                op("act", lambda: A.activation(out=junk[:], in_=xap, func=AF.Square, accum_out=st[:, 0:1]), r=[xkey], w=[kj, ks])
                op("act", lambda: A.activation(out=st[:, 1:2], in_=st[:, 0:1], func=AF.Ln, bias=EPS, scale=1.0 / D), r=[ks], w=[ks])
                op("act", lambda: A.activation(out=st[:, 2:3], in_=st[:, 1:2], func=AF.Exp, scale=-0.5), r=[ks], w=[ks])
                op("dve", lambda: V.tensor_scalar(out=xs[:], in0=xap, scalar1=st[:, 2:3], scalar2=None, op0=ALU.mult),
                   r=[xkey, ks], w=[kx])
                return i

            def ln_B(i, gs, sh, hdst, hkey, pbanks):
                xs = S["xs"][i]
                kx = ("lnx", i)
                for half in range(2):
                    pb = pbanks[half]
                    for j4 in range(4):
                        j = half * 4 + j4
                        op("pe", lambda j=j, j4=j4, pb=pb: PE.transpose(out=PS[pb][:, j4 * 128:(j4 + 1) * 128],
                                                                        in_=xs[:, j * 128:(j + 1) * 128], identity=ident[:]),
                           r=[kx, "ident"], w=[pk[pb]])
                    for j4 in range(4):
                        j = half * 4 + j4
                        op("act", lambda j=j, j4=j4, pb=pb: A.activation(out=hdst(j), in_=PS[pb][:, j4 * 128:(j4 + 1) * 128],
                                                                         func=AF.Identity, scale=gs[:, j:j + 1], bias=sh[:, j:j + 1]),
                           r=[pk[pb], "mcol", "gsall"], w=[hkey])

            def ln_T(xap, xkey, gs, sh, hdst, hkey, pbanks):
                ln_B(ln_A(xap, xkey), gs, sh, hdst, hkey, pbanks)
            ln_T.A = ln_A
            ln_T.B = ln_B
            return ln_T

        def tiles_for(first, with_ctx):
            tl = []
            if with_ctx:
                tl.append((1, ctx_in[:, :] if first else X[0:CTX, :], X[0:CTX, :]))
            for i in range(SEQ // 256):
                src = x_in[i * 256:(i + 1) * 256, :] if first else X[CTX + i * 256:CTX + (i + 1) * 256, :]
                tl.append((0, src, X[CTX + i * 256:CTX + (i + 1) * 256, :]))
            return tl

        def ffn_phase(l, s, fi, first, with_ctx, pre=None):
            with contextlib.ExitStack() as ph:
                if pre is None:
                    w1 = sb(ph, "w1", [128, 8, 2 * DFF], BF16)
                    w2 = sb(ph, "w2", [128, NF, D], BF16)
                else:
                    w1, w2 = pre
                xts = [sb(ph, f"xt{i}", [128, 2, D]) for i in range(2)]
                hT = sb(ph, "hT", [128, 8, 256], BF16)
                actT = sb(ph, "actT", [128, NF, 256], BF16)
                sg = [sb(ph, f"sg{i}", [128, 256]) for i in range(2)]
                tmp = [sb(ph, f"tmp{i}", [128, 512]) for i in range(2)]
                gb = [sb(ph, f"gb{i}", [128, D]) for i in range(2)]
                dg = sb(ph, "dg", [128, 128])
                ln_T = make_ln(ph)
                if pre is None:
                    load_ffn_w(w1, w2, fi)
                for v in range(2):
                    if v == 1 and not with_ctx:
                        continue
                    bcast_row(gb[v], ("gb", v), gate_ap(l, s, v), 0.5, dg)
                n = 0
                tl = tiles_for(first, with_ctx)

                def load(ti):
                    dma("sp", xts[ti % 2][:, :, :], tl[ti][1].rearrange("(b p) d -> p b d", p=128), w=[("xt", ti % 2)])

                def lnA(ti):
                    return [ln_T.A(xts[ti % 2][:, b, :], ("xt", ti % 2)) for b in range(2)]

                load(0)
                hA = lnA(0)
                for ti, (v, src, dst) in enumerate(tl):
                    xt = xts[ti % 2]
                    xk = ("xt", ti % 2)
                    if ti + 1 < len(tl):
                        load(ti + 1)
                    for b in range(2):
                        ln_T.B(hA[b], gs_ap(l, s, v), sh_ap(l, s, v), lambda j, b=b: hT[:, j, b * 128:(b + 1) * 128], "hT", (0, 1))
                    for f in range(NF):
                        pg, pu = 2 + 2 * (f % 2), 3 + 2 * (f % 2)
                        for which, pb in ((0, pg), (1, pu)):
                            c0 = which * DFF + f * 128
                            for kk in range(8):
                                op("pe", lambda kk=kk, pb=pb, c0=c0: PE.matmul(PS[pb][:, 0:256], lhsT=w1[:, kk, c0:c0 + 128],
                                                                               rhs=hT[:, kk, :], start=(kk == 0), stop=(kk == 7)),
                                   r=[("w1", which, f // 2), "hT"], w=[pk[pb]])
                        sgi = sg[f % 2]
                        op("act", lambda pg=pg, sgi=sgi: A.activation(out=sgi[:], in_=PS[pg][:, 0:256], func=AF.Silu),
                           r=[pk[pg]], w=[("sg", f % 2)])
                        op("dve", lambda pu=pu, sgi=sgi, f=f: V.tensor_tensor(out=actT[:, f, :], in0=PS[pu][:, 0:256], in1=sgi[:],
                                                                              op=ALU.mult), r=[pk[pu], ("sg", f % 2)], w=[("actT", f)])
                    if ti + 1 < len(tl):
                        hA = lnA(ti + 1)
                    for tb in range(2):
                        for dh in range(2):
                            po = n % 2
                            tm = tmp[n % 2]
                            n += 1
                            for f in range(NF):
                                op("pe", lambda f=f, po=po, tb=tb, dh=dh: PE.matmul(
                                    PS[po][:, :], lhsT=actT[:, f, tb * 128:(tb + 1) * 128], rhs=w2[:, f, dh * 512:(dh + 1) * 512],
                                    start=(f == 0), stop=(f == NF - 1)), r=[("actT", f), ("w2", f)], w=[pk[po]])
                            op("dve", lambda po=po, tm=tm, dh=dh, v=v: V.tensor_tensor(
                                out=tm[:], in0=PS[po][:, :], in1=gb[v][:, dh * 512:(dh + 1) * 512], op=ALU.mult),
                               r=[pk[po], ("gb", v)], w=[("tmp", po)])
                            op("pool", lambda tm=tm, tb=tb, dh=dh, xt=xt: PO.tensor_tensor(
                                out=xt[:, tb, dh * 512:(dh + 1) * 512], in0=tm[:], in1=xt[:, tb, dh * 512:(dh + 1) * 512], op=ALU.add),
                               r=[("tmp", po), xk], w=[xk])
                    dma("sp", dst.rearrange("(b p) d -> p b d", p=128), xt[:, :, :], r=[xk])
                k.barrier()

        def lru_phase():
            l, s = 0, 1
            cw = lambda j, ch: cols[:, C_LCW + j * 8 + ch:C_LCW + j * 8 + ch + 1]
            cb = lambda ch: cols[:, C_LCB + ch:C_LCB + ch + 1]
            bg = lambda d, g, ch: cols[:, C_LBG + (d * 2 + g) * 8 + ch:C_LBG + (d * 2 + g) * 8 + ch + 1]
            tl = tiles_for(False, True)
            toff = lambda ti: 0 if ti == 0 else CTX + (ti - 1) * 256
            poff = lambda ti: CTX0 if ti == 0 else LAT0 + (ti - 1) * 256
            with contextlib.ExitStack() as ph:
                wi = sb(ph, "lwi", [128, 8, 2048], BF16)
                xts = [sb(ph, f"xt{i}", [128, 2, D]) for i in range(2)]
                hT = sb(ph, "hT", [128, 8, 256], BF16)
                pts = [sb(ph, f"pt{i}", [128, 16, 256]) for i in range(2)]
                ln_T = make_ln(ph)
                for kk in range(8):
                    dma("pool", wi[:, kk, :], lru_w_in[kk * 128:(kk + 1) * 128, :], w=[("lwi", kk)])
                for ti, (v, src, dst) in enumerate(tl):
                    xt = xts[ti % 2]
                    xk = ("xt", ti % 2)
                    pt = pts[ti % 2]
                    ptk = ("pt", ti % 2)
                    dma("sp", xt[:, :, :], src.rearrange("(b p) d -> p b d", p=128), w=[xk])
                    for b in range(2):
                        ln_T(xt[:, b, :], xk, gs_ap(l, s, v), sh_ap(l, s, v),
                             lambda j, b=b: hT[:, j, b * 128:(b + 1) * 128], "hT", (0, 1))
                    for ch in range(16):
                        pb = 2 + ch % 4
                        for kk in range(8):
                            op("pe", lambda kk=kk, pb=pb, ch=ch: PE.matmul(PS[pb][:, 0:256], lhsT=wi[:, kk, ch * 128:(ch + 1) * 128],
                                                                           rhs=hT[:, kk, :], start=(kk == 0), stop=(kk == 7)),
                               r=[("lwi", kk), "hT"], w=[pk[pb]])
                        if ch % 2 == 0:
                            op("act", lambda pb=pb, ch=ch, pt=pt: A.copy(out=pt[:, ch, :], in_=PS[pb][:, 0:256]), r=[pk[pb]], w=[ptk])
                        else:
                            op("dve", lambda pb=pb, ch=ch, pt=pt: V.tensor_copy(out=pt[:, ch, :], in_=PS[pb][:, 0:256]), r=[pk[pb]], w=[ptk])
                    dma("sp", PT[:, :, poff(ti):poff(ti) + 256].rearrange("c p t -> p c t"), pt[:, :, :], r=[ptk])
                k.barrier()
            with contextlib.ExitStack() as ph:
                wg = sb(ph, "wg", [128, 32, 256], BF16)
                coef = sb(ph, "coef", [128, 16])
                coef2 = sb(ph, "coef2", [128, 16])
                ee = sb(ph, "ee", [128, 16])
                pp = sb(ph, "pp", [128, 16])
                upre = sb(ph, "upre", [128, 2, TP])
                u = sb(ph, "u", [128, 2, T])
                ub = sb(ph, "ub", [128, 2, T], BF16)
                hsum = sb(ph, "hsum", [128, 2, T])
                tr = [[sb(ph, f"lt{n}_{i}", [128, 256]) for i in range(5)] for n in range(4)]
                sts = [[sb(ph, f"lst{d}{oc}", [128, 1]) for oc in range(2)] for d in range(2)]
                for q in range(16):
                    dma("pool", wg[:, q * 2:(q + 1) * 2, :], lru_w_gate[q].rearrange("(kc p) j -> p kc j", p=128), w=[("wg", q)])
                op("act", lambda: A.activation(out=ee[:], in_=cols[:, C_LAM:C_LAM + 16], func=AF.Exp, scale=-1.0), r=["cols"], w=["ee"])
                op("dve", lambda: V.memset(pp[:], 1.0 / 12), w=["pp"])
                for n in range(11, 0, -1):
                    op("dve", lambda: V.tensor_tensor(out=pp[:], in0=pp[:], in1=ee[:], op=ALU.mult), r=["pp", "ee"], w=["pp"])
                    op("dve", lambda n=n: V.tensor_scalar(out=pp[:], in0=pp[:], scalar1=-1.0, scalar2=1.0 / n, op0=ALU.mult, op1=ALU.add),
                       r=["pp"], w=["pp"])
                op("dve", lambda: V.scalar_tensor_tensor(out=coef[:], in0=pp[:], scalar=-8.0, in1=ee[:], op0=ALU.mult, op1=ALU.mult),
                   r=["pp", "ee"], w=["coef"])
                op("dve", lambda: V.tensor_scalar(out=coef2[:], in0=coef[:], scalar1=2.0, scalar2=None, op0=ALU.mult), r=["coef"], w=["coef2"])
                for hd in range(4):
                    dma("sp", upre[:, :, :], PT[8 + 2 * hd:8 + 2 * hd + 2, :, :].rearrange("c p t -> p c t"), w=["upre"])
                    for (a, b) in ((0, 2), (CTX0 + CTX, LAT0), (TP - 1, TP)):
                        op("pool", lambda a=a, b=b: PO.memset(upre[:, :, a:b], 0.0), r=["upre"], w=["upre"])
                    for oc in range(2):
                        ch = 2 * hd + oc
                        for (o0, n, base) in ((0, CTX, CTX0), (CTX, SEQ, LAT0)):
                            op("dve", lambda oc=oc, ch=ch, o0=o0, n=n, base=base: V.tensor_scalar(
                                out=u[:, oc, o0:o0 + n], in0=upre[:, oc, base - 2:base - 2 + n], scalar1=cw(0, ch), scalar2=cb(ch),
                                op0=ALU.mult, op1=ALU.add), r=["upre", "cols"], w=[("u", oc)])
                            for j in range(1, 4):
                                op("dve", lambda oc=oc, ch=ch, o0=o0, n=n, base=base, j=j: V.scalar_tensor_tensor(
                                    out=u[:, oc, o0:o0 + n], in0=upre[:, oc, base - 2 + j:base - 2 + j + n], scalar=cw(j, ch),
                                    in1=u[:, oc, o0:o0 + n], op0=ALU.mult, op1=ALU.add), r=["upre", "cols", ("u", oc)], w=[("u", oc)])
                        op("act", lambda oc=oc: A.copy(out=ub[:, oc, :], in_=u[:, oc, :]), r=[("u", oc)], w=["ub"])
                    for oc in range(2):
                        op("pool", lambda oc=oc: PO.memset(hsum[:, oc, :], 0.0), w=[("hsum", oc)])
                    for d in range(2):
                        for oc in range(2):
                            op("dve", lambda d=d, oc=oc: V.memset(sts[d][oc][:], 0.0), w=[("lst", d, oc)])
                    order = [list(range(17)), [0] + list(range(16, 0, -1))]
                    for step in range(17):
                        chains = [(d, oc) for d in range(2) for oc in range(2)]
                        for ci, (d, oc) in enumerate(chains):
                            ti = order[d][step]
                            t0 = toff(ti)
                            ch = 2 * hd + oc
                            R_, I_ = tr[ci][0], tr[ci][1]
                            pr, pi = 2 * ci, 2 * ci + 1
                            for g, pb in ((0, pr), (1, pi)):
                                for kc in range(2):
                                    op("pe", lambda g=g, pb=pb, kc=kc, d=d, oc=oc, t0=t0: PE.matmul(
                                        PS[pb][:, 0:256], lhsT=wg[:, ((d * 2 + g) * 4 + hd) * 2 + kc, oc * 128:(oc + 1) * 128],
                                        rhs=ub[:, kc, t0:t0 + 256], start=(kc == 0), stop=(kc == 1)),
                                       r=[("wg", (d * 2 + g) * 4 + hd), "ub"], w=[pk[pb]])
                            op("act", lambda: A.activation(out=R_[:], in_=PS[pr][:, 0:256], func=AF.Sigmoid, bias=bg(d, 0, ch)),
                               r=[pk[pr], "cols"], w=[("ltR", ci)])
                            op("act", lambda: A.activation(out=I_[:], in_=PS[pi][:, 0:256], func=AF.Sigmoid, bias=bg(d, 1, ch)),
                               r=[pk[pi], "cols"], w=[("ltI", ci)])
                        for ci, (d, oc) in enumerate(chains):
                            ch = 2 * hd + oc
                            cidx = d * 8 + ch
                            R_, A_, SQ_ = tr[ci][0], tr[ci][2], tr[ci][3]
                            op("act", lambda: A.activation(out=A_[:], in_=R_[:], func=AF.Exp, scale=coef[:, cidx:cidx + 1]),
                               r=[("ltR", ci), "coef"], w=[("ltA", ci)])
                            op("pool", lambda: PO.tensor_tensor(out=SQ_[:], in0=A_[:], in1=A_[:], op=ALU.mult),
                               r=[("ltA", ci)], w=[("ltS", ci)])
                            op("act", lambda: A.activation(out=SQ_[:], in_=SQ_[:], func=AF.Ln, scale=-1.0, bias=1.0),
                               r=[("ltS", ci)], w=[("ltS", ci)])
                            op("act", lambda: A.activation(out=SQ_[:], in_=SQ_[:], func=AF.Exp, scale=0.5),
                               r=[("ltS", ci)], w=[("ltS", ci)])
                        for ci, (d, oc) in enumerate(chains):
                            ti = order[d][step]
                            t0 = toff(ti)
                            I_, A_, SQ_, HB_ = tr[ci][1], tr[ci][2], tr[ci][3], tr[ci][4]
                            stt = sts[d][oc]
                            sk = ("lst", d, oc)
                            hk = ("hsum", oc)
                            op("dve", lambda: V.tensor_tensor(out=I_[:], in0=I_[:], in1=u[:, oc, t0:t0 + 256], op=ALU.mult),
                               r=[("ltI", ci), ("u", oc)], w=[("ltI", ci)])
                            op("dve", lambda: V.tensor_tensor(out=I_[:], in0=I_[:], in1=SQ_[:], op=ALU.mult),
                               r=[("ltI", ci), ("ltS", ci)], w=[("ltI", ci)])
                            if d == 0:
                                op("dve", lambda: V.tensor_tensor_scan(out=HB_[:], data0=A_[:], data1=I_[:],
                                                                       initial=stt[:, 0:1], op0=ALU.mult, op1=ALU.add),
                                   r=[("ltA", ci), ("ltI", ci), sk], w=[("ltH", ci)])
                                op("dve", lambda: V.tensor_copy(out=stt[:, 0:1], in_=HB_[:, 255:256]), r=[("ltH", ci)], w=[sk])
                            else:
                                op("dve", lambda: V.tensor_tensor_scan(out=HB_[:, ::-1], data0=A_[:, ::-1], data1=I_[:, ::-1],
                                                                       initial=stt[:, 0:1], op0=ALU.mult, op1=ALU.add),
                                   r=[("ltA", ci), ("ltI", ci), sk], w=[("ltH", ci)])
                                op("dve", lambda: V.tensor_copy(out=stt[:, 0:1], in_=HB_[:, 0:1]), r=[("ltH", ci)], w=[sk])
                            op("pool", lambda: PO.tensor_tensor(out=hsum[:, oc, t0:t0 + 256], in0=hsum[:, oc, t0:t0 + 256],
                                                                in1=HB_[:], op=ALU.add), r=[("ltH", ci), hk], w=[hk])
                    dma("sp", upre[:, :, 0:CTX], PT[2 * hd:2 * hd + 2, :, CTX0:CTX0 + CTX].rearrange("c p t -> p c t"), w=["upre"])
                    dma("sp", upre[:, :, CTX:T], PT[2 * hd:2 * hd + 2, :, LAT0:LAT0 + SEQ].rearrange("c p t -> p c t"), w=["upre"])
                    for oc in range(2):
                        op("act", lambda oc=oc: A.activation(out=upre[:, oc, 0:T], in_=upre[:, oc, 0:T], func=AF.Gelu_apprx_tanh),
                           r=["upre"], w=["upre"])
                        op("dve", lambda oc=oc: V.tensor_tensor(out=ub[:, oc, :], in0=upre[:, oc, 0:T], in1=hsum[:, oc, :], op=ALU.mult),
                           r=["upre", ("hsum", oc)], w=["ub"])
                    dma("sp", ZT[2 * hd:2 * hd + 2, :, :].rearrange("c p t -> p c t"), ub[:, :, :], r=["ub"])
                k.barrier()
            with contextlib.ExitStack() as ph:
                wo = sb(ph, "lwo", [128, 8, D], BF16)
                gb = [sb(ph, f"gb{i}", [128, D]) for i in range(2)]
                dg = sb(ph, "dg", [128, 128])
                xts = [sb(ph, f"xt{i}", [128, 2, D]) for i in range(2)]
                zts = [sb(ph, f"zt{i}", [128, 8, 256], BF16) for i in range(2)]
                tmp = [sb(ph, f"tmp{i}", [128, 512]) for i in range(2)]
                for c in range(8):
                    dma("pool", wo[:, c, :], lru_w_out[c * 128:(c + 1) * 128, :], w=[("lwo", c)])
                for v in range(2):
                    bcast_row(gb[v], ("gb", v), gate_ap(l, s, v), 1.0, dg)
                n = 0
                for ti, (v, src, dst) in enumerate(tl):
                    xt, xk = xts[ti % 2], ("xt", ti % 2)
                    zt, zk = zts[ti % 2], ("zt", ti % 2)
                    dma("sp", xt[:, :, :], src.rearrange("(b p) d -> p b d", p=128), w=[xk])
                    dma("sp", zt[:, :, :], ZT[:, :, toff(ti):toff(ti) + 256].rearrange("c p t -> p c t"), w=[zk])
                    for tb in range(2):
                        for dh in range(2):
                            po = n % 2
                            tm = tmp[n % 2]
                            n += 1
                            for c in range(8):
                                op("pe", lambda c=c, po=po, tb=tb, dh=dh, zt=zt: PE.matmul(
                                    PS[po][:, :], lhsT=zt[:, c, tb * 128:(tb + 1) * 128], rhs=wo[:, c, dh * 512:(dh + 1) * 512],
                                    start=(c == 0), stop=(c == 7)), r=[zk, ("lwo", c)], w=[pk[po]])
                            op("dve", lambda po=po, tm=tm, dh=dh, v=v: V.tensor_tensor(
                                out=tm[:], in0=PS[po][:, :], in1=gb[v][:, dh * 512:(dh + 1) * 512], op=ALU.mult),
                               r=[pk[po], ("gb", v)], w=[("tmp", po)])
                            op("pool", lambda tm=tm, tb=tb, dh=dh, xt=xt: PO.tensor_tensor(
                                out=xt[:, tb, dh * 512:(dh + 1) * 512], in0=tm[:], in1=xt[:, tb, dh * 512:(dh + 1) * 512], op=ALU.add),
                               r=[("tmp", po), xk], w=[xk])
                    dma("sp", dst.rearrange("(b p) d -> p b d", p=128), xt[:, :, :], r=[xk])
                k.barrier()

        def dn_phase():
            l, s = 1, 1
            toffs = [0] + [CTX + i * 256 for i in range(16)]
            poffs = [CTX0] + [LAT0 + i * 256 for i in range(16)]
            with contextlib.ExitStack() as ph:
                wi = sb(ph, "dwi", [128, 8, 4128], BF16)
                xts = [sb(ph, f"xt{i}", [128, 2, D]) for i in range(2)]
                hT = sb(ph, "hT", [128, 8, 256], BF16)
                pqs = [sb(ph, f"pq{i}", [128, 24, 256], BF16) for i in range(2)]
                pzs = [sb(ph, f"pz{i}", [128, 8, 256]) for i in range(2)]
                gbt = [sb(ph, f"gbt{i}", [128, 2, 32]) for i in range(2)]
                t1 = [sb(ph, f"t1{i}", [128, 16]) for i in range(2)]
                negA = sb(ph, "negA", [128, 16])
                ln_T = make_ln(ph)
                for kk in range(8):
                    dma("pool", wi[:, kk, :], dn_w_in[kk * 128:(kk + 1) * 128, :], w=[("dwi", kk)])
                op("act", lambda: A.activation(out=negA[:], in_=rowsb[:, 0:16], func=AF.Exp), r=["rowsb"], w=["negA"])
                op("dve", lambda: V.tensor_scalar(out=negA[:], in0=negA[:], scalar1=-1.0, scalar2=None, op0=ALU.mult), r=["negA"], w=["negA"])
                nb = 0

                def d1_load(ti):
                    xt, xk = xts[ti % 2], ("xt", ti % 2)
                    if ti == 0:
                        dma("sp", xt[:, :, :], X[0:CTX, :].rearrange("(b p) d -> p b d", p=128), w=[xk])
                    else:
                        for b in range(2):
                            for cl in range(2):
                                col = 4 * (ti - 1) + 2 * b + cl
                                dma("sp", xt[cl * 64:(cl + 1) * 64, b, :], Xlat_cm[col], w=[xk])

                def d1_lnA(ti):
                    return [ln_T.A(xts[ti % 2][:, b, :], ("xt", ti % 2)) for b in range(2)]

                d1_load(0)
                hA = d1_lnA(0)
                for ti in range(17):
                    v = 1 if ti == 0 else 0
                    xt, xk = xts[ti % 2], ("xt", ti % 2)
                    pz, pzk = pzs[ti % 2], ("pz", ti % 2)
                    pq, pqk = pqs[ti % 2], ("pq", ti % 2)
                    if ti + 1 < 17:
                        d1_load(ti + 1)
                    for b in range(2):
                        ln_T.B(hA[b], gs_ap(l, s, v), sh_ap(l, s, v), lambda j, b=b: hT[:, j, b * 128:(b + 1) * 128], "hT", (0, 1))
                    for ch in range(32):
                        pb = 2 + ch % 4
                        for kk in range(8):
                            op("pe", lambda kk=kk, pb=pb, ch=ch: PE.matmul(PS[pb][:, 0:256], lhsT=wi[:, kk, ch * 128:(ch + 1) * 128],
                                                                           rhs=hT[:, kk, :], start=(kk == 0), stop=(kk == 7)),
                               r=[("dwi", kk), "hT"], w=[pk[pb]])
                        if ch >= 24:
                            op("act", lambda pb=pb, ch=ch, pz=pz: A.activation(out=pz[:, ch - 24, :], in_=PS[pb][:, 0:256], func=AF.Silu),
                               r=[pk[pb]], w=[pzk])
                        elif ch % 2 == 0:
                            op("act", lambda pb=pb, ch=ch, pq=pq: A.copy(out=pq[:, ch, :], in_=PS[pb][:, 0:256]), r=[pk[pb]], w=[pqk])
                        else:
                            op("dve", lambda pb=pb, ch=ch, pq=pq: V.tensor_copy(out=pq[:, ch, :], in_=PS[pb][:, 0:256]), r=[pk[pb]], w=[pqk])
                    for b in range(2):
                        pb = 6 + b
                        g_, t_ = gbt[nb % 2], t1[nb % 2]
                        gk, tk = ("gbt", nb % 2), ("t1", nb % 2)
                        nb += 1
                        for kk in range(8):
                            op("pe", lambda kk=kk, pb=pb, b=b: PE.matmul(PS[pb][:, 0:32], lhsT=hT[:, kk, b * 128:(b + 1) * 128],
                                                                         rhs=wi[:, kk, 4096:4128], start=(kk == 0), stop=(kk == 7)),
                               r=[("dwi", kk), "hT"], w=[pk[pb]])
                        op("dve", lambda: V.tensor_tensor(out=t_[:], in0=PS[pb][:, 0:16], in1=rowsb[:, 16:32], op=ALU.add),
                           r=[pk[pb], "rowsb"], w=[tk])
                        op("act", lambda: A.activation(out=t_[:], in_=t_[:], func=AF.Exp), r=[tk], w=[tk])
                        op("act", lambda: A.activation(out=t_[:], in_=t_[:], func=AF.Ln, bias=1.0), r=[tk], w=[tk])
                        op("dve", lambda: V.tensor_tensor(out=g_[:, 0, 0:16], in0=t_[:], in1=negA[:], op=ALU.mult), r=[tk, "negA"], w=[gk])
                        op("act", lambda: A.activation(out=g_[:, 0, 16:32], in_=PS[pb][:, 16:32], func=AF.Sigmoid), r=[pk[pb]], w=[gk])
                        r0 = toffs[ti] + b * 128
                        dma("sp", GD[r0:r0 + 128, :], g_[:, 0, 0:16], r=[gk])
                        dma("sp", BD[r0:r0 + 128, :], g_[:, 0, 16:32], r=[gk])
                    if ti + 1 < 17:
                        hA = d1_lnA(ti + 1)
                    dma("sp", QP[:, :, poffs[ti]:poffs[ti] + 256].rearrange("c p t -> p c t"), pq[:, :, :], r=[pqk])
                    dma("sp", ZD[:, :, toffs[ti]:toffs[ti] + 256].rearrange("c p t -> p c t"), pz[:, :, :], r=[pzk])
                k.barrier()
            with contextlib.ExitStack() as ph:
                pres = [sb(ph, f"pre{i}", [128, TP], BF16) for i in range(2)]
                dws = [sb(ph, f"dw{i}", [128, 4, 128], BF16) for i in range(2)]
                vals = [sb(ph, f"val{i}", [128, T]) for i in range(2)]
                toks = [sb(ph, f"tok{i}", [128, 34, 128]) for i in range(2)]
                sqs = [sb(ph, f"sq{i}", [128, 512]) for i in range(2)]
                rins = [sb(ph, f"rin{i}", [128, 512]) for i in range(2)]
                nt = 0
                for c in range(24):
                    pre, prk = pres[c % 2], ("pre", c % 2)
                    val, vk = vals[c % 2], ("val", c % 2)
                    dma("sp", pre[:, :], QP[c], w=[prk])
                    for (a, b) in ((0, 2), (CTX0 + CTX, LAT0), (TP - 1, TP)):
                        op("pool", lambda a=a, b=b, pre=pre: PO.memset(pre[:, a:b], 0.0), r=[prk], w=[prk])
                    cw = lambda j: cols[:, C_DCW + j * 24 + c:C_DCW + j * 24 + c + 1]
                    dw, dwk = dws[c % 2], ("dw", c % 2)
                    for j in range(4):
                        op("dve", lambda j=j: V.tensor_scalar(out=dw[:, j, :], in0=ident[:], scalar1=cw(j), scalar2=None, op0=ALU.mult),
                           r=["ident", "cols"], w=[dwk])
                    for (o0, n, base) in ((0, CTX, CTX0), (CTX, SEQ, LAT0)):
                        for a0 in range(0, n, 512):
                            nn = min(512, n - a0)
                            pb = 4 + (nt % 4)
                            nt += 1
                            for j in range(4):
                                op("pe", lambda j=j, pb=pb, a0=a0, nn=nn, base=base: PE.matmul(
                                    PS[pb][:, 0:nn], lhsT=dw[:, j, :], rhs=pre[:, base - 2 + j + a0:base - 2 + j + a0 + nn],
                                    start=(j == 0), stop=(j == 3)), r=[dwk, prk], w=[pk[pb]])
                            op("act", lambda pb=pb, a0=a0, nn=nn, o0=o0: A.activation(out=val[:, o0 + a0:o0 + a0 + nn], in_=PS[pb][:, 0:nn],
                                                                                    func=AF.Silu), r=[pk[pb]], w=[vk])
                    if c < 16:
                        for a in range(0, T, 512):
                            n = min(512, T - a)
                            sq, sk = sqs[nt % 2], ("sq", nt % 2)
                            rin, rk = rins[nt % 2], ("rin", nt % 2)
                            pb = nt % 2
                            nt += 1
                            op("act", lambda: A.activation(out=sq[:, 0:n], in_=val[:, a:a + n], func=AF.Square), r=[vk], w=[sk])
                            op("pe", lambda: PE.matmul(PS[pb][:, 0:n], lhsT=ones[:, :], rhs=sq[:, 0:n], start=True, stop=True),
                               r=[sk, "ones"], w=[pk[pb]])
                            op("act", lambda: A.activation(out=rin[:, 0:n], in_=PS[pb][:, 0:n], func=AF.Ln, bias=EPS), r=[pk[pb]], w=[rk])
                            op("act", lambda: A.activation(out=rin[:, 0:n], in_=rin[:, 0:n], func=AF.Exp, scale=-0.5,
                                                           bias=(math.log(128.0 ** -0.5) if c < 8 else 0.0)), r=[rk], w=[rk])
                            op("dve", lambda: V.tensor_tensor(out=val[:, a:a + n], in0=val[:, a:a + n], in1=rin[:, 0:n], op=ALU.mult),
                               r=[vk, rk], w=[vk])
                    if c < 8:
                        dma("sp", QT[c], val[:, :], r=[vk])
                    elif c < 16:
                        dma("sp", KT[c - 8], val[:, :], r=[vk])
                    if c >= 8:
                        tok, tkk = toks[c % 2], ("tok", c % 2)
                        for b0 in range(0, 34, 4):
                            nbk = min(4, 34 - b0)
                            pb = 2 + (b0 // 4) % 2
                            for bb in range(nbk):
                                blk = b0 + bb
                                op("pe", lambda bb=bb, blk=blk, pb=pb: PE.transpose(out=PS[pb][:, bb * 128:(bb + 1) * 128],
                                                                                     in_=val[:, blk * 128:(blk + 1) * 128], identity=ident[:]),
                                   r=[vk, "ident"], w=[pk[pb]])
                            src_ap = PS[pb][:, 0:nbk * 128].rearrange("p (b f) -> p b f", f=128)
                            if (b0 // 4) % 2 == 0:
                                op("act", lambda: A.copy(out=tok[:, b0:b0 + nbk, :], in_=src_ap), r=[pk[pb]], w=[tkk])
                            else:
                                op("dve", lambda: V.tensor_copy(out=tok[:, b0:b0 + nbk, :], in_=src_ap), r=[pk[pb]], w=[tkk])
                        dst = (KTOK[c - 8] if c < 16 else VTOK[c - 16]).rearrange("(b p) f -> p b f", p=128)
                        dma("sp", dst, tok[:, :, :], r=[tkk])
                k.barrier()
            with contextlib.ExitStack() as ph:
                W3 = [64, 8, 64]
                offd_t = sb(ph, "offd", [128, 64])
                mi_t = sb(ph, "mi", [128, 64]); ntm_t = sb(ph, "ntm", [128, 64]); ncm_t = sb(ph, "ncm", [128, 64])
                MI, NT_, NC_, OFFD, I64 = [], [], [], [], []
                for d in range(2):
                    po = d * 64
                    v64 = iot[po:po + 64, po:po + 64]
                    mi, ntm, ncm, offd = mi_t[po:po + 64, :], ntm_t[po:po + 64, :], ncm_t[po:po + 64, :], offd_t[po:po + 64, :]
                    op("dve", lambda: V.tensor_scalar(out=offd, in0=v64, scalar1=0.0, scalar2=None, op0=ALU.not_equal), r=["iot"], w=["msk"])
                    op("dve", lambda: V.tensor_scalar(out=mi, in0=v64, scalar1=0.0, scalar2=None,
                                                      op0=(ALU.is_ge if d == 0 else ALU.is_le)), r=["iot"], w=["msk"])
                    op("dve", lambda: V.tensor_scalar(out=ntm, in0=mi, scalar1=-1.0, scalar2=-NEG, op0=ALU.add, op1=ALU.mult),
                       r=["msk"], w=["msk"])
                    op("dve", lambda: V.tensor_scalar(out=ncm, in0=v64, scalar1=0.0, scalar2=None,
                                                      op0=(ALU.is_lt if d == 0 else ALU.is_gt)), r=["iot"], w=["msk"])
                    op("dve", lambda: V.tensor_scalar(out=ncm, in0=ncm, scalar1=-1.0, scalar2=-NEG, op0=ALU.add, op1=ALU.mult),
                       r=["msk"], w=["msk"])
                    MI.append(mi); NT_.append(ntm); NC_.append(ncm); OFFD.append(offd); I64.append(ident[po:po + 64, po:po + 64])
                bcm = lambda m: m.unsqueeze(1).broadcast_to(W3)
                bc = lambda ap2, n: ap2.unsqueeze(2).broadcast_to([ap2.shape[0], 8, n])
                TL = [[None, None], [None, None]]
                for par in range(2):
                    big = {}
                    for nm in ("ktok", "vtok", "kbg", "vb", "kd", "vn"):
                        big[nm] = sb(ph, f"{nm}B{par}", [128, 8, 128])
                    for nm in ("gM", "bdg", "E1", "E2", "DTi", "DCs", "DTs", "Q0", "P0", "Q1", "P1", "QKm", "R", "Rr", "tmpw"):
                        big[nm] = sb(ph, f"{nm}B{par}", [128, 8, 64])
                    for nm in ("g16", "b16"):
                        big[nm] = sb(ph, f"{nm}B{par}", [128, 16])
                    for nm in ("gcc", "egc", "bco"):
                        big[nm] = sb(ph, f"{nm}B{par}", [128, 8])
                    for d in range(2):
                        t = {}
                        for nm in ("kT", "qT", "qdT", "wTn", "osb", "egr"):
                            t[nm] = sb(ph, f"{nm}{d}{par}", [128, 8, 64])
                        for nm in big:
                            t[nm] = big[nm][d * 64:(d + 1) * 64]
                        TL[d][par] = t
                for d in range(2):
                    S_ = sb(ph, f"S{d}", [128, 8, 128])
                    TL[d][0]["S"] = S_
                    TL[d][1]["S"] = S_
                    t = TL[d][0]
                    op("pool", lambda t=t: PO.memset(t["egr"][:, :, :], 0.0), w=[("egr", d, 0)])
                    for h in range(8):
                        op("dve", lambda t=t, h=h: V.tensor_copy(out=t["S"][:, h, :].bitcast(F32R), in_=t["egr"][:, 0:2, :].rearrange("p a b -> p (a b)")),
                           r=[("egr", d, 0)], w=[("S", d, h)])
                w3 = lambda ps_, po: ps_[po:po + 64, :].rearrange("p (h j) -> p h j", h=8)
                fl = lambda tl: tl[:, :, :].rearrange("p h j -> p (h j)")

                def dn_A(d, t0, is_lat, par):
                    t = TL[d][par]
                    po = d * 64
                    i64, offd = I64[d], OFFD[d]
                    rr = lambda ap: ap.bitcast(F32R)
                    rq = rr if d == 0 else (lambda ap: ap)
                    K_ = lambda nm: (nm, d, par)
                    T1a, T1b, T2a, T2b = PS[d * 4], PS[d * 4 + 1], PS[d * 4 + 2], PS[d * 4 + 3]
                    k1a, k1b, k2a, k2b = pk[d * 4], pk[d * 4 + 1], pk[d * 4 + 2], pk[d * 4 + 3]
                    last = 63 if d == 0 else 0
                    gd = t["g16"][:, d * 8:(d + 1) * 8]
                    bd = t["b16"][:, d * 8:(d + 1) * 8]
                    dma("sp", t["kT"][:, :, :], KT[:, :, t0:t0 + 64].rearrange("h p t -> p h t"), w=[K_("kT")])
                    dma("sp", t["qT"][:, :, :], QT[:, :, t0:t0 + 64].rearrange("h p t -> p h t"), w=[K_("qT")])
                    dma("sp", t["ktok"][:, :, :], KTOK[:, t0:t0 + 64, :].rearrange("h t f -> t h f"), w=[K_("ktok")])
                    dma("sp", t["vtok"][:, :, :], VTOK[:, t0:t0 + 64, :].rearrange("h t f -> t h f"), w=[K_("vtok")])
                    dma("sp", t["g16"][:, :], GD[t0:t0 + 64, :], w=[K_("g16")])
                    dma("sp", t["b16"][:, :], BD[t0:t0 + 64, :], w=[K_("b16")])
                    yield
                    op("dve", lambda: V.tensor_tensor(out=t["gM"][:, :, :], in0=bcm(MI[d]), in1=bc(gd, 64), op=ALU.mult),
                       r=["msk", K_("g16")], w=[K_("gM")])
                    op("pe", lambda: PE.matmul(T1a[:, :], lhsT=ones[po:po + 64, :], rhs=fl(t["gM"]), start=True, stop=True),
                       r=["ones", K_("gM")], w=[k1a])
                    op("pe", lambda: PE.matmul(T2b[po:po + 64, 0:8], lhsT=MI[d], rhs=gd, start=True, stop=True),
                       r=["msk", K_("g16")], w=[k2b])
                    op("dve", lambda: V.tensor_tensor(out=t["bdg"][:, :, :], in0=bcm(i64), in1=bc(bd, 64), op=ALU.mult),
                       r=["ident", K_("b16")], w=[K_("bdg")])
                    op("pe", lambda: PE.matmul(T2a[po:po + 64, :], lhsT=ones[po:po + 64, 0:64], rhs=fl(t["bdg"]), start=True, stop=True),
                       r=["ones", K_("bdg")], w=[k2a])
                    op("act", lambda: A.copy(out=t["gcc"][:, :], in_=T2b[po:po + 64, 0:8]), r=[k2b], w=[K_("gcc")])
                    op("dve", lambda: V.tensor_tensor(out=t["E1"][:, :, :], in0=w3(T1a, po), in1=bc(t["gcc"][:, :], 64), op=ALU.subtract),
                       r=[k1a, K_("gcc")], w=[K_("E1")])
                    op("dve", lambda: V.tensor_tensor(out=t["E2"][:, :, :], in0=bc(t["gcc"][:, :], 64), in1=w3(T1a, po), op=ALU.subtract),
                       r=[k1a, K_("gcc")], w=[K_("E2")])
                    op("dve", lambda: V.scalar_tensor_tensor(out=t["E1"][:, :, :], in0=t["E1"][:, :, :], scalar=0.0, in1=bcm(NT_[d]),
                                                             op0=ALU.min, op1=ALU.add), r=[K_("E1"), "msk"], w=[K_("E1")])
                    op("dve", lambda: V.scalar_tensor_tensor(out=t["E2"][:, :, :], in0=t["E2"][:, :, :], scalar=0.0, in1=bcm(NC_[d]),
                                                             op0=ALU.min, op1=ALU.add), r=[K_("E2"), "msk"], w=[K_("E2")])
                    op("act", lambda: A.activation(out=t["DTi"][:, :, :], in_=t["E1"][:, :, :], func=AF.Exp), r=[K_("E1")], w=[K_("DTi")])
                    op("act", lambda: A.activation(out=t["DCs"][:, :, :], in_=t["E2"][:, :, :], func=AF.Exp), r=[K_("E2")], w=[K_("DCs")])
                    op("act", lambda: A.activation(out=fl(t["egr"]), in_=T1a[:, :], func=AF.Exp), r=[k1a], w=[K_("egr")])
                    op("act", lambda: A.activation(out=t["egc"][:, :], in_=t["gcc"][:, :], func=AF.Exp), r=[K_("gcc")], w=[K_("egc")])
                    yield
                    for h in range(8):
                        op("pe", lambda h=h: PE.matmul(T2b[po:po + 64, h * 64:(h + 1) * 64], lhsT=t["kT"][:, h, :], rhs=t["kT"][:, h, :],
                                                       start=True, stop=True), r=[K_("kT")], w=[k2b])
                    for h in range(8):
                        op("pe", lambda h=h: PE.matmul(T1a[po:po + 64, h * 64:(h + 1) * 64], lhsT=t["kT"][:, h, :], rhs=t["qT"][:, h, :],
                                                       start=True, stop=True), r=[K_("kT"), K_("qT")], w=[k1a])
                    yield
                    op("pool", lambda: PO.tensor_tensor(out=t["DTs"][:, :, :], in0=t["DTi"][:, :, :], in1=bcm(offd), op=ALU.mult),
                       r=[K_("DTi"), "msk"], w=[K_("DTs")])
                    op("dve", lambda: V.tensor_tensor(out=t["tmpw"][:, :, :], in0=w3(T2b, po), in1=t["DTs"][:, :, :], op=ALU.mult),
                       r=[k2b, K_("DTs")], w=[K_("tmpw")])
                    op("dve", lambda: V.tensor_tensor(out=t["Q0"][:, :, :], in0=w3(T2a, po), in1=t["tmpw"][:, :, :], op=ALU.mult),
                       r=[k2a, K_("tmpw")], w=[K_("Q0")])
                    op("dve", lambda: V.tensor_tensor(out=t["tmpw"][:, :, :], in0=w3(T2b, po), in1=t["DCs"][:, :, :], op=ALU.mult),
                       r=[k2b, K_("DCs"), K_("tmpw")], w=[K_("tmpw")])
                    op("dve", lambda: V.tensor_tensor(out=t["P0"][:, :, :], in0=t["tmpw"][:, :, :], in1=bc(bd, 64), op=ALU.mult),
                       r=[K_("tmpw"), K_("b16")], w=[K_("P0")])
                    op("dve", lambda: V.tensor_tensor(out=rr(t["QKm"][:, :, :]), in0=w3(T1a, po), in1=t["DTi"][:, :, :], op=ALU.mult),
                       r=[k1a, K_("DTi")], w=[K_("QKm")])
                    op("dve", lambda: V.tensor_tensor(out=t["R"][:, :, :], in0=bcm(i64), in1=t["Q0"][:, :, :], op=ALU.subtract),
                       r=["ident", K_("Q0")], w=[K_("R")])
                    yield
                    Qc, Pc, Qn, Pn = "Q0", "P0", "Q1", "P1"
                    for lev in range(1, 6):
                        for h in range(8):
                            op("pe", lambda h=h: PE.matmul(T2a[po:po + 64, h * 64:(h + 1) * 64], lhsT=t[Qc][:, h, :], rhs=t[Pc][:, h, :],
                                                           start=True, stop=True), r=[K_(Qc), K_(Pc)], w=[k2a])
                        if lev < 5:
                            for h in range(8):
                                op("pe", lambda h=h: PE.matmul(T1a[po:po + 64, h * 64:(h + 1) * 64], lhsT=t[Pc][:, h, :], rhs=t[Qc][:, h, :],
                                                               start=True, stop=True), r=[K_(Qc), K_(Pc)], w=[k1a])
                        yield
                        op("act", lambda: A.copy(out=t[Pn][:, :, :], in_=w3(T2a, po)), r=[k2a], w=[K_(Pn)])
                        if lev < 5:
                            op("dve", lambda: V.tensor_copy(out=t[Qn][:, :, :], in_=w3(T1a, po)), r=[k1a], w=[K_(Qn)])
                        for h in range(8):
                            op("pe", lambda h=h: PE.matmul(T2b[po:po + 64, h * 64:(h + 1) * 64], lhsT=t[Pn][:, h, :], rhs=t["R"][:, h, :],
                                                           start=True, stop=True), r=[K_(Pn), K_("R")], w=[k2b])
                        op("dve", lambda: V.tensor_tensor(out=t["R"][:, :, :], in0=w3(T2b, po), in1=t["R"][:, :, :], op=ALU.add),
                           r=[k2b, K_("R")], w=[K_("R")])
                        Qc, Pc, Qn, Pn = Qn, Pn, Qc, Pc
                        yield

                def dn_B(d, t0, is_lat, par):
                    t = TL[d][par]
                    po = d * 64
                    i64, offd = I64[d], OFFD[d]
                    rr = lambda ap: ap.bitcast(F32R)
                    rq = rr if d == 0 else (lambda ap: ap)
                    K_ = lambda nm: (nm, d, par)
                    T1a, T1b, T2a, T2b = PS[d * 4], PS[d * 4 + 1], PS[d * 4 + 2], PS[d * 4 + 3]
                    k1a, k1b, k2a, k2b = pk[d * 4], pk[d * 4 + 1], pk[d * 4 + 2], pk[d * 4 + 3]
                    last = 63 if d == 0 else 0
                    gd = t["g16"][:, d * 8:(d + 1) * 8]
                    bd = t["b16"][:, d * 8:(d + 1) * 8]
                    op("act", lambda: A.copy(out=rr(t["Rr"][:, :, :]), in_=t["R"][:, :, :]), r=[K_("R")], w=[K_("Rr")])
                    op("dve", lambda: V.tensor_tensor(out=t["bco"][:, :], in0=bd, in1=t["egc"][:, :], op=ALU.mult),
                       r=[K_("b16"), K_("egc")], w=[K_("bco")])
                    op("dve", lambda: V.tensor_tensor(out=rr(t["kbg"][:, :, :]), in0=t["ktok"][:, :, :], in1=bc(t["bco"][:, :], 128), op=ALU.mult),
                       r=[K_("ktok"), K_("bco")], w=[K_("kbg")])
                    op("dve", lambda: V.tensor_tensor(out=rr(t["vb"][:, :, :]), in0=t["vtok"][:, :, :], in1=bc(bd, 128), op=ALU.mult),
                       r=[K_("vtok"), K_("b16")], w=[K_("vb")])
                    op("dve", lambda: V.tensor_tensor(out=rr(t["kd"][:, :, :]), in0=t["ktok"][:, :, :],
                                                        in1=bc(t["DTi"][:, :, last], 128), op=ALU.mult),
                       r=[K_("ktok"), K_("DTi")], w=[K_("kd")])
                    op("dve", lambda: V.tensor_tensor(out=rr(t["qdT"][:, :, :]), in0=t["qT"][:, :, :], in1=t["egr"][:, :, :], op=ALU.mult),
                       r=[K_("qT"), K_("egr")], w=[K_("qdT")])
                    yield
                    for h in range(8):
                        op("pe", lambda h=h: PE.matmul(T1b[:, h * 64:(h + 1) * 64], lhsT=rr(t["kbg"][:, h, :]), rhs=rr(t["Rr"][:, h, :]),
                                                       start=True, stop=True), r=[K_("kbg"), K_("Rr")], w=[k1b])
                    yield
                    op("act", lambda: A.activation(out=rr(fl(t["wTn"])), in_=T1b[:, :], func=AF.Copy, scale=-1.0), r=[k1b], w=[K_("wTn")])
                    for half in range(2):
                        for h in range(half * 4, half * 4 + 4):
                            o_ = T1b[po:po + 64, (h % 4) * 128:(h % 4 + 1) * 128]
                            op("pe", lambda h=h, o_=o_: PE.matmul(o_, lhsT=rq(t["Rr"][:, h, :]), rhs=rq(t["vb"][:, h, :]), start=True, stop=False),
                               r=[K_("Rr"), K_("vb")], w=[k1b])
                            op("pe", lambda h=h, o_=o_: PE.matmul(o_, lhsT=rq(t["wTn"][:, h, :]), rhs=rq(t["S"][:, h, :]), start=False, stop=True),
                               r=[K_("wTn"), ("S", d, h)], w=[k1b])
                        src_ = T1b[po:po + 64, :].rearrange("p (h f) -> p h f", h=4)
                        if half == 0:
                            op("act", lambda: A.copy(out=rr(t["vn"][:, 0:4, :]), in_=src_), r=[k1b], w=[K_("vn")])
                        else:
                            op("dve", lambda: V.tensor_copy(out=rr(t["vn"][:, 4:8, :]), in_=src_), r=[k1b], w=[K_("vn")])
                        yield
                    if is_lat:
                        for h in range(8):
                            o_ = T1b[:, h * 64:(h + 1) * 64]
                            op("pe", lambda h=h, o_=o_: PE.matmul(o_, lhsT=rr(t["S"][:, h, :]), rhs=rr(t["qdT"][:, h, :]), start=True, stop=False),
                               r=[("S", d, h), K_("qdT")], w=[k1b])
                            op("pe", lambda h=h, o_=o_: PE.matmul(o_, lhsT=rr(t["vn"][:, h, :]), rhs=rr(t["QKm"][:, h, :]), start=False, stop=True),
                               r=[K_("vn"), K_("QKm")], w=[k1b])
                        op("act", lambda: A.copy(out=fl(t["osb"]), in_=T1b[:, :]), r=[k1b], w=[K_("osb")])
                        s0 = t0 - CTX
                        dma("sp", OFB[d][:, :, s0:s0 + 64].rearrange("h p t -> p h t"), t["osb"][:, :, :], r=[K_("osb")])
                        yield
                    for half in range(2):
                        for h in range(half * 4, half * 4 + 4):
                            o_ = T1b[:, (h % 4) * 128:(h % 4 + 1) * 128]
                            op("pe", lambda h=h, o_=o_: PE.matmul(o_, lhsT=rr(t["kd"][:, h, :]), rhs=rr(t["vn"][:, h, :]), start=True, stop=True),
                               r=[K_("kd"), K_("vn")], w=[k1b])
                        for h in range(half * 4, half * 4 + 4):
                            o_ = T1b[:, (h % 4) * 128:(h % 4 + 1) * 128]
                            op("dve", lambda h=h, o_=o_: V.scalar_tensor_tensor(out=rr(t["S"][:, h, :]), in0=t["S"][:, h, :],
                                                                                scalar=t["egr"][:, h, last:last + 1], in1=o_,
                                                                                op0=ALU.mult, op1=ALU.add),
                               r=[("S", d, h), K_("egr"), k1b], w=[("S", d, h)])
                        yield

                fw = [(64 * j, False) for j in range(4)] + [(CTX + 64 * j, True) for j in range(64)]
                bw = [(64 * j, False) for j in range(3, -1, -1)] + [(CTX + 64 * j, True) for j in range(63, -1, -1)]
                for step in range(69):
                    gens = []
                    if step >= 1:
                        gens += [dn_B(0, *fw[step - 1], (step - 1) % 2), dn_B(1, *bw[step - 1], (step - 1) % 2)]
                    if step < 68:
                        gens += [dn_A(0, *fw[step], step % 2), dn_A(1, *bw[step], step % 2)]
                    while gens:
                        for g in list(gens):
                            try:
                                next(g)
                            except StopIteration:
                                gens.remove(g)
                k.barrier()
            with contextlib.ExitStack() as ph:
                wo = sb(ph, "dwo", [128, 8, D], BF16)
                gb0 = sb(ph, "gb0", [128, D])
                dg = sb(ph, "dg", [128, 128])
                xts = [sb(ph, f"xt{i}", [128, 2, D]) for i in range(2)]
                ofs = [sb(ph, f"of{i}", [128, 8, 256]) for i in range(2)]
                obs = [sb(ph, f"ob{i}", [128, 8, 256]) for i in range(2)]
                zss = [sb(ph, f"zs{i}", [128, 8, 256]) for i in range(2)]
                sq = sb(ph, "rsq", [128, 2048])
                rr = sb(ph, "rr", [128, 2048])
                yb = sb(ph, "yb", [128, 8, 256], BF16)
                tmp = [sb(ph, f"tmp{i}", [128, 512]) for i in range(2)]
                for c in range(8):
                    dma("pool", wo[:, c, :], dn_w_out[c * 128:(c + 1) * 128, :], w=[("dwo", c)])
                bcast_row(gb0, "gb0", gate_ap(l, s, 0), 1.0, dg)
                gn = cols[:, C_GN:C_GN + 1]
                n = 0
                for ti in range(16):
                    xt, xk = xts[ti % 2], ("xt", ti % 2)
                    of, ofk = ofs[ti % 2], ("of", ti % 2)
                    ob, obk = obs[ti % 2], ("ob", ti % 2)
                    zs, zk = zss[ti % 2], ("zs", ti % 2)
                    s0 = ti * 256
                    for b in range(2):
                        for cl in range(2):
                            dma("sp", xt[cl * 64:(cl + 1) * 64, b, :], Xlat_cm[4 * ti + 2 * b + cl], w=[xk])
                    dma("sp", of[:, :, :], OFB[0][:, :, s0:s0 + 256].rearrange("h p t -> p h t"), w=[ofk])
                    dma("sp", ob[:, :, :], OFB[1][:, :, s0:s0 + 256].rearrange("h p t -> p h t"), w=[obk])
                    dma("sp", zs[:, :, :], ZD[:, :, CTX + s0:CTX + s0 + 256].rearrange("c p t -> p c t"), w=[zk])
                    off = of[:, :, :].rearrange("p h t -> p (h t)")
                    op("dve", lambda: V.tensor_tensor(out=off, in0=off, in1=ob[:, :, :].rearrange("p h t -> p (h t)"), op=ALU.add),
                       r=[ofk, obk], w=[ofk])
                    op("act", lambda: A.activation(out=sq[:, :], in_=off, func=AF.Square), r=[ofk], w=["rsq"])
                    for q in range(4):
                        pb = 2 + q
                        op("pe", lambda q=q, pb=pb: PE.matmul(PS[pb][:, :], lhsT=ones[:, :], rhs=sq[:, q * 512:(q + 1) * 512],
                                                              start=True, stop=True), r=["rsq", "ones"], w=[pk[pb]])
                        op("act", lambda q=q, pb=pb: A.activation(out=rr[:, q * 512:(q + 1) * 512], in_=PS[pb][:, :], func=AF.Ln,
                                                                  bias=EPS, scale=1.0 / 128), r=[pk[pb]], w=["rr"])
                    op("act", lambda: A.activation(out=rr[:, :], in_=rr[:, :], func=AF.Exp, scale=-0.5), r=["rr"], w=["rr"])
                    op("dve", lambda: V.tensor_tensor(out=off, in0=off, in1=rr[:, :], op=ALU.mult), r=[ofk, "rr"], w=[ofk])
                    op("dve", lambda: V.scalar_tensor_tensor(out=yb[:, :, :].rearrange("p h t -> p (h t)"), in0=off, scalar=gn,
                                                             in1=zs[:, :, :].rearrange("p h t -> p (h t)"), op0=ALU.mult, op1=ALU.mult),
                       r=[ofk, zk, "cols"], w=["yb"])
                    for tb in range(2):
                        for dh in range(2):
                            po = n % 2
                            tm = tmp[n % 2]
                            n += 1
                            for c in range(8):
                                op("pe", lambda c=c, po=po, tb=tb, dh=dh: PE.matmul(
                                    PS[po][:, :], lhsT=yb[:, c, tb * 128:(tb + 1) * 128], rhs=wo[:, c, dh * 512:(dh + 1) * 512],
                                    start=(c == 0), stop=(c == 7)), r=["yb", ("dwo", c)], w=[pk[po]])
                            op("dve", lambda po=po, tm=tm, dh=dh: V.tensor_tensor(
                                out=tm[:], in0=PS[po][:, :], in1=gb0[:, dh * 512:(dh + 1) * 512], op=ALU.mult),
                               r=[pk[po], "gb0"], w=[("tmp", po)])
                            op("pool", lambda tm=tm, tb=tb, dh=dh, xt=xt: PO.tensor_tensor(
                                out=xt[:, tb, dh * 512:(dh + 1) * 512], in0=tm[:], in1=xt[:, tb, dh * 512:(dh + 1) * 512], op=ALU.add),
                               r=[("tmp", po), xk], w=[xk])
                    for b in range(2):
                        for cl in range(2):
                            dma("sp", Xlat_cm[4 * ti + 2 * b + cl], xt[cl * 64:(cl + 1) * 64, b, :], r=[xk])
                k.barrier()

        def final_phase(raw):
            with contextlib.ExitStack() as ph:
                xts = [sb(ph, f"fx{i}", [128, 2, D]) for i in range(2)]
                junk = sb(ph, "fj", [128, D])
                sts = [sb(ph, f"fs{i}", [128, 4]) for i in range(2)]
                for ti in range(SEQ // 256):
                    xt = xts[ti % 2]
                    xk = ("fx", ti % 2)
                    dma("sp", xt[:, :, :], X[CTX + ti * 256:CTX + (ti + 1) * 256, :].rearrange("(b p) d -> p b d", p=128), w=[xk])
                    if not raw:
                        for b in range(2):
                            st = sts[b]
                            ks = ("fs", b)
                            op("act", lambda b=b, st=st: A.activation(out=junk[:], in_=xt[:, b, :], func=AF.Square, accum_out=st[:, 0:1]),
                               r=[xk], w=["fj", ks])
                            op("act", lambda st=st: A.activation(out=st[:, 1:2], in_=st[:, 0:1], func=AF.Ln, bias=EPS, scale=1.0 / D),
                               r=[ks], w=[ks])
                            op("act", lambda st=st: A.activation(out=st[:, 2:3], in_=st[:, 1:2], func=AF.Exp, scale=-0.5), r=[ks], w=[ks])
                            op("dve", lambda b=b, st=st: V.scalar_tensor_tensor(out=xt[:, b, :], in0=xt[:, b, :], scalar=st[:, 2:3],
                                                                               in1=rowsb[:, 32:1056], op0=ALU.mult, op1=ALU.mult),
                               r=[xk, ks, "rowsb"], w=[xk])
                    dma("sp", out[ti * 256:(ti + 1) * 256, :].rearrange("(b p) d -> p b d", p=128), xt[:, :, :], r=[xk])
                k.barrier()

        stages = []
        def ffn_first():
            ffn_phase(0, 0, 0, True, True, pre=pre_w)
            pre_stack.close()
        stages.append(ffn_first)
        stages.append(lru_phase)
        stages.append(lambda: ffn_phase(0, 2, 1, False, True))
        stages.append(lambda: ffn_phase(1, 0, 2, False, True))
        stages.append(dn_phase)
        stages.append(lambda: ffn_phase(1, 2, 3, False, False))
        from_lru = len(stages)
        nst = 0
        for f in stages:
            if nst >= stage:
                break
            f()
            nst += 1
        if stage < 1:
            pre_stack.close()
        final_phase(raw=dbg)
    return nc


def host_inputs(inputs, b):
    f = lambda a: np.ascontiguousarray(np.asarray(a, dtype=np.float32))
    cols = np.zeros((NCOLS, 128), np.float32)
    cols[C_C:C_C + 8] = f(inputs["c"])[b].reshape(8, 128)
    cols[C_CC:C_CC + 8] = f(inputs["c_ctx"]).reshape(8, 128)
    cols[C_BADA:C_BADA + 144] = f(inputs["b_ada"]).reshape(144, 128)
    cols[C_GSUB:C_GSUB + 48] = f(inputs["g_sub"]).reshape(48, 128)
    cols[C_LCW:C_LCW + 32] = f(inputs["lru_conv_w"])[0].reshape(32, 128)
    cols[C_LCB:C_LCB + 8] = f(inputs["lru_conv_b"])[0].reshape(8, 128)
    cols[C_LBG:C_LBG + 32] = f(inputs["lru_b_gate"])[0].reshape(32, 128)
    cols[C_LAM:C_LAM + 16] = f(inputs["lru_lambda"])[0].reshape(16, 128)
    cols[C_DCW:C_DCW + 96] = f(inputs["dn_conv_w"])[0].reshape(96, 128)
    cols[C_GN:C_GN + 1] = f(inputs["dn_g_norm"])[0].reshape(1, 128)
    rows = np.concatenate([f(inputs["dn_a_log"])[0].reshape(16), f(inputs["dn_dt_bias"])[0].reshape(16),
                           f(inputs["g_final"]).reshape(1024)]).reshape(1, 1056)
    return {
        "x": f(inputs["x"])[b], "ctx": f(inputs["ctx"])[b], "cols_src": cols, "rows_src": rows,
        "w_ada": f(inputs["w_ada"]), "ffn_w_in": f(inputs["ffn_w_in"]).reshape(4, D, 2 * DFF),
        "ffn_w_out": f(inputs["ffn_w_out"]).reshape(4, DFF, D), "lru_w_in": f(inputs["lru_w_in"])[0],
        "lru_w_gate": f(inputs["lru_w_gate"])[0].reshape(16, 256, 256), "lru_w_out": f(inputs["lru_w_out"])[0],
        "dn_w_in": f(inputs["dn_w_in"])[0], "dn_w_out": f(inputs["dn_w_out"])[0],
    }


def kernel(**inputs):
    nc = build()
    in_maps = [host_inputs(inputs, c % 4) for c in range(8)]
    res = run_bass_kernel_spmd(nc, in_maps, core_ids=list(range(8)))
    return np.stack([np.asarray(res.results[b]["out"], dtype=np.float32) for b in range(4)], axis=0)
```

```python
import contextlib
import math
import numpy as np
import concourse.bass as bass
import concourse.mybir as mybir
from concourse.bass_utils import run_bass_kernel_spmd

F32 = mybir.dt.float32
BF16 = mybir.dt.bfloat16
F32R = mybir.dt.float32r
AF = mybir.ActivationFunctionType
ALU = mybir.AluOpType

D = 1024
SEQ = 4096
CTX = 256
T = SEQ + CTX
DFF = 2816
NF = DFF // 128
EPS = 1e-6
CTX0 = 2
LAT0 = CTX0 + CTX + 3
TP = LAT0 + SEQ + 1
NEG = -30000.0

C_C, C_CC, C_BADA, C_GSUB, C_LCW, C_LCB, C_LBG, C_LAM, C_DCW, C_GN = 0, 8, 16, 160, 208, 240, 248, 280, 296, 392
NCOLS = 512


class K:
    def __init__(self, nc, es):
        self.nc = nc
        self.es = es
        self.eng = {"pe": nc.tensor, "dve": nc.vector, "act": nc.scalar, "pool": nc.gpsimd, "sp": nc.sync}
        self.ce = ["pe", "dve", "act", "pool"]
        self.EPOCH = 30000
        self.cnt = {e: 0 for e in self.ce}
        self.ep = {e: 0 for e in self.ce}
        self.sems = {e: [es.enter_context(nc.semaphore(f"s_{e}_0"))] for e in self.ce}
        self.KD = 8
        self.dq = ["sp", "pool"]
        self.dsem = {q: [es.enter_context(nc.semaphore(f"d_{q}_{i}")) for i in range(self.KD)] for q in self.dq}
        self.dn = {q: 0 for q in self.dq}
        self.seen = {e: {} for e in self.eng}
        self.st = {}

    def _wait(self, e, tok):
        if tok is None:
            return
        if tok[0] == "c":
            _, f, ep, idx = tok
            if f == e and e == "pe":
                return
            key = (f, ep)
            if self.seen[e].get(key, 0) >= idx:
                return
            self.eng[e].wait_ge(self.sems[f][ep], idx)
            self.seen[e][key] = idx
        else:
            _, q, si, cnt = tok
            key = ("d", q, si)
            if self.seen[e].get(key, 0) >= cnt:
                return
            self.eng[e].wait_ge(self.dsem[q][si], cnt)
            self.seen[e][key] = cnt

    def _deps(self, e, r, w):
        for k in r:
            s = self.st.get(k)
            if s:
                self._wait(e, s[0])
        for k in w:
            s = self.st.get(k)
            if s:
                self._wait(e, s[0])
                for t in s[1].values():
                    self._wait(e, t)

    def _record(self, tok, r, w, rid):
        for k in w:
            self.st[k] = [tok, {}]
        for k in r:
            s = self.st.setdefault(k, [None, {}])
            s[1][rid] = tok

    def op(self, e, fn, r=(), w=()):
        self._deps(e, r, w)
        inst = fn()
        if self.cnt[e] >= self.EPOCH:
            self.ep[e] += 1
            self.cnt[e] = 0
            self.sems[e].append(self.es.enter_context(self.nc.semaphore(f"s_{e}_{self.ep[e]}")))
        self.cnt[e] += 1
        inst.then_inc(self.sems[e][self.ep[e]], 1)
        tok = ("c", e, self.ep[e], self.cnt[e])
        self._record(tok, r, w, e)
        return tok

    def dma(self, q, out, in_, r=(), w=()):
        n = self.dn[q]
        si = n % self.KD
        cnt = 16 * (n // self.KD + 1)
        if cnt > 16:
            self._wait(q, ("d", q, si, cnt - 16))
        self._deps(q, r, w)
        self.eng[q].dma_start(out=out, in_=in_).then_inc(self.dsem[q][si], 16)
        self.dn[q] = n + 1
        tok = ("d", q, si, cnt)
        self._record(tok, r, w, ("d", q, si))
        return tok

    def barrier(self):
        toks = []
        for f in self.ce:
            if self.cnt[f] > 0:
                toks.append(("c", f, self.ep[f], self.cnt[f]))
        for q in self.dq:
            n = self.dn[q]
            for si in range(self.KD):
                uses = (n - si + self.KD - 1) // self.KD if n > si else 0
                if uses > 0:
                    toks.append(("d", q, si, 16 * uses))
        for e in self.eng:
            for t in toks:
                if t[0] == "c" and t[1] == e:
                    continue
                self._wait(e, t)
        self.st = {}


def build(stage=99, dbg=False):
    nc = bass.Bass("TRN2", target_bir_lowering=False)
    dt = lambda name, shape, dtype=F32, kind="ExternalInput": nc.dram_tensor(name, shape, dtype, kind=kind).ap()
    x_in = dt("x", [SEQ, D])
    ctx_in = dt("ctx", [CTX, D])
    cols_src = dt("cols_src", [NCOLS, 128])
    rows_src = dt("rows_src", [1, 1056])
    w_ada = dt("w_ada", [2, D, 9 * D])
    ffn_w_in = dt("ffn_w_in", [4, D, 2 * DFF])
    ffn_w_out = dt("ffn_w_out", [4, DFF, D])
    lru_w_in = dt("lru_w_in", [D, 2048])
    lru_w_gate = dt("lru_w_gate", [16, 256, 256])
    lru_w_out = dt("lru_w_out", [D, D])
    dn_w_in = dt("dn_w_in", [D, 4128])
    dn_w_out = dt("dn_w_out", [D, D])
    out = dt("out", [SEQ, D], kind="ExternalOutput")
    X = dt("Xs", [T, D], kind="Internal")
    PT = dt("PTs", [16, 128, TP], kind="Internal")
    ZT = dt("ZTs", [8, 128, T], BF16, kind="Internal")
    QP = dt("QPs", [24, 128, TP], BF16, kind="Internal")
    ZD = dt("ZDs", [8, 128, T], kind="Internal")
    GD = dt("GDs", [T, 16], kind="Internal")
    BD = dt("BDs", [T, 16], kind="Internal")
    QT = dt("QTs", [8, 128, T], kind="Internal")
    KT = dt("KTs", [8, 128, T], kind="Internal")
    KTOK = dt("KTOKs", [8, T, 128], kind="Internal")
    VTOK = dt("VTOKs", [8, T, 128], kind="Internal")
    OFB = [dt("OFs", [8, 128, SEQ], kind="Internal"), dt("OBs", [8, 128, SEQ], kind="Internal")]
    Xlat_cm = X[CTX:T, :].rearrange("(r c) d -> c r d", c=64)

    with contextlib.ExitStack() as es:
        k = K(nc, es)
        op, dma = k.op, k.dma
        V, A, PE, PO = nc.vector, nc.scalar, nc.tensor, nc.gpsimd

        uid = [0]

        def sb(st, name, shape, dtype=F32):
            uid[0] += 1
            return st.enter_context(nc.sbuf_tensor(f"{name}_u{uid[0]}", shape, dtype))

        PS = [es.enter_context(nc.psum_tensor(f"ps{i}", [128, 512], F32)) for i in range(8)]
        pk = [("ps", i) for i in range(8)]

        ident = sb(es, "ident", [128, 128])
        ones = sb(es, "ones", [128, 128])
        iot = sb(es, "iot", [128, 128])
        cols = sb(es, "cols", [128, NCOLS])
        mcol = sb(es, "mcol", [128, 2, 72, 2])
        gsall = sb(es, "gsall", [128, 2, 3, 2, 8])
        rowsb = sb(es, "rowsb", [128, 1056])
        op("pool", lambda: PO.iota(iot[:], pattern=[[1, 128]], base=0, channel_multiplier=-1,
                                   allow_small_or_imprecise_dtypes=True), w=["iot"])
        op("dve", lambda: V.tensor_scalar(out=ident[:], in0=iot[:], scalar1=0.0, scalar2=None, op0=ALU.is_equal),
           r=["iot"], w=["ident"])
        op("dve", lambda: V.memset(ones[:], 1.0), w=["ones"])

        def load_ffn_w(w1, w2, fi):
            for fb in range(NF // 2):
                for which in range(2):
                    c0 = which * DFF + fb * 256
                    dma("pool", w1[:, :, c0:c0 + 256], ffn_w_in[fi][:, c0:c0 + 256].rearrange("(k p) c -> p k c", p=128),
                        w=[("w1", which, fb)])
            for f in range(NF):
                dma("pool", w2[:, f, :], ffn_w_out[fi][f * 128:(f + 1) * 128, :], w=[("w2", f)])

        pre_stack = contextlib.ExitStack()
        pre_w = (sb(pre_stack, "w1p", [128, 8, 2 * DFF], BF16), sb(pre_stack, "w2p", [128, NF, D], BF16))
        if stage >= 1:
            load_ffn_w(pre_w[0], pre_w[1], 0)

        with contextlib.ExitStack() as ph:
            stg = sb(ph, "stg", [128, 4, 128])
            rows1 = sb(ph, "rows1", [1, 1056])
            sc = sb(ph, "sc", [128, 8, 2])
            was = [sb(ph, f"wa{i}", [128, 8, 512]) for i in range(2)]
            dma("sp", stg[:, :, :], cols_src.rearrange("(g p) f -> p g f", p=128), w=["stg"])
            dma("sp", rows1[:, :], rows_src[:, :], w=["rows1"])
            for g in range(4):
                op("pe", lambda g=g: PE.transpose(out=PS[0][:, g * 128:(g + 1) * 128], in_=stg[:, g, :], identity=ident[:]),
                   r=["stg", "ident"], w=[pk[0]])
            op("dve", lambda: V.tensor_copy(out=cols[:], in_=PS[0][:, :]), r=[pk[0]], w=["cols"])
            for i, (a, b) in enumerate([(0, 512), (512, 1024), (1024, 1056)]):
                op("pe", lambda a=a, b=b, i=i: PE.matmul(PS[1 + i][:, 0:b - a], lhsT=ones[0:1, :], rhs=rows1[0:1, a:b],
                                                         start=True, stop=True), r=["ones", "rows1"], w=[pk[1 + i]])
                op("act", lambda a=a, b=b, i=i: A.copy(out=rowsb[:, a:b], in_=PS[1 + i][:, 0:b - a]), r=[pk[1 + i]], w=["rowsb"])
            op("act", lambda: A.activation(out=sc[:, :, 0], in_=cols[:, C_C:C_C + 8], func=AF.Silu), r=["cols"], w=["sc"])
            op("act", lambda: A.activation(out=sc[:, :, 1], in_=cols[:, C_CC:C_CC + 8], func=AF.Silu), r=["cols"], w=["sc"])
            it = 0
            for l in range(2):
                for cg in range(18):
                    wa = was[it % 2]
                    wk = ("wa", it % 2)
                    pz = 4 + (it % 2)
                    dma("sp", wa[:, :, :], w_ada[l][:, cg * 512:(cg + 1) * 512].rearrange("(k p) c -> p k c", p=128), w=[wk])
                    for cc in range(4):
                        for kk in range(8):
                            op("pe", lambda cc=cc, kk=kk, wa=wa, pz=pz: PE.matmul(
                                PS[pz][:, cc * 2:cc * 2 + 2], lhsT=wa[:, kk, cc * 128:(cc + 1) * 128], rhs=sc[:, kk, :],
                                start=(kk == 0), stop=(kk == 7)), r=[wk, "sc"], w=[pk[pz]])
                    op("dve", lambda l=l, cg=cg, pz=pz: V.tensor_tensor(
                        out=mcol[:, l, cg * 4:(cg + 1) * 4, :], in0=PS[pz][:, 0:8].rearrange("p (c v) -> p c v", v=2),
                        in1=cols[:, C_BADA + l * 72 + cg * 4:C_BADA + l * 72 + cg * 4 + 4].unsqueeze(2).broadcast_to([128, 4, 2]),
                        op=ALU.add), r=[pk[pz], "cols"], w=["mcol"])
                    it += 1
            for l in range(2):
                for s in range(3):
                    for v in range(2):
                        op("dve", lambda l=l, s=s, v=v: V.scalar_tensor_tensor(
                            out=gsall[:, l, s, v, :], in0=mcol[:, l, (s * 3 + 1) * 8:(s * 3 + 2) * 8, v], scalar=1.0,
                            in1=cols[:, C_GSUB + (l * 3 + s) * 8:C_GSUB + (l * 3 + s) * 8 + 8], op0=ALU.add, op1=ALU.mult),
                           r=["mcol", "cols"], w=["gsall"])
            k.barrier()

        def sh_ap(l, s, v):
            return mcol[:, l, (s * 3) * 8:(s * 3) * 8 + 8, v]

        def gate_ap(l, s, v):
            return mcol[:, l, (s * 3 + 2) * 8:(s * 3 + 2) * 8 + 8, v]

        def gs_ap(l, s, v):
            return gsall[:, l, s, v, :]

        def bcast_row(dst, dkey, col_ap, factor, dg):
            for j in range(8):
                op("dve", lambda j=j: V.tensor_scalar(out=dg[:, :], in0=ident[:], scalar1=col_ap[:, j:j + 1], scalar2=float(factor),
                                                      op0=ALU.mult, op1=ALU.mult), r=["ident", "mcol"], w=["dg"])
                op("pe", lambda j=j: PE.matmul(PS[7][:, 0:128], lhsT=ones[:, :], rhs=dg[:, :], start=True, stop=True),
                   r=["dg", "ones"], w=[pk[7]])
                op("act", lambda j=j: A.copy(out=dst[:, j * 128:(j + 1) * 128], in_=PS[7][:, 0:128]), r=[pk[7]], w=[dkey])

        def make_ln(ph):
            S = {"junk": [sb(ph, f"lnj{i}", [128, 1024]) for i in range(2)],
                 "xs": [sb(ph, f"lnx{i}", [128, 1024]) for i in range(2)],
                 "st": [sb(ph, f"lns{i}", [128, 4]) for i in range(2)], "n": 0}

            def ln_A(xap, xkey):
                i = S["n"] % 2
                S["n"] += 1
                junk, xs, st = S["junk"][i], S["xs"][i], S["st"][i]
                kj, kx, ks = ("lnj", i), ("lnx", i), ("lns", i)
                op("act", lambda: A.activation(out=junk[:], in_=xap, func=AF.Square, accum_out=st[:, 0:1]), r=[xkey], w=[kj, ks])
                op("act", lambda: A.activation(out=st[:, 1:2], in_=st[:, 0:1], func=AF.Ln, bias=EPS, scale=1.0 / D), r=[ks], w=[ks])
                op("act", lambda: A.activation(out=st[:, 2:3], in_=st[:, 1:2], func=AF.Exp, scale=-0.5), r=[ks], w=[ks])
                op("dve", lambda: V.tensor_scalar(out=xs[:], in0=xap, scalar1=st[:, 2:3], scalar2=None, op0=ALU.mult),
                   r=[xkey, ks], w=[kx])
                return i

            def ln_B(i, gs, sh, hdst, hkey, pbanks):
                xs = S["xs"][i]
                kx = ("lnx", i)
                for half in range(2):
                    pb = pbanks[half]
                    for j4 in range(4):
                        j = half * 4 + j4
                        op("pe", lambda j=j, j4=j4, pb=pb: PE.transpose(out=PS[pb][:, j4 * 128:(j4 + 1) * 128],
                                                                        in_=xs[:, j * 128:(j + 1) * 128], identity=ident[:]),
                           r=[kx, "ident"], w=[pk[pb]])
                    for j4 in range(4):
                        j = half * 4 + j4
                        op("act", lambda j=j, j4=j4, pb=pb: A.activation(out=hdst(j), in_=PS[pb][:, j4 * 128:(j4 + 1) * 128],
                                                                         func=AF.Identity, scale=gs[:, j:j + 1], bias=sh[:, j:j + 1]),
                           r=[pk[pb], "mcol", "gsall"], w=[hkey])

            def ln_T(xap, xkey, gs, sh, hdst, hkey, pbanks):
                ln_B(ln_A(xap, xkey), gs, sh, hdst, hkey, pbanks)
            ln_T.A = ln_A
            ln_T.B = ln_B
            return ln_T

        def tiles_for(first, with_ctx):
            tl = []
            if with_ctx:
                tl.append((1, ctx_in[:, :] if first else X[0:CTX, :], X[0:CTX, :]))
            for i in range(SEQ // 256):
                src = x_in[i * 256:(i + 1) * 256, :] if first else X[CTX + i * 256:CTX + (i + 1) * 256, :]
                tl.append((0, src, X[CTX + i * 256:CTX + (i + 1) * 256, :]))
            return tl

        def ffn_phase(l, s, fi, first, with_ctx, pre=None):
            with contextlib.ExitStack() as ph:
                if pre is None:
                    w1 = sb(ph, "w1", [128, 8, 2 * DFF], BF16)
                    w2 = sb(ph, "w2", [128, NF, D], BF16)
                else:
                    w1, w2 = pre
                xts = [sb(ph, f"xt{i}", [128, 2, D]) for i in range(2)]
                hT = sb(ph, "hT", [128, 8, 256], BF16)
                actT = sb(ph, "actT", [128, NF, 256], BF16)
                sg = [sb(ph, f"sg{i}", [128, 256]) for i in range(2)]
                tmp = [sb(ph, f"tmp{i}", [128, 512]) for i in range(2)]
                gb = [sb(ph, f"gb{i}", [128, D]) for i in range(2)]
                dg = sb(ph, "dg", [128, 128])
                ln_T = make_ln(ph)
                if pre is None:
                    load_ffn_w(w1, w2, fi)
                for v in range(2):
                    if v == 1 and not with_ctx:
                        continue
                    bcast_row(gb[v], ("gb", v), gate_ap(l, s, v), 0.5, dg)
                n = 0
                tl = tiles_for(first, with_ctx)

                def load(ti):
                    dma("sp", xts[ti % 2][:, :, :], tl[ti][1].rearrange("(b p) d -> p b d", p=128), w=[("xt", ti % 2)])

                def lnA(ti):
                    return [ln_T.A(xts[ti % 2][:, b, :], ("xt", ti % 2)) for b in range(2)]

                load(0)
                hA = lnA(0)
                for ti, (v, src, dst) in enumerate(tl):
                    xt = xts[ti % 2]
                    xk = ("xt", ti % 2)
                    if ti + 1 < len(tl):
                        load(ti + 1)
                    for b in range(2):
                        ln_T.B(hA[b], gs_ap(l, s, v), sh_ap(l, s, v), lambda j, b=b: hT[:, j, b * 128:(b + 1) * 128], "hT", (0, 1))
                    for f in range(NF):
                        pg, pu = 2 + 2 * (f % 2), 3 + 2 * (f % 2)
                        for which, pb in ((0, pg), (1, pu)):
                            c0 = which * DFF + f * 128
                            for kk in range(8):
                                op("pe", lambda kk=kk, pb=pb, c0=c0: PE.matmul(PS[pb][:, 0:256], lhsT=w1[:, kk, c0:c0 + 128],
                                                                               rhs=hT[:, kk, :], start=(kk == 0), stop=(kk == 7)),
                                   r=[("w1", which, f // 2), "hT"], w=[pk[pb]])
                        sgi = sg[f % 2]
                        op("act", lambda pg=pg, sgi=sgi: A.activation(out=sgi[:], in_=PS[pg][:, 0:256], func=AF.Silu),
                           r=[pk[pg]], w=[("sg", f % 2)])
                        op("dve", lambda pu=pu, sgi=sgi, f=f: V.tensor_tensor(out=actT[:, f, :], in0=PS[pu][:, 0:256], in1=sgi[:],
                                                                              op=ALU.mult), r=[pk[pu], ("sg", f % 2)], w=[("actT", f)])
                    if ti + 1 < len(tl):
                        hA = lnA(ti + 1)
                    for tb in range(2):
                        for dh in range(2):
                            po = n % 2
                            tm = tmp[n % 2]
                            n += 1
                            for f in range(NF):
                                op("pe", lambda f=f, po=po, tb=tb, dh=dh: PE.matmul(
                                    PS[po][:, :], lhsT=actT[:, f, tb * 128:(tb + 1) * 128], rhs=w2[:, f, dh * 512:(dh + 1) * 512],
                                    start=(f == 0), stop=(f == NF - 1)), r=[("actT", f), ("w2", f)], w=[pk[po]])
                            op("dve", lambda po=po, tm=tm, dh=dh, v=v: V.tensor_tensor(
                                out=tm[:], in0=PS[po][:, :], in1=gb[v][:, dh * 512:(dh + 1) * 512], op=ALU.mult),
                               r=[pk[po], ("gb", v)], w=[("tmp", po)])
                            op("pool", lambda tm=tm, tb=tb, dh=dh, xt=xt: PO.tensor_tensor(
                                out=xt[:, tb, dh * 512:(dh + 1) * 512], in0=tm[:], in1=xt[:, tb, dh * 512:(dh + 1) * 512], op=ALU.add),
                               r=[("tmp", po), xk], w=[xk])
                    dma("sp", dst.rearrange("(b p) d -> p b d", p=128), xt[:, :, :], r=[xk])
                k.barrier()

        def lru_phase():
            l, s = 0, 1
            cw = lambda j, ch: cols[:, C_LCW + j * 8 + ch:C_LCW + j * 8 + ch + 1]
            cb = lambda ch: cols[:, C_LCB + ch:C_LCB + ch + 1]
            bg = lambda d, g, ch: cols[:, C_LBG + (d * 2 + g) * 8 + ch:C_LBG + (d * 2 + g) * 8 + ch + 1]
            tl = tiles_for(False, True)
            toff = lambda ti: 0 if ti == 0 else CTX + (ti - 1) * 256
            poff = lambda ti: CTX0 if ti == 0 else LAT0 + (ti - 1) * 256
            with contextlib.ExitStack() as ph:
                wi = sb(ph, "lwi", [128, 8, 2048], BF16)
                xts = [sb(ph, f"xt{i}", [128, 2, D]) for i in range(2)]
                hT = sb(ph, "hT", [128, 8, 256], BF16)
                pts = [sb(ph, f"pt{i}", [128, 16, 256]) for i in range(2)]
                ln_T = make_ln(ph)
                for kk in range(8):
                    dma("pool", wi[:, kk, :], lru_w_in[kk * 128:(kk + 1) * 128, :], w=[("lwi", kk)])
                for ti, (v, src, dst) in enumerate(tl):
                    xt = xts[ti % 2]
                    xk = ("xt", ti % 2)
                    pt = pts[ti % 2]
                    ptk = ("pt", ti % 2)
                    dma("sp", xt[:, :, :], src.rearrange("(b p) d -> p b d", p=128), w=[xk])
                    for b in range(2):
                        ln_T(xt[:, b, :], xk, gs_ap(l, s, v), sh_ap(l, s, v),
                             lambda j, b=b: hT[:, j, b * 128:(b + 1) * 128], "hT", (0, 1))
                    for ch in range(16):
                        pb = 2 + ch % 4
                        for kk in range(8):
                            op("pe", lambda kk=kk, pb=pb, ch=ch: PE.matmul(PS[pb][:, 0:256], lhsT=wi[:, kk, ch * 128:(ch + 1) * 128],
                                                                           rhs=hT[:, kk, :], start=(kk == 0), stop=(kk == 7)),
                               r=[("lwi", kk), "hT"], w=[pk[pb]])
                        if ch % 2 == 0:
                            op("act", lambda pb=pb, ch=ch, pt=pt: A.copy(out=pt[:, ch, :], in_=PS[pb][:, 0:256]), r=[pk[pb]], w=[ptk])
                        else:
                            op("dve", lambda pb=pb, ch=ch, pt=pt: V.tensor_copy(out=pt[:, ch, :], in_=PS[pb][:, 0:256]), r=[pk[pb]], w=[ptk])
                    dma("sp", PT[:, :, poff(ti):poff(ti) + 256].rearrange("c p t -> p c t"), pt[:, :, :], r=[ptk])
                k.barrier()
            with contextlib.ExitStack() as ph:
                wg = sb(ph, "wg", [128, 32, 256], BF16)
                coef = sb(ph, "coef", [128, 16])
                coef2 = sb(ph, "coef2", [128, 16])
                ee = sb(ph, "ee", [128, 16])
                pp = sb(ph, "pp", [128, 16])
                upre = sb(ph, "upre", [128, 2, TP])
                u = sb(ph, "u", [128, 2, T])
                ub = sb(ph, "ub", [128, 2, T], BF16)
                hsum = sb(ph, "hsum", [128, 2, T])
                tr = [[sb(ph, f"lt{n}_{i}", [128, 256]) for i in range(5)] for n in range(4)]
                sts = [[sb(ph, f"lst{d}{oc}", [128, 1]) for oc in range(2)] for d in range(2)]
                for q in range(16):
                    dma("pool", wg[:, q * 2:(q + 1) * 2, :], lru_w_gate[q].rearrange("(kc p) j -> p kc j", p=128), w=[("wg", q)])
                op("act", lambda: A.activation(out=ee[:], in_=cols[:, C_LAM:C_LAM + 16], func=AF.Exp, scale=-1.0), r=["cols"], w=["ee"])
                op("dve", lambda: V.memset(pp[:], 1.0 / 12), w=["pp"])
                for n in range(11, 0, -1):
                    op("dve", lambda: V.tensor_tensor(out=pp[:], in0=pp[:], in1=ee[:], op=ALU.mult), r=["pp", "ee"], w=["pp"])
                    op("dve", lambda n=n: V.tensor_scalar(out=pp[:], in0=pp[:], scalar1=-1.0, scalar2=1.0 / n, op0=ALU.mult, op1=ALU.add),
                       r=["pp"], w=["pp"])
                op("dve", lambda: V.scalar_tensor_tensor(out=coef[:], in0=pp[:], scalar=-8.0, in1=ee[:], op0=ALU.mult, op1=ALU.mult),
                   r=["pp", "ee"], w=["coef"])
                op("dve", lambda: V.tensor_scalar(out=coef2[:], in0=coef[:], scalar1=2.0, scalar2=None, op0=ALU.mult), r=["coef"], w=["coef2"])
                for hd in range(4):
                    dma("sp", upre[:, :, :], PT[8 + 2 * hd:8 + 2 * hd + 2, :, :].rearrange("c p t -> p c t"), w=["upre"])
                    for (a, b) in ((0, 2), (CTX0 + CTX, LAT0), (TP - 1, TP)):
                        op("pool", lambda a=a, b=b: PO.memset(upre[:, :, a:b], 0.0), r=["upre"], w=["upre"])
                    for oc in range(2):
                        ch = 2 * hd + oc
                        for (o0, n, base) in ((0, CTX, CTX0), (CTX, SEQ, LAT0)):
                            op("dve", lambda oc=oc, ch=ch, o0=o0, n=n, base=base: V.tensor_scalar(
                                out=u[:, oc, o0:o0 + n], in0=upre[:, oc, base - 2:base - 2 + n], scalar1=cw(0, ch), scalar2=cb(ch),
                                op0=ALU.mult, op1=ALU.add), r=["upre", "cols"], w=[("u", oc)])
                            for j in range(1, 4):
                                op("dve", lambda oc=oc, ch=ch, o0=o0, n=n, base=base, j=j: V.scalar_tensor_tensor(
                                    out=u[:, oc, o0:o0 + n], in0=upre[:, oc, base - 2 + j:base - 2 + j + n], scalar=cw(j, ch),
                                    in1=u[:, oc, o0:o0 + n], op0=ALU.mult, op1=ALU.add), r=["upre", "cols", ("u", oc)], w=[("u", oc)])
                        op("act", lambda oc=oc: A.copy(out=ub[:, oc, :], in_=u[:, oc, :]), r=[("u", oc)], w=["ub"])
                    for oc in range(2):
                        op("pool", lambda oc=oc: PO.memset(hsum[:, oc, :], 0.0), w=[("hsum", oc)])
                    for d in range(2):
                        for oc in range(2):
                            op("dve", lambda d=d, oc=oc: V.memset(sts[d][oc][:], 0.0), w=[("lst", d, oc)])
                    order = [list(range(17)), [0] + list(range(16, 0, -1))]
                    for step in range(17):
                        chains = [(d, oc) for d in range(2) for oc in range(2)]
                        for ci, (d, oc) in enumerate(chains):
                            ti = order[d][step]
                            t0 = toff(ti)
                            ch = 2 * hd + oc
                            R_, I_ = tr[ci][0], tr[ci][1]
                            pr, pi = 2 * ci, 2 * ci + 1
                            for g, pb in ((0, pr), (1, pi)):
                                for kc in range(2):
                                    op("pe", lambda g=g, pb=pb, kc=kc, d=d, oc=oc, t0=t0: PE.matmul(
                                        PS[pb][:, 0:256], lhsT=wg[:, ((d * 2 + g) * 4 + hd) * 2 + kc, oc * 128:(oc + 1) * 128],
                                        rhs=ub[:, kc, t0:t0 + 256], start=(kc == 0), stop=(kc == 1)),
                                       r=[("wg", (d * 2 + g) * 4 + hd), "ub"], w=[pk[pb]])
                            op("act", lambda: A.activation(out=R_[:], in_=PS[pr][:, 0:256], func=AF.Sigmoid, bias=bg(d, 0, ch)),
                               r=[pk[pr], "cols"], w=[("ltR", ci)])
                            op("act", lambda: A.activation(out=I_[:], in_=PS[pi][:, 0:256], func=AF.Sigmoid, bias=bg(d, 1, ch)),
                               r=[pk[pi], "cols"], w=[("ltI", ci)])
                        for ci, (d, oc) in enumerate(chains):
                            ch = 2 * hd + oc
                            cidx = d * 8 + ch
                            R_, A_, SQ_ = tr[ci][0], tr[ci][2], tr[ci][3]
                            op("act", lambda: A.activation(out=A_[:], in_=R_[:], func=AF.Exp, scale=coef[:, cidx:cidx + 1]),
                               r=[("ltR", ci), "coef"], w=[("ltA", ci)])
                            op("pool", lambda: PO.tensor_tensor(out=SQ_[:], in0=A_[:], in1=A_[:], op=ALU.mult),
                               r=[("ltA", ci)], w=[("ltS", ci)])
                            op("act", lambda: A.activation(out=SQ_[:], in_=SQ_[:], func=AF.Ln, scale=-1.0, bias=1.0),
                               r=[("ltS", ci)], w=[("ltS", ci)])
                            op("act", lambda: A.activation(out=SQ_[:], in_=SQ_[:], func=AF.Exp, scale=0.5),
                               r=[("ltS", ci)], w=[("ltS", ci)])
                        for ci, (d, oc) in enumerate(chains):
                            ti = order[d][step]
                            t0 = toff(ti)
                            I_, A_, SQ_, HB_ = tr[ci][1], tr[ci][2], tr[ci][3], tr[ci][4]
                            stt = sts[d][oc]
                            sk = ("lst", d, oc)
                            hk = ("hsum", oc)
                            op("dve", lambda: V.tensor_tensor(out=I_[:], in0=I_[:], in1=u[:, oc, t0:t0 + 256], op=ALU.mult),
                               r=[("ltI", ci), ("u", oc)], w=[("ltI", ci)])
                            op("dve", lambda: V.tensor_tensor(out=I_[:], in0=I_[:], in1=SQ_[:], op=ALU.mult),
                               r=[("ltI", ci), ("ltS", ci)], w=[("ltI", ci)])
                            if d == 0:
                                op("dve", lambda: V.tensor_tensor_scan(out=HB_[:], data0=A_[:], data1=I_[:],
                                                                       initial=stt[:, 0:1], op0=ALU.mult, op1=ALU.add),
                                   r=[("ltA", ci), ("ltI", ci), sk], w=[("ltH", ci)])
                                op("dve", lambda: V.tensor_copy(out=stt[:, 0:1], in_=HB_[:, 255:256]), r=[("ltH", ci)], w=[sk])
                            else:
                                op("dve", lambda: V.tensor_tensor_scan(out=HB_[:, ::-1], data0=A_[:, ::-1], data1=I_[:, ::-1],
                                                                       initial=stt[:, 0:1], op0=ALU.mult, op1=ALU.add),
                                   r=[("ltA", ci), ("ltI", ci), sk], w=[("ltH", ci)])
                                op("dve", lambda: V.tensor_copy(out=stt[:, 0:1], in_=HB_[:, 0:1]), r=[("ltH", ci)], w=[sk])
                            op("pool", lambda: PO.tensor_tensor(out=hsum[:, oc, t0:t0 + 256], in0=hsum[:, oc, t0:t0 + 256],
                                                                in1=HB_[:], op=ALU.add), r=[("ltH", ci), hk], w=[hk])
                    dma("sp", upre[:, :, 0:CTX], PT[2 * hd:2 * hd + 2, :, CTX0:CTX0 + CTX].rearrange("c p t -> p c t"), w=["upre"])
                    dma("sp", upre[:, :, CTX:T], PT[2 * hd:2 * hd + 2, :, LAT0:LAT0 + SEQ].rearrange("c p t -> p c t"), w=["upre"])
                    for oc in range(2):
                        op("act", lambda oc=oc: A.activation(out=upre[:, oc, 0:T], in_=upre[:, oc, 0:T], func=AF.Gelu_apprx_tanh),
                           r=["upre"], w=["upre"])
                        op("dve", lambda oc=oc: V.tensor_tensor(out=ub[:, oc, :], in0=upre[:, oc, 0:T], in1=hsum[:, oc, :], op=ALU.mult),
                           r=["upre", ("hsum", oc)], w=["ub"])
                    dma("sp", ZT[2 * hd:2 * hd + 2, :, :].rearrange("c p t -> p c t"), ub[:, :, :], r=["ub"])
                k.barrier()
            with contextlib.ExitStack() as ph:
                wo = sb(ph, "lwo", [128, 8, D], BF16)
                gb = [sb(ph, f"gb{i}", [128, D]) for i in range(2)]
                dg = sb(ph, "dg", [128, 128])
                xts = [sb(ph, f"xt{i}", [128, 2, D]) for i in range(2)]
                zts = [sb(ph, f"zt{i}", [128, 8, 256], BF16) for i in range(2)]
                tmp = [sb(ph, f"tmp{i}", [128, 512]) for i in range(2)]
                for c in range(8):
                    dma("pool", wo[:, c, :], lru_w_out[c * 128:(c + 1) * 128, :], w=[("lwo", c)])
                for v in range(2):
                    bcast_row(gb[v], ("gb", v), gate_ap(l, s, v), 1.0, dg)
                n = 0
                for ti, (v, src, dst) in enumerate(tl):
                    xt, xk = xts[ti % 2], ("xt", ti % 2)
                    zt, zk = zts[ti % 2], ("zt", ti % 2)
                    dma("sp", xt[:, :, :], src.rearrange("(b p) d -> p b d", p=128), w=[xk])
                    dma("sp", zt[:, :, :], ZT[:, :, toff(ti):toff(ti) + 256].rearrange("c p t -> p c t"), w=[zk])
                    for tb in range(2):
                        for dh in range(2):
                            po = n % 2
                            tm = tmp[n % 2]
                            n += 1
                            for c in range(8):
                                op("pe", lambda c=c, po=po, tb=tb, dh=dh, zt=zt: PE.matmul(
                                    PS[po][:, :], lhsT=zt[:, c, tb * 128:(tb + 1) * 128], rhs=wo[:, c, dh * 512:(dh + 1) * 512],
                                    start=(c == 0), stop=(c == 7)), r=[zk, ("lwo", c)], w=[pk[po]])
                            op("dve", lambda po=po, tm=tm, dh=dh, v=v: V.tensor_tensor(
                                out=tm[:], in0=PS[po][:, :], in1=gb[v][:, dh * 512:(dh + 1) * 512], op=ALU.mult),
                               r=[pk[po], ("gb", v)], w=[("tmp", po)])
                            op("pool", lambda tm=tm, tb=tb, dh=dh, xt=xt: PO.tensor_tensor(
                                out=xt[:, tb, dh * 512:(dh + 1) * 512], in0=tm[:], in1=xt[:, tb, dh * 512:(dh + 1) * 512], op=ALU.add),
                               r=[("tmp", po), xk], w=[xk])
                    dma("sp", dst.rearrange("(b p) d -> p b d", p=128), xt[:, :, :], r=[xk])
                k.barrier()

        def dn_phase():
            l, s = 1, 1
            toffs = [0] + [CTX + i * 256 for i in range(16)]
            poffs = [CTX0] + [LAT0 + i * 256 for i in range(16)]
            with contextlib.ExitStack() as ph:
                wi = sb(ph, "dwi", [128, 8, 4128], BF16)
                xts = [sb(ph, f"xt{i}", [128, 2, D]) for i in range(2)]
                hT = sb(ph, "hT", [128, 8, 256], BF16)
                pqs = [sb(ph, f"pq{i}", [128, 24, 256], BF16) for i in range(2)]
                pzs = [sb(ph, f"pz{i}", [128, 8, 256]) for i in range(2)]
                gbt = [sb(ph, f"gbt{i}", [128, 2, 32]) for i in range(2)]
                t1 = [sb(ph, f"t1{i}", [128, 16]) for i in range(2)]
                negA = sb(ph, "negA", [128, 16])
                ln_T = make_ln(ph)
                for kk in range(8):
                    dma("pool", wi[:, kk, :], dn_w_in[kk * 128:(kk + 1) * 128, :], w=[("dwi", kk)])
                op("act", lambda: A.activation(out=negA[:], in_=rowsb[:, 0:16], func=AF.Exp), r=["rowsb"], w=["negA"])
                op("dve", lambda: V.tensor_scalar(out=negA[:], in0=negA[:], scalar1=-1.0, scalar2=None, op0=ALU.mult), r=["negA"], w=["negA"])
                nb = 0

                def d1_load(ti):
                    xt, xk = xts[ti % 2], ("xt", ti % 2)
                    if ti == 0:
                        dma("sp", xt[:, :, :], X[0:CTX, :].rearrange("(b p) d -> p b d", p=128), w=[xk])
                    else:
                        for b in range(2):
                            for cl in range(2):
                                col = 4 * (ti - 1) + 2 * b + cl
                                dma("sp", xt[cl * 64:(cl + 1) * 64, b, :], Xlat_cm[col], w=[xk])

                def d1_lnA(ti):
                    return [ln_T.A(xts[ti % 2][:, b, :], ("xt", ti % 2)) for b in range(2)]

                d1_load(0)
                hA = d1_lnA(0)
                for ti in range(17):
                    v = 1 if ti == 0 else 0
                    xt, xk = xts[ti % 2], ("xt", ti % 2)
                    pz, pzk = pzs[ti % 2], ("pz", ti % 2)
                    pq, pqk = pqs[ti % 2], ("pq", ti % 2)
                    if ti + 1 < 17:
                        d1_load(ti + 1)
                    for b in range(2):
                        ln_T.B(hA[b], gs_ap(l, s, v), sh_ap(l, s, v), lambda j, b=b: hT[:, j, b * 128:(b + 1) * 128], "hT", (0, 1))
                    for ch in range(32):
                        pb = 2 + ch % 4
                        for kk in range(8):
                            op("pe", lambda kk=kk, pb=pb, ch=ch: PE.matmul(PS[pb][:, 0:256], lhsT=wi[:, kk, ch * 128:(ch + 1) * 128],
                                                                           rhs=hT[:, kk, :], start=(kk == 0), stop=(kk == 7)),
                               r=[("dwi", kk), "hT"], w=[pk[pb]])
                        if ch >= 24:
                            op("act", lambda pb=pb, ch=ch, pz=pz: A.activation(out=pz[:, ch - 24, :], in_=PS[pb][:, 0:256], func=AF.Silu),
                               r=[pk[pb]], w=[pzk])
                        elif ch % 2 == 0:
                            op("act", lambda pb=pb, ch=ch, pq=pq: A.copy(out=pq[:, ch, :], in_=PS[pb][:, 0:256]), r=[pk[pb]], w=[pqk])
                        else:
                            op("dve", lambda pb=pb, ch=ch, pq=pq: V.tensor_copy(out=pq[:, ch, :], in_=PS[pb][:, 0:256]), r=[pk[pb]], w=[pqk])
                    for b in range(2):
                        pb = 6 + b
                        g_, t_ = gbt[nb % 2], t1[nb % 2]
                        gk, tk = ("gbt", nb % 2), ("t1", nb % 2)
                        nb += 1
                        for kk in range(8):
                            op("pe", lambda kk=kk, pb=pb, b=b: PE.matmul(PS[pb][:, 0:32], lhsT=hT[:, kk, b * 128:(b + 1) * 128],
                                                                         rhs=wi[:, kk, 4096:4128], start=(kk == 0), stop=(kk == 7)),
                               r=[("dwi", kk), "hT"], w=[pk[pb]])
                        op("dve", lambda: V.tensor_tensor(out=t_[:], in0=PS[pb][:, 0:16], in1=rowsb[:, 16:32], op=ALU.add),
                           r=[pk[pb], "rowsb"], w=[tk])
                        op("act", lambda: A.activation(out=t_[:], in_=t_[:], func=AF.Exp), r=[tk], w=[tk])
                        op("act", lambda: A.activation(out=t_[:], in_=t_[:], func=AF.Ln, bias=1.0), r=[tk], w=[tk])
                        op("dve", lambda: V.tensor_tensor(out=g_[:, 0, 0:16], in0=t_[:], in1=negA[:], op=ALU.mult), r=[tk, "negA"], w=[gk])
                        op("act", lambda: A.activation(out=g_[:, 0, 16:32], in_=PS[pb][:, 16:32], func=AF.Sigmoid), r=[pk[pb]], w=[gk])
                        r0 = toffs[ti] + b * 128
                        dma("sp", GD[r0:r0 + 128, :], g_[:, 0, 0:16], r=[gk])
                        dma("sp", BD[r0:r0 + 128, :], g_[:, 0, 16:32], r=[gk])
                    if ti + 1 < 17:
                        hA = d1_lnA(ti + 1)
                    dma("sp", QP[:, :, poffs[ti]:poffs[ti] + 256].rearrange("c p t -> p c t"), pq[:, :, :], r=[pqk])
                    dma("sp", ZD[:, :, toffs[ti]:toffs[ti] + 256].rearrange("c p t -> p c t"), pz[:, :, :], r=[pzk])
                k.barrier()
            with contextlib.ExitStack() as ph:
                pres = [sb(ph, f"pre{i}", [128, TP], BF16) for i in range(2)]
                dws = [sb(ph, f"dw{i}", [128, 4, 128], BF16) for i in range(2)]
                vals = [sb(ph, f"val{i}", [128, T]) for i in range(2)]
                toks = [sb(ph, f"tok{i}", [128, 34, 128]) for i in range(2)]
                sqs = [sb(ph, f"sq{i}", [128, 2048]) for i in range(2)]
                rins = [sb(ph, f"rin{i}", [128, 2048]) for i in range(2)]
                nt = 0
                for c in range(24):
                    pre, prk = pres[c % 2], ("pre", c % 2)
                    val, vk = vals[c % 2], ("val", c % 2)
                    dma("sp", pre[:, :], QP[c], w=[prk])
                    for (a, b) in ((0, 2), (CTX0 + CTX, LAT0), (TP - 1, TP)):
                        op("pool", lambda a=a, b=b, pre=pre: PO.memset(pre[:, a:b], 0.0), r=[prk], w=[prk])
                    cw = lambda j: cols[:, C_DCW + j * 24 + c:C_DCW + j * 24 + c + 1]
                    dw, dwk = dws[c % 2], ("dw", c % 2)
                    for j in range(4):
                        op("dve", lambda j=j: V.tensor_scalar(out=dw[:, j, :], in0=ident[:], scalar1=cw(j), scalar2=None, op0=ALU.mult),
                           r=["ident", "cols"], w=[dwk])
                    for (o0, n, base) in ((0, CTX, CTX0), (CTX, SEQ, LAT0)):
                        for a0 in range(0, n, 512):
                            nn = min(512, n - a0)
                            pb = 4 + (nt % 4)
                            nt += 1
                            for j in range(4):
                                op("pe", lambda j=j, pb=pb, a0=a0, nn=nn, base=base: PE.matmul(
                                    PS[pb][:, 0:nn], lhsT=dw[:, j, :], rhs=pre[:, base - 2 + j + a0:base - 2 + j + a0 + nn],
                                    start=(j == 0), stop=(j == 3)), r=[dwk, prk], w=[pk[pb]])
                            op("act", lambda pb=pb, a0=a0, nn=nn, o0=o0: A.activation(out=val[:, o0 + a0:o0 + a0 + nn], in_=PS[pb][:, 0:nn],
                                                                                    func=AF.Silu), r=[pk[pb]], w=[vk])
                    if c < 16:
                        tls = [(a, min(512, T - a)) for a in range(0, T, 512)]
                        for g0 in range(0, len(tls), 4):
                            grp = tls[g0:g0 + 4]
                            sq, sk = sqs[(g0 // 4) % 2], ("sq", (g0 // 4) % 2)
                            rin, rk = rins[(g0 // 4) % 2], ("rin", (g0 // 4) % 2)
                            for gi, (a, n) in enumerate(grp):
                                op("act", lambda gi=gi, a=a, n=n: A.activation(out=sq[:, gi * 512:gi * 512 + n], in_=val[:, a:a + n],
                                                                              func=AF.Square), r=[vk], w=[sk])
                            for gi, (a, n) in enumerate(grp):
                                op("pe", lambda gi=gi, n=n: PE.matmul(PS[gi][:, 0:n], lhsT=ones[:, :], rhs=sq[:, gi * 512:gi * 512 + n],
                                                                      start=True, stop=True), r=[sk, "ones"], w=[pk[gi]])
                            for gi, (a, n) in enumerate(grp):
                                op("act", lambda gi=gi, n=n: A.activation(out=rin[:, gi * 512:gi * 512 + n], in_=PS[gi][:, 0:n],
                                                                          func=AF.Ln, bias=EPS), r=[pk[gi]], w=[rk])
                            n_tot = (len(grp) - 1) * 512 + grp[-1][1]
                            op("act", lambda n_tot=n_tot: A.activation(out=rin[:, 0:n_tot], in_=rin[:, 0:n_tot], func=AF.Exp, scale=-0.5,
                                                                       bias=(math.log(128.0 ** -0.5) if c < 8 else 0.0)), r=[rk], w=[rk])
                            a0 = grp[0][0]
                            op("dve", lambda a0=a0, n_tot=n_tot: V.tensor_tensor(out=val[:, a0:a0 + n_tot], in0=val[:, a0:a0 + n_tot],
                                                                                in1=rin[:, 0:n_tot], op=ALU.mult), r=[vk, rk], w=[vk])
                    if c < 8:
                        dma("sp", QT[c], val[:, :], r=[vk])
                    elif c < 16:
                        dma("sp", KT[c - 8], val[:, :], r=[vk])
                    if c >= 8:
                        tok, tkk = toks[c % 2], ("tok", c % 2)
                        for b0 in range(0, 34, 4):
                            nbk = min(4, 34 - b0)
                            pb = 4 + (b0 // 4) % 4
                            for bb in range(nbk):
                                blk = b0 + bb
                                op("pe", lambda bb=bb, blk=blk, pb=pb: PE.transpose(out=PS[pb][:, bb * 128:(bb + 1) * 128],
                                                                                     in_=val[:, blk * 128:(blk + 1) * 128], identity=ident[:]),
                                   r=[vk, "ident"], w=[pk[pb]])
                            src_ap = PS[pb][:, 0:nbk * 128].rearrange("p (b f) -> p b f", f=128)
                            if (b0 // 4) % 2 == 0:
                                op("act", lambda: A.copy(out=tok[:, b0:b0 + nbk, :], in_=src_ap), r=[pk[pb]], w=[tkk])
                            else:
                                op("dve", lambda: V.tensor_copy(out=tok[:, b0:b0 + nbk, :], in_=src_ap), r=[pk[pb]], w=[tkk])
                        dst = (KTOK[c - 8] if c < 16 else VTOK[c - 16]).rearrange("(b p) f -> p b f", p=128)
                        dma("sp", dst, tok[:, :, :], r=[tkk])
                k.barrier()
            with contextlib.ExitStack() as ph:
                W3 = [64, 8, 64]
                offd_t = sb(ph, "offd", [128, 64])
                mi_t = sb(ph, "mi", [128, 64]); ntm_t = sb(ph, "ntm", [128, 64]); ncm_t = sb(ph, "ncm", [128, 64])
                MI, NT_, NC_, OFFD, I64 = [], [], [], [], []
                for d in range(2):
                    po = d * 64
                    v64 = iot[po:po + 64, po:po + 64]
                    mi, ntm, ncm, offd = mi_t[po:po + 64, :], ntm_t[po:po + 64, :], ncm_t[po:po + 64, :], offd_t[po:po + 64, :]
                    op("dve", lambda: V.tensor_scalar(out=offd, in0=v64, scalar1=0.0, scalar2=None, op0=ALU.not_equal), r=["iot"], w=["msk"])
                    op("dve", lambda: V.tensor_scalar(out=mi, in0=v64, scalar1=0.0, scalar2=None,
                                                      op0=(ALU.is_ge if d == 0 else ALU.is_le)), r=["iot"], w=["msk"])
                    op("dve", lambda: V.tensor_scalar(out=ntm, in0=mi, scalar1=-1.0, scalar2=-NEG, op0=ALU.add, op1=ALU.mult),
                       r=["msk"], w=["msk"])
                    op("dve", lambda: V.tensor_scalar(out=ncm, in0=v64, scalar1=0.0, scalar2=None,
                                                      op0=(ALU.is_lt if d == 0 else ALU.is_gt)), r=["iot"], w=["msk"])
                    op("dve", lambda: V.tensor_scalar(out=ncm, in0=ncm, scalar1=-1.0, scalar2=-NEG, op0=ALU.add, op1=ALU.mult),
                       r=["msk"], w=["msk"])
                    MI.append(mi); NT_.append(ntm); NC_.append(ncm); OFFD.append(offd); I64.append(ident[po:po + 64, po:po + 64])
                bcm = lambda m: m.unsqueeze(1).broadcast_to(W3)
                bc = lambda ap2, n: ap2.unsqueeze(2).broadcast_to([ap2.shape[0], 8, n])
                TL = [[None, None], [None, None]]
                for par in range(2):
                    big = {}
                    for nm in ("ktok", "vtok", "kbg", "vb", "kd", "vn"):
                        big[nm] = sb(ph, f"{nm}B{par}", [128, 8, 128])
                    for nm in ("gM", "bdg", "E1", "E2", "DTi", "DCs", "DTs", "Q0", "P0", "Q1", "P1", "QKm", "R", "Rr", "tmpw"):
                        big[nm] = sb(ph, f"{nm}B{par}", [128, 8, 64])
                    for nm in ("g16", "b16"):
                        big[nm] = sb(ph, f"{nm}B{par}", [128, 16])
                    for nm in ("gcc", "egc", "bco"):
                        big[nm] = sb(ph, f"{nm}B{par}", [128, 8])
                    for d in range(2):
                        t = {}
                        for nm in ("kT", "qT", "qdT", "wTn", "osb", "egr"):
                            t[nm] = sb(ph, f"{nm}{d}{par}", [128, 8, 64])
                        for nm in big:
                            t[nm] = big[nm][d * 64:(d + 1) * 64]
                        TL[d][par] = t
                for d in range(2):
                    S_ = sb(ph, f"S{d}", [128, 8, 128])
                    TL[d][0]["S"] = S_
                    TL[d][1]["S"] = S_
                    t = TL[d][0]
                    op("pool", lambda t=t: PO.memset(t["egr"][:, :, :], 0.0), w=[("egr", d, 0)])
                    for h in range(8):
                        op("dve", lambda t=t, h=h: V.tensor_copy(out=t["S"][:, h, :].bitcast(F32R), in_=t["egr"][:, 0:2, :].rearrange("p a b -> p (a b)")),
                           r=[("egr", d, 0)], w=[("S", d, h)])
                w3 = lambda ps_, po: ps_[po:po + 64, :].rearrange("p (h j) -> p h j", h=8)
                fl = lambda tl: tl[:, :, :].rearrange("p h j -> p (h j)")

                def dn_A(d, t0, is_lat, par):
                    t = TL[d][par]
                    po = d * 64
                    i64, offd = I64[d], OFFD[d]
                    rr = lambda ap: ap.bitcast(F32R)
                    rq = rr if d == 0 else (lambda ap: ap)
                    K_ = lambda nm: (nm, d, par)
                    T1a, T1b, T2a, T2b = PS[d * 4], PS[d * 4 + 1], PS[d * 4 + 2], PS[d * 4 + 3]
                    k1a, k1b, k2a, k2b = pk[d * 4], pk[d * 4 + 1], pk[d * 4 + 2], pk[d * 4 + 3]
                    last = 63 if d == 0 else 0
                    gd = t["g16"][:, d * 8:(d + 1) * 8]
                    bd = t["b16"][:, d * 8:(d + 1) * 8]
                    dma("sp", t["kT"][:, :, :], KT[:, :, t0:t0 + 64].rearrange("h p t -> p h t"), w=[K_("kT")])
                    dma("sp", t["qT"][:, :, :], QT[:, :, t0:t0 + 64].rearrange("h p t -> p h t"), w=[K_("qT")])
                    dma("sp", t["ktok"][:, :, :], KTOK[:, t0:t0 + 64, :].rearrange("h t f -> t h f"), w=[K_("ktok")])
                    dma("sp", t["vtok"][:, :, :], VTOK[:, t0:t0 + 64, :].rearrange("h t f -> t h f"), w=[K_("vtok")])
                    dma("sp", t["g16"][:, :], GD[t0:t0 + 64, :], w=[K_("g16")])
                    dma("sp", t["b16"][:, :], BD[t0:t0 + 64, :], w=[K_("b16")])
                    yield
                    op("dve", lambda: V.tensor_tensor(out=t["gM"][:, :, :], in0=bcm(MI[d]), in1=bc(gd, 64), op=ALU.mult),
                       r=["msk", K_("g16")], w=[K_("gM")])
                    op("pe", lambda: PE.matmul(T1a[:, :], lhsT=ones[po:po + 64, :], rhs=fl(t["gM"]), start=True, stop=True),
                       r=["ones", K_("gM")], w=[k1a])
                    op("pe", lambda: PE.matmul(T2b[po:po + 64, 0:8], lhsT=MI[d], rhs=gd, start=True, stop=True),
                       r=["msk", K_("g16")], w=[k2b])
                    op("dve", lambda: V.tensor_tensor(out=t["bdg"][:, :, :], in0=bcm(i64), in1=bc(bd, 64), op=ALU.mult),
                       r=["ident", K_("b16")], w=[K_("bdg")])
                    op("pe", lambda: PE.matmul(T2a[po:po + 64, :], lhsT=ones[po:po + 64, 0:64], rhs=fl(t["bdg"]), start=True, stop=True),
                       r=["ones", K_("bdg")], w=[k2a])
                    op("act", lambda: A.copy(out=t["gcc"][:, :], in_=T2b[po:po + 64, 0:8]), r=[k2b], w=[K_("gcc")])
                    op("dve", lambda: V.tensor_tensor(out=t["E1"][:, :, :], in0=w3(T1a, po), in1=bc(t["gcc"][:, :], 64), op=ALU.subtract),
                       r=[k1a, K_("gcc")], w=[K_("E1")])
                    op("dve", lambda: V.tensor_tensor(out=t["E2"][:, :, :], in0=bc(t["gcc"][:, :], 64), in1=w3(T1a, po), op=ALU.subtract),
                       r=[k1a, K_("gcc")], w=[K_("E2")])
                    op("dve", lambda: V.scalar_tensor_tensor(out=t["E1"][:, :, :], in0=t["E1"][:, :, :], scalar=0.0, in1=bcm(NT_[d]),
                                                             op0=ALU.min, op1=ALU.add), r=[K_("E1"), "msk"], w=[K_("E1")])
                    op("dve", lambda: V.scalar_tensor_tensor(out=t["E2"][:, :, :], in0=t["E2"][:, :, :], scalar=0.0, in1=bcm(NC_[d]),
                                                             op0=ALU.min, op1=ALU.add), r=[K_("E2"), "msk"], w=[K_("E2")])
                    op("act", lambda: A.activation(out=t["DTi"][:, :, :], in_=t["E1"][:, :, :], func=AF.Exp), r=[K_("E1")], w=[K_("DTi")])
                    op("act", lambda: A.activation(out=t["DCs"][:, :, :], in_=t["E2"][:, :, :], func=AF.Exp), r=[K_("E2")], w=[K_("DCs")])
                    op("act", lambda: A.activation(out=fl(t["egr"]), in_=T1a[:, :], func=AF.Exp), r=[k1a], w=[K_("egr")])
                    op("act", lambda: A.activation(out=t["egc"][:, :], in_=t["gcc"][:, :], func=AF.Exp), r=[K_("gcc")], w=[K_("egc")])
                    yield
                    for h in range(8):
                        op("pe", lambda h=h: PE.matmul(T2b[po:po + 64, h * 64:(h + 1) * 64], lhsT=t["kT"][:, h, :], rhs=t["kT"][:, h, :],
                                                       start=True, stop=True), r=[K_("kT")], w=[k2b])
                    for h in range(8):
                        op("pe", lambda h=h: PE.matmul(T1a[po:po + 64, h * 64:(h + 1) * 64], lhsT=t["kT"][:, h, :], rhs=t["qT"][:, h, :],
                                                       start=True, stop=True), r=[K_("kT"), K_("qT")], w=[k1a])
                    yield
                    op("pool", lambda: PO.tensor_tensor(out=t["DTs"][:, :, :], in0=t["DTi"][:, :, :], in1=bcm(offd), op=ALU.mult),
                       r=[K_("DTi"), "msk"], w=[K_("DTs")])
                    op("dve", lambda: V.tensor_tensor(out=t["tmpw"][:, :, :], in0=w3(T2b, po), in1=t["DTs"][:, :, :], op=ALU.mult),
                       r=[k2b, K_("DTs")], w=[K_("tmpw")])
                    op("dve", lambda: V.tensor_tensor(out=t["Q0"][:, :, :], in0=w3(T2a, po), in1=t["tmpw"][:, :, :], op=ALU.mult),
                       r=[k2a, K_("tmpw")], w=[K_("Q0")])
                    op("dve", lambda: V.tensor_tensor(out=t["tmpw"][:, :, :], in0=w3(T2b, po), in1=t["DCs"][:, :, :], op=ALU.mult),
                       r=[k2b, K_("DCs"), K_("tmpw")], w=[K_("tmpw")])
                    op("dve", lambda: V.tensor_tensor(out=t["P0"][:, :, :], in0=t["tmpw"][:, :, :], in1=bc(bd, 64), op=ALU.mult),
                       r=[K_("tmpw"), K_("b16")], w=[K_("P0")])
                    op("dve", lambda: V.tensor_tensor(out=rr(t["QKm"][:, :, :]), in0=w3(T1a, po), in1=t["DTi"][:, :, :], op=ALU.mult),
                       r=[k1a, K_("DTi")], w=[K_("QKm")])
                    op("dve", lambda: V.tensor_tensor(out=t["R"][:, :, :], in0=bcm(i64), in1=t["Q0"][:, :, :], op=ALU.subtract),
                       r=["ident", K_("Q0")], w=[K_("R")])
                    yield
                    Qc, Pc, Qn, Pn = "Q0", "P0", "Q1", "P1"
                    for lev in range(1, 6):
                        for h in range(8):
                            op("pe", lambda h=h: PE.matmul(T2a[po:po + 64, h * 64:(h + 1) * 64], lhsT=t[Qc][:, h, :], rhs=t[Pc][:, h, :],
                                                           start=True, stop=True), r=[K_(Qc), K_(Pc)], w=[k2a])
                        if lev < 5:
                            for h in range(8):
                                op("pe", lambda h=h: PE.matmul(T1a[po:po + 64, h * 64:(h + 1) * 64], lhsT=t[Pc][:, h, :], rhs=t[Qc][:, h, :],
                                                               start=True, stop=True), r=[K_(Qc), K_(Pc)], w=[k1a])
                        yield
                        op("act", lambda: A.copy(out=t[Pn][:, :, :], in_=w3(T2a, po)), r=[k2a], w=[K_(Pn)])
                        if lev < 5:
                            op("dve", lambda: V.tensor_copy(out=t[Qn][:, :, :], in_=w3(T1a, po)), r=[k1a], w=[K_(Qn)])
                        for h in range(8):
                            op("pe", lambda h=h: PE.matmul(T2b[po:po + 64, h * 64:(h + 1) * 64], lhsT=t[Pn][:, h, :], rhs=t["R"][:, h, :],
                                                           start=True, stop=True), r=[K_(Pn), K_("R")], w=[k2b])
                        op("dve", lambda: V.tensor_tensor(out=t["R"][:, :, :], in0=w3(T2b, po), in1=t["R"][:, :, :], op=ALU.add),
                           r=[k2b, K_("R")], w=[K_("R")])
                        Qc, Pc, Qn, Pn = Qn, Pn, Qc, Pc
                        yield

                def dn_B(d, t0, is_lat, par):
                    t = TL[d][par]
                    po = d * 64
                    i64, offd = I64[d], OFFD[d]
                    rr = lambda ap: ap.bitcast(F32R)
                    rq = rr if d == 0 else (lambda ap: ap)
                    K_ = lambda nm: (nm, d, par)
                    T1a, T1b, T2a, T2b = PS[d * 4], PS[d * 4 + 1], PS[d * 4 + 2], PS[d * 4 + 3]
                    k1a, k1b, k2a, k2b = pk[d * 4], pk[d * 4 + 1], pk[d * 4 + 2], pk[d * 4 + 3]
                    last = 63 if d == 0 else 0
                    gd = t["g16"][:, d * 8:(d + 1) * 8]
                    bd = t["b16"][:, d * 8:(d + 1) * 8]
                    op("act", lambda: A.copy(out=rr(t["Rr"][:, :, :]), in_=t["R"][:, :, :]), r=[K_("R")], w=[K_("Rr")])
                    op("dve", lambda: V.tensor_tensor(out=t["bco"][:, :], in0=bd, in1=t["egc"][:, :], op=ALU.mult),
                       r=[K_("b16"), K_("egc")], w=[K_("bco")])
                    op("dve", lambda: V.tensor_tensor(out=rr(t["kbg"][:, :, :]), in0=t["ktok"][:, :, :], in1=bc(t["bco"][:, :], 128), op=ALU.mult),
                       r=[K_("ktok"), K_("bco")], w=[K_("kbg")])
                    op("dve", lambda: V.tensor_tensor(out=rr(t["vb"][:, :, :]), in0=t["vtok"][:, :, :], in1=bc(bd, 128), op=ALU.mult),
                       r=[K_("vtok"), K_("b16")], w=[K_("vb")])
                    op("dve", lambda: V.tensor_tensor(out=rr(t["kd"][:, :, :]), in0=t["ktok"][:, :, :],
                                                        in1=bc(t["DTi"][:, :, last], 128), op=ALU.mult),
                       r=[K_("ktok"), K_("DTi")], w=[K_("kd")])
                    op("dve", lambda: V.tensor_tensor(out=rr(t["qdT"][:, :, :]), in0=t["qT"][:, :, :], in1=t["egr"][:, :, :], op=ALU.mult),
                       r=[K_("qT"), K_("egr")], w=[K_("qdT")])
                    yield
                    for h in range(8):
                        op("pe", lambda h=h: PE.matmul(T1b[:, h * 64:(h + 1) * 64], lhsT=rr(t["kbg"][:, h, :]), rhs=rr(t["Rr"][:, h, :]),
                                                       start=True, stop=True), r=[K_("kbg"), K_("Rr")], w=[k1b])
                    yield
                    op("act", lambda: A.activation(out=rr(fl(t["wTn"])), in_=T1b[:, :], func=AF.Copy, scale=-1.0), r=[k1b], w=[K_("wTn")])
                    for half in range(2):
                        for h in range(half * 4, half * 4 + 4):
                            o_ = T1b[po:po + 64, (h % 4) * 128:(h % 4 + 1) * 128]
                            op("pe", lambda h=h, o_=o_: PE.matmul(o_, lhsT=rq(t["Rr"][:, h, :]), rhs=rq(t["vb"][:, h, :]), start=True, stop=False),
                               r=[K_("Rr"), K_("vb")], w=[k1b])
                            op("pe", lambda h=h, o_=o_: PE.matmul(o_, lhsT=rq(t["wTn"][:, h, :]), rhs=rq(t["S"][:, h, :]), start=False, stop=True),
                               r=[K_("wTn"), ("S", d, h)], w=[k1b])
                        src_ = T1b[po:po + 64, :].rearrange("p (h f) -> p h f", h=4)
                        if half == 0:
                            op("act", lambda: A.copy(out=rr(t["vn"][:, 0:4, :]), in_=src_), r=[k1b], w=[K_("vn")])
                        else:
                            op("dve", lambda: V.tensor_copy(out=rr(t["vn"][:, 4:8, :]), in_=src_), r=[k1b], w=[K_("vn")])
                        yield
                    if is_lat:
                        for h in range(8):
                            o_ = T1b[:, h * 64:(h + 1) * 64]
                            op("pe", lambda h=h, o_=o_: PE.matmul(o_, lhsT=rr(t["S"][:, h, :]), rhs=rr(t["qdT"][:, h, :]), start=True, stop=False),
                               r=[("S", d, h), K_("qdT")], w=[k1b])
                            op("pe", lambda h=h, o_=o_: PE.matmul(o_, lhsT=rr(t["vn"][:, h, :]), rhs=rr(t["QKm"][:, h, :]), start=False, stop=True),
                               r=[K_("vn"), K_("QKm")], w=[k1b])
                        op("act", lambda: A.copy(out=fl(t["osb"]), in_=T1b[:, :]), r=[k1b], w=[K_("osb")])
                        s0 = t0 - CTX
                        dma("sp", OFB[d][:, :, s0:s0 + 64].rearrange("h p t -> p h t"), t["osb"][:, :, :], r=[K_("osb")])
                        yield
                    for half in range(2):
                        for h in range(half * 4, half * 4 + 4):
                            o_ = T1b[:, (h % 4) * 128:(h % 4 + 1) * 128]
                            op("pe", lambda h=h, o_=o_: PE.matmul(o_, lhsT=rr(t["kd"][:, h, :]), rhs=rr(t["vn"][:, h, :]), start=True, stop=True),
                               r=[K_("kd"), K_("vn")], w=[k1b])
                        for h in range(half * 4, half * 4 + 4):
                            o_ = T1b[:, (h % 4) * 128:(h % 4 + 1) * 128]
                            op("dve", lambda h=h, o_=o_: V.scalar_tensor_tensor(out=rr(t["S"][:, h, :]), in0=t["S"][:, h, :],
                                                                                scalar=t["egr"][:, h, last:last + 1], in1=o_,
                                                                                op0=ALU.mult, op1=ALU.add),
                               r=[("S", d, h), K_("egr"), k1b], w=[("S", d, h)])
                        yield

                fw = [(64 * j, False) for j in range(4)] + [(CTX + 64 * j, True) for j in range(64)]
                bw = [(64 * j, False) for j in range(3, -1, -1)] + [(CTX + 64 * j, True) for j in range(63, -1, -1)]
                for step in range(69):
                    gens = []
                    if step >= 1:
                        gens += [dn_B(0, *fw[step - 1], (step - 1) % 2), dn_B(1, *bw[step - 1], (step - 1) % 2)]
                    if step < 68:
                        gens += [dn_A(0, *fw[step], step % 2), dn_A(1, *bw[step], step % 2)]
                    while gens:
                        for g in list(gens):
                            try:
                                next(g)
                            except StopIteration:
                                gens.remove(g)
                k.barrier()
            with contextlib.ExitStack() as ph:
                wo = sb(ph, "dwo", [128, 8, D], BF16)
                gb0 = sb(ph, "gb0", [128, D])
                dg = sb(ph, "dg", [128, 128])
                xts = [sb(ph, f"xt{i}", [128, 2, D]) for i in range(2)]
                ofs = [sb(ph, f"of{i}", [128, 8, 256]) for i in range(2)]
                obs = [sb(ph, f"ob{i}", [128, 8, 256]) for i in range(2)]
                zss = [sb(ph, f"zs{i}", [128, 8, 256]) for i in range(2)]
                sq4 = [sb(ph, f"rsq{i}", [128, 2048]) for i in range(2)]
                rr4 = [sb(ph, f"rr{i}", [128, 2048]) for i in range(2)]
                yb4 = [sb(ph, f"yb{i}", [128, 8, 256], BF16) for i in range(2)]
                tmp = [sb(ph, f"tmp{i}", [128, 512]) for i in range(2)]
                for c in range(8):
                    dma("pool", wo[:, c, :], dn_w_out[c * 128:(c + 1) * 128, :], w=[("dwo", c)])
                bcast_row(gb0, "gb0", gate_ap(l, s, 0), 1.0, dg)
                gn = cols[:, C_GN:C_GN + 1]
                n = 0
                for ti in range(16):
                    xt, xk = xts[ti % 2], ("xt", ti % 2)
                    of, ofk = ofs[ti % 2], ("of", ti % 2)
                    ob, obk = obs[ti % 2], ("ob", ti % 2)
                    zs, zk = zss[ti % 2], ("zs", ti % 2)
                    s0 = ti * 256
                    for b in range(2):
                        for cl in range(2):
                            dma("sp", xt[cl * 64:(cl + 1) * 64, b, :], Xlat_cm[4 * ti + 2 * b + cl], w=[xk])
                    dma("sp", of[:, :, :], OFB[0][:, :, s0:s0 + 256].rearrange("h p t -> p h t"), w=[ofk])
                    dma("sp", ob[:, :, :], OFB[1][:, :, s0:s0 + 256].rearrange("h p t -> p h t"), w=[obk])
                    dma("sp", zs[:, :, :], ZD[:, :, CTX + s0:CTX + s0 + 256].rearrange("c p t -> p c t"), w=[zk])
                    sq, rr, yb = sq4[ti % 2], rr4[ti % 2], yb4[ti % 2]
                    off = of[:, :, :].rearrange("p h t -> p (h t)")
                    op("dve", lambda: V.tensor_tensor(out=off, in0=off, in1=ob[:, :, :].rearrange("p h t -> p (h t)"), op=ALU.add),
                       r=[ofk, obk], w=[ofk])
                    op("act", lambda: A.activation(out=sq[:, :], in_=off, func=AF.Square), r=[ofk], w=[("rsq", ti % 2)])
                    for q in range(4):
                        pb = 4 + q
                        op("pe", lambda q=q, pb=pb: PE.matmul(PS[pb][:, :], lhsT=ones[:, :], rhs=sq[:, q * 512:(q + 1) * 512],
                                                              start=True, stop=True), r=[("rsq", ti % 2), "ones"], w=[pk[pb]])
                        op("act", lambda q=q, pb=pb: A.activation(out=rr[:, q * 512:(q + 1) * 512], in_=PS[pb][:, :], func=AF.Ln,
                                                                  bias=EPS, scale=1.0 / 128), r=[pk[pb]], w=[("rr", ti % 2)])
                    op("act", lambda: A.activation(out=rr[:, :], in_=rr[:, :], func=AF.Exp, scale=-0.5), r=[("rr", ti % 2)], w=[("rr", ti % 2)])
                    op("dve", lambda: V.tensor_tensor(out=off, in0=off, in1=rr[:, :], op=ALU.mult), r=[ofk, ("rr", ti % 2)], w=[ofk])
                    op("dve", lambda: V.scalar_tensor_tensor(out=yb[:, :, :].rearrange("p h t -> p (h t)"), in0=off, scalar=gn,
                                                             in1=zs[:, :, :].rearrange("p h t -> p (h t)"), op0=ALU.mult, op1=ALU.mult),
                       r=[ofk, zk, "cols"], w=[("yb", ti % 2)])
                    for tb in range(2):
                        for dh in range(2):
                            po = n % 2
                            tm = tmp[n % 2]
                            n += 1
                            for c in range(8):
                                op("pe", lambda c=c, po=po, tb=tb, dh=dh: PE.matmul(
                                    PS[po][:, :], lhsT=yb[:, c, tb * 128:(tb + 1) * 128], rhs=wo[:, c, dh * 512:(dh + 1) * 512],
                                    start=(c == 0), stop=(c == 7)), r=[("yb", ti % 2), ("dwo", c)], w=[pk[po]])
                            op("dve", lambda po=po, tm=tm, dh=dh: V.tensor_tensor(
                                out=tm[:], in0=PS[po][:, :], in1=gb0[:, dh * 512:(dh + 1) * 512], op=ALU.mult),
                               r=[pk[po], "gb0"], w=[("tmp", po)])
                            op("pool", lambda tm=tm, tb=tb, dh=dh, xt=xt: PO.tensor_tensor(
                                out=xt[:, tb, dh * 512:(dh + 1) * 512], in0=tm[:], in1=xt[:, tb, dh * 512:(dh + 1) * 512], op=ALU.add),
                               r=[("tmp", po), xk], w=[xk])
                    for b in range(2):
                        for cl in range(2):
                            dma("sp", Xlat_cm[4 * ti + 2 * b + cl], xt[cl * 64:(cl + 1) * 64, b, :], r=[xk])
                k.barrier()

        def final_phase(raw):
            with contextlib.ExitStack() as ph:
                xts = [sb(ph, f"fx{i}", [128, 2, D]) for i in range(2)]
                junk = sb(ph, "fj", [128, D])
                sts = [sb(ph, f"fs{i}", [128, 4]) for i in range(2)]
                for ti in range(SEQ // 256):
                    xt = xts[ti % 2]
                    xk = ("fx", ti % 2)
                    dma("sp", xt[:, :, :], X[CTX + ti * 256:CTX + (ti + 1) * 256, :].rearrange("(b p) d -> p b d", p=128), w=[xk])
                    if not raw:
                        for b in range(2):
                            st = sts[b]
                            ks = ("fs", b)
                            op("act", lambda b=b, st=st: A.activation(out=junk[:], in_=xt[:, b, :], func=AF.Square, accum_out=st[:, 0:1]),
                               r=[xk], w=["fj", ks])
                            op("act", lambda st=st: A.activation(out=st[:, 1:2], in_=st[:, 0:1], func=AF.Ln, bias=EPS, scale=1.0 / D),
                               r=[ks], w=[ks])
                            op("act", lambda st=st: A.activation(out=st[:, 2:3], in_=st[:, 1:2], func=AF.Exp, scale=-0.5), r=[ks], w=[ks])
                            op("dve", lambda b=b, st=st: V.scalar_tensor_tensor(out=xt[:, b, :], in0=xt[:, b, :], scalar=st[:, 2:3],
                                                                               in1=rowsb[:, 32:1056], op0=ALU.mult, op1=ALU.mult),
                               r=[xk, ks, "rowsb"], w=[xk])
                    dma("sp", out[ti * 256:(ti + 1) * 256, :].rearrange("(b p) d -> p b d", p=128), xt[:, :, :], r=[xk])
                k.barrier()

        stages = []
        def ffn_first():
            ffn_phase(0, 0, 0, True, True, pre=pre_w)
            pre_stack.close()
        stages.append(ffn_first)
        stages.append(lru_phase)
        stages.append(lambda: ffn_phase(0, 2, 1, False, True))
        stages.append(lambda: ffn_phase(1, 0, 2, False, True))
        stages.append(dn_phase)
        stages.append(lambda: ffn_phase(1, 2, 3, False, False))
        from_lru = len(stages)
        nst = 0
        for f in stages:
            if nst >= stage:
                break
            f()
            nst += 1
        if stage < 1:
            pre_stack.close()
        final_phase(raw=dbg)
    return nc


def host_inputs(inputs, b):
    f = lambda a: np.ascontiguousarray(np.asarray(a, dtype=np.float32))
    cols = np.zeros((NCOLS, 128), np.float32)
    cols[C_C:C_C + 8] = f(inputs["c"])[b].reshape(8, 128)
    cols[C_CC:C_CC + 8] = f(inputs["c_ctx"]).reshape(8, 128)
    cols[C_BADA:C_BADA + 144] = f(inputs["b_ada"]).reshape(144, 128)
    cols[C_GSUB:C_GSUB + 48] = f(inputs["g_sub"]).reshape(48, 128)
    cols[C_LCW:C_LCW + 32] = f(inputs["lru_conv_w"])[0].reshape(32, 128)
    cols[C_LCB:C_LCB + 8] = f(inputs["lru_conv_b"])[0].reshape(8, 128)
    cols[C_LBG:C_LBG + 32] = f(inputs["lru_b_gate"])[0].reshape(32, 128)
    cols[C_LAM:C_LAM + 16] = f(inputs["lru_lambda"])[0].reshape(16, 128)
    cols[C_DCW:C_DCW + 96] = f(inputs["dn_conv_w"])[0].reshape(96, 128)
    cols[C_GN:C_GN + 1] = f(inputs["dn_g_norm"])[0].reshape(1, 128)
    rows = np.concatenate([f(inputs["dn_a_log"])[0].reshape(16), f(inputs["dn_dt_bias"])[0].reshape(16),
                           f(inputs["g_final"]).reshape(1024)]).reshape(1, 1056)
    return {
        "x": f(inputs["x"])[b], "ctx": f(inputs["ctx"])[b], "cols_src": cols, "rows_src": rows,
        "w_ada": f(inputs["w_ada"]), "ffn_w_in": f(inputs["ffn_w_in"]).reshape(4, D, 2 * DFF),
        "ffn_w_out": f(inputs["ffn_w_out"]).reshape(4, DFF, D), "lru_w_in": f(inputs["lru_w_in"])[0],
        "lru_w_gate": f(inputs["lru_w_gate"])[0].reshape(16, 256, 256), "lru_w_out": f(inputs["lru_w_out"])[0],
        "dn_w_in": f(inputs["dn_w_in"])[0], "dn_w_out": f(inputs["dn_w_out"])[0],
    }


def kernel(**inputs):
    nc = build()
    in_maps = [host_inputs(inputs, c % 4) for c in range(8)]
    res = run_bass_kernel_spmd(nc, in_maps, core_ids=list(range(8)))
    return np.stack([np.asarray(res.results[b]["out"], dtype=np.float32) for b in range(4)], axis=0)
```

```python
import contextlib
import math
import numpy as np
import concourse.bass as bass
import concourse.mybir as mybir
from concourse.bass_utils import run_bass_kernel_spmd

F32 = mybir.dt.float32
BF16 = mybir.dt.bfloat16
F32R = mybir.dt.float32r
AF = mybir.ActivationFunctionType
ALU = mybir.AluOpType

D = 1024
SEQ = 4096
CTX = 256
T = SEQ + CTX
DFF = 2816
NF = DFF // 128
EPS = 1e-6
CTX0 = 2
LAT0 = CTX0 + CTX + 3
TP = LAT0 + SEQ + 1
NEG = -30000.0

C_C, C_CC, C_BADA, C_GSUB, C_LCW, C_LCB, C_LBG, C_LAM, C_DCW, C_GN = 0, 8, 16, 160, 208, 240, 248, 280, 296, 392
NCOLS = 512


class K:
    def __init__(self, nc, es):
        self.nc = nc
        self.es = es
        self.eng = {"pe": nc.tensor, "dve": nc.vector, "act": nc.scalar, "pool": nc.gpsimd, "sp": nc.sync}
        self.ce = ["pe", "dve", "act", "pool"]
        self.EPOCH = 30000
        self.cnt = {e: 0 for e in self.ce}
        self.ep = {e: 0 for e in self.ce}
        self.sems = {e: [es.enter_context(nc.semaphore(f"s_{e}_0"))] for e in self.ce}
        self.KD = 8
        self.dq = ["sp", "pool"]
        self.dsem = {q: [es.enter_context(nc.semaphore(f"d_{q}_{i}")) for i in range(self.KD)] for q in self.dq}
        self.dn = {q: 0 for q in self.dq}
        self.seen = {e: {} for e in self.eng}
        self.st = {}

    def _wait(self, e, tok):
        if tok is None:
            return
        if tok[0] == "c":
            _, f, ep, idx = tok
            if f == e and e == "pe":
                return
            key = (f, ep)
            if self.seen[e].get(key, 0) >= idx:
                return
            self.eng[e].wait_ge(self.sems[f][ep], idx)
            self.seen[e][key] = idx
        else:
            _, q, si, cnt = tok
            key = ("d", q, si)
            if self.seen[e].get(key, 0) >= cnt:
                return
            self.eng[e].wait_ge(self.dsem[q][si], cnt)
            self.seen[e][key] = cnt

    def _deps(self, e, r, w):
        for k in r:
            s = self.st.get(k)
            if s:
                self._wait(e, s[0])
        for k in w:
            s = self.st.get(k)
            if s:
                self._wait(e, s[0])
                for t in s[1].values():
                    self._wait(e, t)

    def _record(self, tok, r, w, rid):
        for k in w:
            self.st[k] = [tok, {}]
        for k in r:
            s = self.st.setdefault(k, [None, {}])
            s[1][rid] = tok

    def op(self, e, fn, r=(), w=()):
        self._deps(e, r, w)
        inst = fn()
        if self.cnt[e] >= self.EPOCH:
            self.ep[e] += 1
            self.cnt[e] = 0
            self.sems[e].append(self.es.enter_context(self.nc.semaphore(f"s_{e}_{self.ep[e]}")))
        self.cnt[e] += 1
        inst.then_inc(self.sems[e][self.ep[e]], 1)
        tok = ("c", e, self.ep[e], self.cnt[e])
        self._record(tok, r, w, e)
        return tok

    def dma(self, q, out, in_, r=(), w=()):
        n = self.dn[q]
        si = n % self.KD
        cnt = 16 * (n // self.KD + 1)
        if cnt > 16:
            self._wait(q, ("d", q, si, cnt - 16))
        self._deps(q, r, w)
        self.eng[q].dma_start(out=out, in_=in_).then_inc(self.dsem[q][si], 16)
        self.dn[q] = n + 1
        tok = ("d", q, si, cnt)
        self._record(tok, r, w, ("d", q, si))
        return tok

    def barrier(self):
        toks = []
        for f in self.ce:
            if self.cnt[f] > 0:
                toks.append(("c", f, self.ep[f], self.cnt[f]))
        for q in self.dq:
            n = self.dn[q]
            for si in range(self.KD):
                uses = (n - si + self.KD - 1) // self.KD if n > si else 0
                if uses > 0:
                    toks.append(("d", q, si, 16 * uses))
        for e in self.eng:
            for t in toks:
                if t[0] == "c" and t[1] == e:
                    continue
                self._wait(e, t)
        self.st = {}


def build(stage=99, dbg=False):
    nc = bass.Bass("TRN2", target_bir_lowering=False)
    dt = lambda name, shape, dtype=F32, kind="ExternalInput": nc.dram_tensor(name, shape, dtype, kind=kind).ap()
    x_in = dt("x", [SEQ, D])
    ctx_in = dt("ctx", [CTX, D])
    cols_src = dt("cols_src", [NCOLS, 128])
    rows_src = dt("rows_src", [1, 1056])
    w_ada = dt("w_ada", [2, D, 9 * D])
    ffn_w_in = dt("ffn_w_in", [4, D, 2 * DFF])
    ffn_w_out = dt("ffn_w_out", [4, DFF, D])
    lru_w_in = dt("lru_w_in", [D, 2048])
    lru_w_gate = dt("lru_w_gate", [16, 256, 256])
    lru_w_out = dt("lru_w_out", [D, D])
    dn_w_in = dt("dn_w_in", [D, 4128])
    dn_w_out = dt("dn_w_out", [D, D])
    out = dt("out", [SEQ, D], kind="ExternalOutput")
    X = dt("Xs", [T, D], kind="Internal")
    PT = dt("PTs", [16, 128, TP], kind="Internal")
    ZT = dt("ZTs", [8, 128, T], BF16, kind="Internal")
    QP = dt("QPs", [24, 128, TP], BF16, kind="Internal")
    ZD = dt("ZDs", [8, 128, T], kind="Internal")
    GD = dt("GDs", [T, 16], kind="Internal")
    BD = dt("BDs", [T, 16], kind="Internal")
    QT = dt("QTs", [8, 128, T], kind="Internal")
    KT = dt("KTs", [8, 128, T], kind="Internal")
    KTOK = dt("KTOKs", [8, T, 128], kind="Internal")
    VTOK = dt("VTOKs", [8, T, 128], kind="Internal")
    OFB = [dt("OFs", [8, 128, SEQ], kind="Internal"), dt("OBs", [8, 128, SEQ], kind="Internal")]
    Xlat_cm = X[CTX:T, :].rearrange("(r c) d -> c r d", c=64)

    with contextlib.ExitStack() as es:
        k = K(nc, es)
        op, dma = k.op, k.dma
        V, A, PE, PO = nc.vector, nc.scalar, nc.tensor, nc.gpsimd

        uid = [0]

        def sb(st, name, shape, dtype=F32):
            uid[0] += 1
            return st.enter_context(nc.sbuf_tensor(f"{name}_u{uid[0]}", shape, dtype))

        PS = [es.enter_context(nc.psum_tensor(f"ps{i}", [128, 512], F32)) for i in range(8)]
        pk = [("ps", i) for i in range(8)]

        ident = sb(es, "ident", [128, 128])
        ones = sb(es, "ones", [128, 128])
        iot = sb(es, "iot", [128, 128])
        cols = sb(es, "cols", [128, NCOLS])
        mcol = sb(es, "mcol", [128, 2, 72, 2])
        gsall = sb(es, "gsall", [128, 2, 3, 2, 8])
        rowsb = sb(es, "rowsb", [128, 1056])
        op("pool", lambda: PO.iota(iot[:], pattern=[[1, 128]], base=0, channel_multiplier=-1,
                                   allow_small_or_imprecise_dtypes=True), w=["iot"])
        op("dve", lambda: V.tensor_scalar(out=ident[:], in0=iot[:], scalar1=0.0, scalar2=None, op0=ALU.is_equal),
           r=["iot"], w=["ident"])
        op("dve", lambda: V.memset(ones[:], 1.0), w=["ones"])

        def load_ffn_w(w1, w2, fi):
            for fb in range(NF // 2):
                for which in range(2):
                    c0 = which * DFF + fb * 256
                    dma("pool", w1[:, :, c0:c0 + 256], ffn_w_in[fi][:, c0:c0 + 256].rearrange("(k p) c -> p k c", p=128),
                        w=[("w1", which, fb)])
            for f in range(NF):
                dma("pool", w2[:, f, :], ffn_w_out[fi][f * 128:(f + 1) * 128, :], w=[("w2", f)])

        pre_stack = contextlib.ExitStack()
        pre_w = (sb(pre_stack, "w1p", [128, 8, 2 * DFF], BF16), sb(pre_stack, "w2p", [128, NF, D], BF16))
        if stage >= 1:
            load_ffn_w(pre_w[0], pre_w[1], 0)

        with contextlib.ExitStack() as ph:
            stg = sb(ph, "stg", [128, 4, 128])
            rows1 = sb(ph, "rows1", [1, 1056])
            sc = sb(ph, "sc", [128, 8, 2])
            was = [sb(ph, f"wa{i}", [128, 8, 512]) for i in range(2)]
            dma("sp", stg[:, :, :], cols_src.rearrange("(g p) f -> p g f", p=128), w=["stg"])
            dma("sp", rows1[:, :], rows_src[:, :], w=["rows1"])
            for g in range(4):
                op("pe", lambda g=g: PE.transpose(out=PS[0][:, g * 128:(g + 1) * 128], in_=stg[:, g, :], identity=ident[:]),
                   r=["stg", "ident"], w=[pk[0]])
            op("dve", lambda: V.tensor_copy(out=cols[:], in_=PS[0][:, :]), r=[pk[0]], w=["cols"])
            for i, (a, b) in enumerate([(0, 512), (512, 1024), (1024, 1056)]):
                op("pe", lambda a=a, b=b, i=i: PE.matmul(PS[1 + i][:, 0:b - a], lhsT=ones[0:1, :], rhs=rows1[0:1, a:b],
                                                         start=True, stop=True), r=["ones", "rows1"], w=[pk[1 + i]])
                op("act", lambda a=a, b=b, i=i: A.copy(out=rowsb[:, a:b], in_=PS[1 + i][:, 0:b - a]), r=[pk[1 + i]], w=["rowsb"])
            op("act", lambda: A.activation(out=sc[:, :, 0], in_=cols[:, C_C:C_C + 8], func=AF.Silu), r=["cols"], w=["sc"])
            op("act", lambda: A.activation(out=sc[:, :, 1], in_=cols[:, C_CC:C_CC + 8], func=AF.Silu), r=["cols"], w=["sc"])
            it = 0
            for l in range(2):
                for cg in range(18):
                    wa = was[it % 2]
                    wk = ("wa", it % 2)
                    pz = 4 + (it % 2)
                    dma("sp", wa[:, :, :], w_ada[l][:, cg * 512:(cg + 1) * 512].rearrange("(k p) c -> p k c", p=128), w=[wk])
                    for cc in range(4):
                        for kk in range(8):
                            op("pe", lambda cc=cc, kk=kk, wa=wa, pz=pz: PE.matmul(
                                PS[pz][:, cc * 2:cc * 2 + 2], lhsT=wa[:, kk, cc * 128:(cc + 1) * 128], rhs=sc[:, kk, :],
                                start=(kk == 0), stop=(kk == 7)), r=[wk, "sc"], w=[pk[pz]])
                    op("dve", lambda l=l, cg=cg, pz=pz: V.tensor_tensor(
                        out=mcol[:, l, cg * 4:(cg + 1) * 4, :], in0=PS[pz][:, 0:8].rearrange("p (c v) -> p c v", v=2),
                        in1=cols[:, C_BADA + l * 72 + cg * 4:C_BADA + l * 72 + cg * 4 + 4].unsqueeze(2).broadcast_to([128, 4, 2]),
                        op=ALU.add), r=[pk[pz], "cols"], w=["mcol"])
                    it += 1
            for l in range(2):
                for s in range(3):
                    for v in range(2):
                        op("dve", lambda l=l, s=s, v=v: V.scalar_tensor_tensor(
                            out=gsall[:, l, s, v, :], in0=mcol[:, l, (s * 3 + 1) * 8:(s * 3 + 2) * 8, v], scalar=1.0,
                            in1=cols[:, C_GSUB + (l * 3 + s) * 8:C_GSUB + (l * 3 + s) * 8 + 8], op0=ALU.add, op1=ALU.mult),
                           r=["mcol", "cols"], w=["gsall"])
            k.barrier()

        def sh_ap(l, s, v):
            return mcol[:, l, (s * 3) * 8:(s * 3) * 8 + 8, v]

        def gate_ap(l, s, v):
            return mcol[:, l, (s * 3 + 2) * 8:(s * 3 + 2) * 8 + 8, v]

        def gs_ap(l, s, v):
            return gsall[:, l, s, v, :]

        def bcast_row(dst, dkey, col_ap, factor, dg):
            for j in range(8):
                op("dve", lambda j=j: V.tensor_scalar(out=dg[:, :], in0=ident[:], scalar1=col_ap[:, j:j + 1], scalar2=float(factor),
                                                      op0=ALU.mult, op1=ALU.mult), r=["ident", "mcol"], w=["dg"])
                op("pe", lambda j=j: PE.matmul(PS[7][:, 0:128], lhsT=ones[:, :], rhs=dg[:, :], start=True, stop=True),
                   r=["dg", "ones"], w=[pk[7]])
                op("act", lambda j=j: A.copy(out=dst[:, j * 128:(j + 1) * 128], in_=PS[7][:, 0:128]), r=[pk[7]], w=[dkey])

        def make_ln(ph):
            S = {"junk": [sb(ph, f"lnj{i}", [128, 1024]) for i in range(2)],
                 "xs": [sb(ph, f"lnx{i}", [128, 1024]) for i in range(2)],
                 "st": [sb(ph, f"lns{i}", [128, 4]) for i in range(2)], "n": 0}

            def ln_A(xap, xkey):
                i = S["n"] % 2
                S["n"] += 1
                junk, xs, st = S["junk"][i], S["xs"][i], S["st"][i]
                kj, kx, ks = ("lnj", i), ("lnx", i), ("lns", i)
                op("act", lambda: A.activation(out=junk[:], in_=xap, func=AF.Square, accum_out=st[:, 0:1]), r=[xkey], w=[kj, ks])
                op("act", lambda: A.activation(out=st[:, 1:2], in_=st[:, 0:1], func=AF.Ln, bias=EPS, scale=1.0 / D), r=[ks], w=[ks])
                op("act", lambda: A.activation(out=st[:, 2:3], in_=st[:, 1:2], func=AF.Exp, scale=-0.5), r=[ks], w=[ks])
                op("dve", lambda: V.tensor_scalar(out=xs[:], in0=xap, scalar1=st[:, 2:3], scalar2=None, op0=ALU.mult),
                   r=[xkey, ks], w=[kx])
                return i

            def ln_B(i, gs, sh, hdst, hkey, pbanks):
                xs = S["xs"][i]
                kx = ("lnx", i)
                for half in range(2):
                    pb = pbanks[half]
                    for j4 in range(4):
                        j = half * 4 + j4
                        op("pe", lambda j=j, j4=j4, pb=pb: PE.transpose(out=PS[pb][:, j4 * 128:(j4 + 1) * 128],
                                                                        in_=xs[:, j * 128:(j + 1) * 128], identity=ident[:]),
                           r=[kx, "ident"], w=[pk[pb]])
                    for j4 in range(4):
                        j = half * 4 + j4
                        op("act", lambda j=j, j4=j4, pb=pb: A.activation(out=hdst(j), in_=PS[pb][:, j4 * 128:(j4 + 1) * 128],
                                                                         func=AF.Identity, scale=gs[:, j:j + 1], bias=sh[:, j:j + 1]),
                           r=[pk[pb], "mcol", "gsall"], w=[hkey])

            def ln_T(xap, xkey, gs, sh, hdst, hkey, pbanks):
                ln_B(ln_A(xap, xkey), gs, sh, hdst, hkey, pbanks)
            ln_T.A = ln_A
            ln_T.B = ln_B
            return ln_T

        def tiles_for(first, with_ctx):
            tl = []
            if with_ctx:
                tl.append((1, ctx_in[:, :] if first else X[0:CTX, :], X[0:CTX, :]))
            for i in range(SEQ // 256):
                src = x_in[i * 256:(i + 1) * 256, :] if first else X[CTX + i * 256:CTX + (i + 1) * 256, :]
                tl.append((0, src, X[CTX + i * 256:CTX + (i + 1) * 256, :]))
            return tl

        def ffn_phase(l, s, fi, first, with_ctx, pre=None):
            with contextlib.ExitStack() as ph:
                if pre is None:
                    w1 = sb(ph, "w1", [128, 8, 2 * DFF], BF16)
                    w2 = sb(ph, "w2", [128, NF, D], BF16)
                else:
                    w1, w2 = pre
                xts = [sb(ph, f"xt{i}", [128, 2, D]) for i in range(2)]
                hT = sb(ph, "hT", [128, 8, 256], BF16)
                actT = sb(ph, "actT", [128, NF, 256], BF16)
                sg = [sb(ph, f"sg{i}", [128, 256]) for i in range(2)]
                tmp = [sb(ph, f"tmp{i}", [128, 512]) for i in range(2)]
                gb = [sb(ph, f"gb{i}", [128, D]) for i in range(2)]
                dg = sb(ph, "dg", [128, 128])
                ln_T = make_ln(ph)
                if pre is None:
                    load_ffn_w(w1, w2, fi)
                for v in range(2):
                    if v == 1 and not with_ctx:
                        continue
                    bcast_row(gb[v], ("gb", v), gate_ap(l, s, v), 0.5, dg)
                n = 0
                tl = tiles_for(first, with_ctx)

                def load(ti):
                    dma("sp", xts[ti % 2][:, :, :], tl[ti][1].rearrange("(b p) d -> p b d", p=128), w=[("xt", ti % 2)])

                def lnA(ti):
                    return [ln_T.A(xts[ti % 2][:, b, :], ("xt", ti % 2)) for b in range(2)]

                load(0)
                hA = lnA(0)
                for ti, (v, src, dst) in enumerate(tl):
                    xt = xts[ti % 2]
                    xk = ("xt", ti % 2)
                    if ti + 1 < len(tl):
                        load(ti + 1)
                    for b in range(2):
                        ln_T.B(hA[b], gs_ap(l, s, v), sh_ap(l, s, v), lambda j, b=b: hT[:, j, b * 128:(b + 1) * 128], "hT", (0, 1))
                    for f in range(NF):
                        pg, pu = 2 + 2 * (f % 2), 3 + 2 * (f % 2)
                        for which, pb in ((0, pg), (1, pu)):
                            c0 = which * DFF + f * 128
                            for kk in range(8):
                                op("pe", lambda kk=kk, pb=pb, c0=c0: PE.matmul(PS[pb][:, 0:256], lhsT=w1[:, kk, c0:c0 + 128],
                                                                               rhs=hT[:, kk, :], start=(kk == 0), stop=(kk == 7)),
                                   r=[("w1", which, f // 2), "hT"], w=[pk[pb]])
                        sgi = sg[f % 2]
                        op("act", lambda pg=pg, sgi=sgi: A.activation(out=sgi[:], in_=PS[pg][:, 0:256], func=AF.Silu),
                           r=[pk[pg]], w=[("sg", f % 2)])
                        op("dve", lambda pu=pu, sgi=sgi, f=f: V.tensor_tensor(out=actT[:, f, :], in0=PS[pu][:, 0:256], in1=sgi[:],
                                                                              op=ALU.mult), r=[pk[pu], ("sg", f % 2)], w=[("actT", f)])
                    if ti + 1 < len(tl):
                        hA = lnA(ti + 1)
                    for tb in range(2):
                        for dh in range(2):
                            po = n % 2
                            tm = tmp[n % 2]
                            n += 1
                            for f in range(NF):
                                op("pe", lambda f=f, po=po, tb=tb, dh=dh: PE.matmul(
                                    PS[po][:, :], lhsT=actT[:, f, tb * 128:(tb + 1) * 128], rhs=w2[:, f, dh * 512:(dh + 1) * 512],
                                    start=(f == 0), stop=(f == NF - 1)), r=[("actT", f), ("w2", f)], w=[pk[po]])
                            op("dve", lambda po=po, tm=tm, dh=dh, v=v: V.tensor_tensor(
                                out=tm[:], in0=PS[po][:, :], in1=gb[v][:, dh * 512:(dh + 1) * 512], op=ALU.mult),
                               r=[pk[po], ("gb", v)], w=[("tmp", po)])
                            op("pool", lambda tm=tm, tb=tb, dh=dh, xt=xt: PO.tensor_tensor(
                                out=xt[:, tb, dh * 512:(dh + 1) * 512], in0=tm[:], in1=xt[:, tb, dh * 512:(dh + 1) * 512], op=ALU.add),
                               r=[("tmp", po), xk], w=[xk])
                    dma("sp", dst.rearrange("(b p) d -> p b d", p=128), xt[:, :, :], r=[xk])
                k.barrier()

        def lru_phase():
            l, s = 0, 1
            cw = lambda j, ch: cols[:, C_LCW + j * 8 + ch:C_LCW + j * 8 + ch + 1]
            cb = lambda ch: cols[:, C_LCB + ch:C_LCB + ch + 1]
            bg = lambda d, g, ch: cols[:, C_LBG + (d * 2 + g) * 8 + ch:C_LBG + (d * 2 + g) * 8 + ch + 1]
            tl = tiles_for(False, True)
            toff = lambda ti: 0 if ti == 0 else CTX + (ti - 1) * 256
            poff = lambda ti: CTX0 if ti == 0 else LAT0 + (ti - 1) * 256
            with contextlib.ExitStack() as ph:
                wi = sb(ph, "lwi", [128, 8, 2048], BF16)
                xts = [sb(ph, f"xt{i}", [128, 2, D]) for i in range(2)]
                hT = sb(ph, "hT", [128, 8, 256], BF16)
                pts = [sb(ph, f"pt{i}", [128, 16, 256]) for i in range(2)]
                ln_T = make_ln(ph)
                for kk in range(8):
                    dma("pool", wi[:, kk, :], lru_w_in[kk * 128:(kk + 1) * 128, :], w=[("lwi", kk)])
                for ti, (v, src, dst) in enumerate(tl):
                    xt = xts[ti % 2]
                    xk = ("xt", ti % 2)
                    pt = pts[ti % 2]
                    ptk = ("pt", ti % 2)
                    dma("sp", xt[:, :, :], src.rearrange("(b p) d -> p b d", p=128), w=[xk])
                    for b in range(2):
                        ln_T(xt[:, b, :], xk, gs_ap(l, s, v), sh_ap(l, s, v),
                             lambda j, b=b: hT[:, j, b * 128:(b + 1) * 128], "hT", (0, 1))
                    for ch in range(16):
                        pb = 2 + ch % 4
                        for kk in range(8):
                            op("pe", lambda kk=kk, pb=pb, ch=ch: PE.matmul(PS[pb][:, 0:256], lhsT=wi[:, kk, ch * 128:(ch + 1) * 128],
                                                                           rhs=hT[:, kk, :], start=(kk == 0), stop=(kk == 7)),
                               r=[("lwi", kk), "hT"], w=[pk[pb]])
                        if ch % 2 == 0:
                            op("act", lambda pb=pb, ch=ch, pt=pt: A.copy(out=pt[:, ch, :], in_=PS[pb][:, 0:256]), r=[pk[pb]], w=[ptk])
                        else:
                            op("dve", lambda pb=pb, ch=ch, pt=pt: V.tensor_copy(out=pt[:, ch, :], in_=PS[pb][:, 0:256]), r=[pk[pb]], w=[ptk])
                    dma("sp", PT[:, :, poff(ti):poff(ti) + 256].rearrange("c p t -> p c t"), pt[:, :, :], r=[ptk])
                k.barrier()
            with contextlib.ExitStack() as ph:
                wg = sb(ph, "wg", [128, 32, 256], BF16)
                coef = sb(ph, "coef", [128, 16])
                coef2 = sb(ph, "coef2", [128, 16])
                ee = sb(ph, "ee", [128, 16])
                pp = sb(ph, "pp", [128, 16])
                upre = sb(ph, "upre", [128, 2, TP])
                u = sb(ph, "u", [128, 2, T])
                ub = sb(ph, "ub", [128, 2, T], BF16)
                hsum = sb(ph, "hsum", [128, 2, T])
                tr = [[sb(ph, f"lt{n}_{i}", [128, 256]) for i in range(5)] for n in range(4)]
                sts = [[sb(ph, f"lst{d}{oc}", [128, 1]) for oc in range(2)] for d in range(2)]
                for q in range(16):
                    dma("pool", wg[:, q * 2:(q + 1) * 2, :], lru_w_gate[q].rearrange("(kc p) j -> p kc j", p=128), w=[("wg", q)])
                op("act", lambda: A.activation(out=ee[:], in_=cols[:, C_LAM:C_LAM + 16], func=AF.Exp, scale=-1.0), r=["cols"], w=["ee"])
                op("dve", lambda: V.memset(pp[:], 1.0 / 12), w=["pp"])
                for n in range(11, 0, -1):
                    op("dve", lambda: V.tensor_tensor(out=pp[:], in0=pp[:], in1=ee[:], op=ALU.mult), r=["pp", "ee"], w=["pp"])
                    op("dve", lambda n=n: V.tensor_scalar(out=pp[:], in0=pp[:], scalar1=-1.0, scalar2=1.0 / n, op0=ALU.mult, op1=ALU.add),
                       r=["pp"], w=["pp"])
                op("dve", lambda: V.scalar_tensor_tensor(out=coef[:], in0=pp[:], scalar=-8.0, in1=ee[:], op0=ALU.mult, op1=ALU.mult),
                   r=["pp", "ee"], w=["coef"])
                op("dve", lambda: V.tensor_scalar(out=coef2[:], in0=coef[:], scalar1=2.0, scalar2=None, op0=ALU.mult), r=["coef"], w=["coef2"])
                for hd in range(4):
                    for (a_, b_) in ((CTX0, CTX0 + CTX), (LAT0, LAT0 + SEQ)):
                        dma("sp", upre[:, :, a_:b_], PT[8 + 2 * hd:8 + 2 * hd + 2, :, a_:b_].rearrange("c p t -> p c t"), w=["upre"])
                    for (a, b) in ((0, 2), (CTX0 + CTX, LAT0), (TP - 1, TP)):
                        op("pool", lambda a=a, b=b: PO.memset(upre[:, :, a:b], 0.0), r=["upre"], w=["upre"])
                    for oc in range(2):
                        ch = 2 * hd + oc
                        for (o0, n, base) in ((0, CTX, CTX0), (CTX, SEQ, LAT0)):
                            op("dve", lambda oc=oc, ch=ch, o0=o0, n=n, base=base: V.tensor_scalar(
                                out=u[:, oc, o0:o0 + n], in0=upre[:, oc, base - 2:base - 2 + n], scalar1=cw(0, ch), scalar2=cb(ch),
                                op0=ALU.mult, op1=ALU.add), r=["upre", "cols"], w=[("u", oc)])
                            for j in range(1, 4):
                                op("dve", lambda oc=oc, ch=ch, o0=o0, n=n, base=base, j=j: V.scalar_tensor_tensor(
                                    out=u[:, oc, o0:o0 + n], in0=upre[:, oc, base - 2 + j:base - 2 + j + n], scalar=cw(j, ch),
                                    in1=u[:, oc, o0:o0 + n], op0=ALU.mult, op1=ALU.add), r=["upre", "cols", ("u", oc)], w=[("u", oc)])
                        op("act", lambda oc=oc: A.copy(out=ub[:, oc, :], in_=u[:, oc, :]), r=[("u", oc)], w=["ub"])
                    for oc in range(2):
                        op("pool", lambda oc=oc: PO.memset(hsum[:, oc, :], 0.0), w=[("hsum", oc)])
                    for d in range(2):
                        for oc in range(2):
                            op("dve", lambda d=d, oc=oc: V.memset(sts[d][oc][:], 0.0), w=[("lst", d, oc)])
                    order = [list(range(17)), [0] + list(range(16, 0, -1))]
                    for step in range(17):
                        chains = [(d, oc) for d in range(2) for oc in range(2)]
                        for ci, (d, oc) in enumerate(chains):
                            ti = order[d][step]
                            t0 = toff(ti)
                            ch = 2 * hd + oc
                            R_, I_ = tr[ci][0], tr[ci][1]
                            pr, pi = 2 * ci, 2 * ci + 1
                            for g, pb in ((0, pr), (1, pi)):
                                for kc in range(2):
                                    op("pe", lambda g=g, pb=pb, kc=kc, d=d, oc=oc, t0=t0: PE.matmul(
                                        PS[pb][:, 0:256], lhsT=wg[:, ((d * 2 + g) * 4 + hd) * 2 + kc, oc * 128:(oc + 1) * 128],
                                        rhs=ub[:, kc, t0:t0 + 256], start=(kc == 0), stop=(kc == 1)),
                                       r=[("wg", (d * 2 + g) * 4 + hd), "ub"], w=[pk[pb]])
                            op("act", lambda: A.activation(out=R_[:], in_=PS[pr][:, 0:256], func=AF.Sigmoid, bias=bg(d, 0, ch)),
                               r=[pk[pr], "cols"], w=[("ltR", ci)])
                            op("act", lambda: A.activation(out=I_[:], in_=PS[pi][:, 0:256], func=AF.Sigmoid, bias=bg(d, 1, ch)),
                               r=[pk[pi], "cols"], w=[("ltI", ci)])
                        for ci, (d, oc) in enumerate(chains):
                            ch = 2 * hd + oc
                            cidx = d * 8 + ch
                            R_, A_, SQ_ = tr[ci][0], tr[ci][2], tr[ci][3]
                            op("act", lambda: A.activation(out=A_[:], in_=R_[:], func=AF.Exp, scale=coef[:, cidx:cidx + 1]),
                               r=[("ltR", ci), "coef"], w=[("ltA", ci)])
                            op("pool", lambda: PO.tensor_tensor(out=SQ_[:], in0=A_[:], in1=A_[:], op=ALU.mult),
                               r=[("ltA", ci)], w=[("ltS", ci)])
                            op("act", lambda: A.activation(out=SQ_[:], in_=SQ_[:], func=AF.Ln, scale=-1.0, bias=1.0),
                               r=[("ltS", ci)], w=[("ltS", ci)])
                            op("act", lambda: A.activation(out=SQ_[:], in_=SQ_[:], func=AF.Exp, scale=0.5),
                               r=[("ltS", ci)], w=[("ltS", ci)])
                        for ci, (d, oc) in enumerate(chains):
                            ti = order[d][step]
                            t0 = toff(ti)
                            I_, A_, SQ_, HB_ = tr[ci][1], tr[ci][2], tr[ci][3], tr[ci][4]
                            stt = sts[d][oc]
                            sk = ("lst", d, oc)
                            hk = ("hsum", oc)
                            op("dve", lambda: V.tensor_tensor(out=I_[:], in0=I_[:], in1=u[:, oc, t0:t0 + 256], op=ALU.mult),
                               r=[("ltI", ci), ("u", oc)], w=[("ltI", ci)])
                            op("dve", lambda: V.tensor_tensor(out=I_[:], in0=I_[:], in1=SQ_[:], op=ALU.mult),
                               r=[("ltI", ci), ("ltS", ci)], w=[("ltI", ci)])
                            if d == 0:
                                op("dve", lambda: V.tensor_tensor_scan(out=HB_[:], data0=A_[:], data1=I_[:],
                                                                       initial=stt[:, 0:1], op0=ALU.mult, op1=ALU.add),
                                   r=[("ltA", ci), ("ltI", ci), sk], w=[("ltH", ci)])
                                op("dve", lambda: V.tensor_copy(out=stt[:, 0:1], in_=HB_[:, 255:256]), r=[("ltH", ci)], w=[sk])
                            else:
                                op("dve", lambda: V.tensor_tensor_scan(out=HB_[:, ::-1], data0=A_[:, ::-1], data1=I_[:, ::-1],
                                                                       initial=stt[:, 0:1], op0=ALU.mult, op1=ALU.add),
                                   r=[("ltA", ci), ("ltI", ci), sk], w=[("ltH", ci)])
                                op("dve", lambda: V.tensor_copy(out=stt[:, 0:1], in_=HB_[:, 0:1]), r=[("ltH", ci)], w=[sk])
                            op("pool", lambda: PO.tensor_tensor(out=hsum[:, oc, t0:t0 + 256], in0=hsum[:, oc, t0:t0 + 256],
                                                                in1=HB_[:], op=ALU.add), r=[("ltH", ci), hk], w=[hk])
                    dma("sp", upre[:, :, 0:CTX], PT[2 * hd:2 * hd + 2, :, CTX0:CTX0 + CTX].rearrange("c p t -> p c t"), w=["upre"])
                    dma("sp", upre[:, :, CTX:T], PT[2 * hd:2 * hd + 2, :, LAT0:LAT0 + SEQ].rearrange("c p t -> p c t"), w=["upre"])
                    for oc in range(2):
                        op("act", lambda oc=oc: A.activation(out=upre[:, oc, 0:T], in_=upre[:, oc, 0:T], func=AF.Gelu_apprx_tanh),
                           r=["upre"], w=["upre"])
                        op("dve", lambda oc=oc: V.tensor_tensor(out=ub[:, oc, :], in0=upre[:, oc, 0:T], in1=hsum[:, oc, :], op=ALU.mult),
                           r=["upre", ("hsum", oc)], w=["ub"])
                    dma("sp", ZT[2 * hd:2 * hd + 2, :, :].rearrange("c p t -> p c t"), ub[:, :, :], r=["ub"])
                k.barrier()
            with contextlib.ExitStack() as ph:
                wo = sb(ph, "lwo", [128, 8, D], BF16)
                gb = [sb(ph, f"gb{i}", [128, D]) for i in range(2)]
                dg = sb(ph, "dg", [128, 128])
                xts = [sb(ph, f"xt{i}", [128, 2, D]) for i in range(2)]
                zts = [sb(ph, f"zt{i}", [128, 8, 256], BF16) for i in range(2)]
                tmp = [sb(ph, f"tmp{i}", [128, 512]) for i in range(2)]
                for c in range(8):
                    dma("pool", wo[:, c, :], lru_w_out[c * 128:(c + 1) * 128, :], w=[("lwo", c)])
                for v in range(2):
                    bcast_row(gb[v], ("gb", v), gate_ap(l, s, v), 1.0, dg)
                n = 0
                for ti, (v, src, dst) in enumerate(tl):
                    xt, xk = xts[ti % 2], ("xt", ti % 2)
                    zt, zk = zts[ti % 2], ("zt", ti % 2)
                    dma("sp", xt[:, :, :], src.rearrange("(b p) d -> p b d", p=128), w=[xk])
                    dma("sp", zt[:, :, :], ZT[:, :, toff(ti):toff(ti) + 256].rearrange("c p t -> p c t"), w=[zk])
                    for tb in range(2):
                        for dh in range(2):
                            po = n % 2
                            tm = tmp[n % 2]
                            n += 1
                            for c in range(8):
                                op("pe", lambda c=c, po=po, tb=tb, dh=dh, zt=zt: PE.matmul(
                                    PS[po][:, :], lhsT=zt[:, c, tb * 128:(tb + 1) * 128], rhs=wo[:, c, dh * 512:(dh + 1) * 512],
                                    start=(c == 0), stop=(c == 7)), r=[zk, ("lwo", c)], w=[pk[po]])
                            op("dve", lambda po=po, tm=tm, dh=dh, v=v: V.tensor_tensor(
                                out=tm[:], in0=PS[po][:, :], in1=gb[v][:, dh * 512:(dh + 1) * 512], op=ALU.mult),
                               r=[pk[po], ("gb", v)], w=[("tmp", po)])
                            op("pool", lambda tm=tm, tb=tb, dh=dh, xt=xt: PO.tensor_tensor(
                                out=xt[:, tb, dh * 512:(dh + 1) * 512], in0=tm[:], in1=xt[:, tb, dh * 512:(dh + 1) * 512], op=ALU.add),
                               r=[("tmp", po), xk], w=[xk])
                    dma("sp", dst.rearrange("(b p) d -> p b d", p=128), xt[:, :, :], r=[xk])
                k.barrier()

        def dn_phase():
            l, s = 1, 1
            toffs = [0] + [CTX + i * 256 for i in range(16)]
            poffs = [CTX0] + [LAT0 + i * 256 for i in range(16)]
            with contextlib.ExitStack() as ph:
                wi = sb(ph, "dwi", [128, 8, 4128], BF16)
                xts = [sb(ph, f"xt{i}", [128, 2, D]) for i in range(2)]
                hT = sb(ph, "hT", [128, 8, 256], BF16)
                pqs = [sb(ph, f"pq{i}", [128, 24, 256], BF16) for i in range(2)]
                pzs = [sb(ph, f"pz{i}", [128, 8, 256]) for i in range(2)]
                gbt = [sb(ph, f"gbt{i}", [128, 2, 32]) for i in range(2)]
                t1 = [sb(ph, f"t1{i}", [128, 16]) for i in range(2)]
                negA = sb(ph, "negA", [128, 16])
                ln_T = make_ln(ph)
                for kk in range(8):
                    dma("pool", wi[:, kk, :], dn_w_in[kk * 128:(kk + 1) * 128, :], w=[("dwi", kk)])
                op("act", lambda: A.activation(out=negA[:], in_=rowsb[:, 0:16], func=AF.Exp), r=["rowsb"], w=["negA"])
                op("dve", lambda: V.tensor_scalar(out=negA[:], in0=negA[:], scalar1=-1.0, scalar2=None, op0=ALU.mult), r=["negA"], w=["negA"])
                nb = 0

                def d1_load(ti):
                    xt, xk = xts[ti % 2], ("xt", ti % 2)
                    if ti == 0:
                        dma("sp", xt[:, :, :], X[0:CTX, :].rearrange("(b p) d -> p b d", p=128), w=[xk])
                    else:
                        for b in range(2):
                            for cl in range(2):
                                col = 4 * (ti - 1) + 2 * b + cl
                                dma("sp", xt[cl * 64:(cl + 1) * 64, b, :], Xlat_cm[col], w=[xk])

                def d1_lnA(ti):
                    return [ln_T.A(xts[ti % 2][:, b, :], ("xt", ti % 2)) for b in range(2)]

                d1_load(0)
                hA = d1_lnA(0)
                for ti in range(17):
                    v = 1 if ti == 0 else 0
                    xt, xk = xts[ti % 2], ("xt", ti % 2)
                    pz, pzk = pzs[ti % 2], ("pz", ti % 2)
                    pq, pqk = pqs[ti % 2], ("pq", ti % 2)
                    if ti + 1 < 17:
                        d1_load(ti + 1)
                    for b in range(2):
                        ln_T.B(hA[b], gs_ap(l, s, v), sh_ap(l, s, v), lambda j, b=b: hT[:, j, b * 128:(b + 1) * 128], "hT", (0, 1))
                    for ch in range(32):
                        pb = 2 + ch % 4
                        for kk in range(8):
                            op("pe", lambda kk=kk, pb=pb, ch=ch: PE.matmul(PS[pb][:, 0:256], lhsT=wi[:, kk, ch * 128:(ch + 1) * 128],
                                                                           rhs=hT[:, kk, :], start=(kk == 0), stop=(kk == 7)),
                               r=[("dwi", kk), "hT"], w=[pk[pb]])
                        if ch >= 24:
                            op("act", lambda pb=pb, ch=ch, pz=pz: A.activation(out=pz[:, ch - 24, :], in_=PS[pb][:, 0:256], func=AF.Silu),
                               r=[pk[pb]], w=[pzk])
                        elif ch % 2 == 0:
                            op("act", lambda pb=pb, ch=ch, pq=pq: A.copy(out=pq[:, ch, :], in_=PS[pb][:, 0:256]), r=[pk[pb]], w=[pqk])
                        else:
                            op("dve", lambda pb=pb, ch=ch, pq=pq: V.tensor_copy(out=pq[:, ch, :], in_=PS[pb][:, 0:256]), r=[pk[pb]], w=[pqk])
                    for b in range(2):
                        pb = 6 + b
                        g_, t_ = gbt[nb % 2], t1[nb % 2]
                        gk, tk = ("gbt", nb % 2), ("t1", nb % 2)
                        nb += 1
                        for kk in range(8):
                            op("pe", lambda kk=kk, pb=pb, b=b: PE.matmul(PS[pb][:, 0:32], lhsT=hT[:, kk, b * 128:(b + 1) * 128],
                                                                         rhs=wi[:, kk, 4096:4128], start=(kk == 0), stop=(kk == 7)),
                               r=[("dwi", kk), "hT"], w=[pk[pb]])
                        op("dve", lambda: V.tensor_tensor(out=t_[:], in0=PS[pb][:, 0:16], in1=rowsb[:, 16:32], op=ALU.add),
                           r=[pk[pb], "rowsb"], w=[tk])
                        op("act", lambda: A.activation(out=t_[:], in_=t_[:], func=AF.Exp), r=[tk], w=[tk])
                        op("act", lambda: A.activation(out=t_[:], in_=t_[:], func=AF.Ln, bias=1.0), r=[tk], w=[tk])
                        op("dve", lambda: V.tensor_tensor(out=g_[:, 0, 0:16], in0=t_[:], in1=negA[:], op=ALU.mult), r=[tk, "negA"], w=[gk])
                        op("act", lambda: A.activation(out=g_[:, 0, 16:32], in_=PS[pb][:, 16:32], func=AF.Sigmoid), r=[pk[pb]], w=[gk])
                        r0 = toffs[ti] + b * 128
                        dma("sp", GD[r0:r0 + 128, :], g_[:, 0, 0:16], r=[gk])
                        dma("sp", BD[r0:r0 + 128, :], g_[:, 0, 16:32], r=[gk])
                    if ti + 1 < 17:
                        hA = d1_lnA(ti + 1)
                    dma("sp", QP[:, :, poffs[ti]:poffs[ti] + 256].rearrange("c p t -> p c t"), pq[:, :, :], r=[pqk])
                    dma("sp", ZD[:, :, toffs[ti]:toffs[ti] + 256].rearrange("c p t -> p c t"), pz[:, :, :], r=[pzk])
                k.barrier()
            with contextlib.ExitStack() as ph:
                pres = [sb(ph, f"pre{i}", [128, TP], BF16) for i in range(2)]
                dws = [sb(ph, f"dw{i}", [128, 4, 128], BF16) for i in range(2)]
                vals = [sb(ph, f"val{i}", [128, T]) for i in range(2)]
                toks = [sb(ph, f"tok{i}", [128, 34, 128]) for i in range(2)]
                sqs = [sb(ph, f"sq{i}", [128, 2048]) for i in range(2)]
                rins = [sb(ph, f"rin{i}", [128, 2048]) for i in range(2)]
                nt = 0
                for c in range(24):
                    pre, prk = pres[c % 2], ("pre", c % 2)
                    val, vk = vals[c % 2], ("val", c % 2)
                    for (a_, b_) in ((CTX0, CTX0 + CTX), (LAT0, LAT0 + SEQ)):
                        dma("sp", pre[:, a_:b_], QP[c][:, a_:b_], w=[prk])
                    for (a, b) in ((0, 2), (CTX0 + CTX, LAT0), (TP - 1, TP)):
                        op("pool", lambda a=a, b=b, pre=pre: PO.memset(pre[:, a:b], 0.0), r=[prk], w=[prk])
                    cw = lambda j: cols[:, C_DCW + j * 24 + c:C_DCW + j * 24 + c + 1]
                    dw, dwk = dws[c % 2], ("dw", c % 2)
                    for j in range(4):
                        op("dve", lambda j=j: V.tensor_scalar(out=dw[:, j, :], in0=ident[:], scalar1=cw(j), scalar2=None, op0=ALU.mult),
                           r=["ident", "cols"], w=[dwk])
                    for (o0, n, base) in ((0, CTX, CTX0), (CTX, SEQ, LAT0)):
                        for a0 in range(0, n, 512):
                            nn = min(512, n - a0)
                            pb = 4 + (nt % 4)
                            nt += 1
                            for j in range(4):
                                op("pe", lambda j=j, pb=pb, a0=a0, nn=nn, base=base: PE.matmul(
                                    PS[pb][:, 0:nn], lhsT=dw[:, j, :], rhs=pre[:, base - 2 + j + a0:base - 2 + j + a0 + nn],
                                    start=(j == 0), stop=(j == 3)), r=[dwk, prk], w=[pk[pb]])
                            op("act", lambda pb=pb, a0=a0, nn=nn, o0=o0: A.activation(out=val[:, o0 + a0:o0 + a0 + nn], in_=PS[pb][:, 0:nn],
                                                                                    func=AF.Silu), r=[pk[pb]], w=[vk])
                    if c < 16:
                        tls = [(a, min(512, T - a)) for a in range(0, T, 512)]
                        for g0 in range(0, len(tls), 4):
                            grp = tls[g0:g0 + 4]
                            sq, sk = sqs[(g0 // 4) % 2], ("sq", (g0 // 4) % 2)
                            rin, rk = rins[(g0 // 4) % 2], ("rin", (g0 // 4) % 2)
                            for gi, (a, n) in enumerate(grp):
                                op("act", lambda gi=gi, a=a, n=n: A.activation(out=sq[:, gi * 512:gi * 512 + n], in_=val[:, a:a + n],
                                                                              func=AF.Square), r=[vk], w=[sk])
                            for gi, (a, n) in enumerate(grp):
                                op("pe", lambda gi=gi, n=n: PE.matmul(PS[gi][:, 0:n], lhsT=ones[:, :], rhs=sq[:, gi * 512:gi * 512 + n],
                                                                      start=True, stop=True), r=[sk, "ones"], w=[pk[gi]])
                            for gi, (a, n) in enumerate(grp):
                                op("act", lambda gi=gi, n=n: A.activation(out=rin[:, gi * 512:gi * 512 + n], in_=PS[gi][:, 0:n],
                                                                          func=AF.Ln, bias=EPS), r=[pk[gi]], w=[rk])
                            n_tot = (len(grp) - 1) * 512 + grp[-1][1]
                            op("act", lambda n_tot=n_tot: A.activation(out=rin[:, 0:n_tot], in_=rin[:, 0:n_tot], func=AF.Exp, scale=-0.5,
                                                                       bias=(math.log(128.0 ** -0.5) if c < 8 else 0.0)), r=[rk], w=[rk])
                            a0 = grp[0][0]
                            op("dve", lambda a0=a0, n_tot=n_tot: V.tensor_tensor(out=val[:, a0:a0 + n_tot], in0=val[:, a0:a0 + n_tot],
                                                                                in1=rin[:, 0:n_tot], op=ALU.mult), r=[vk, rk], w=[vk])
                    if c < 8:
                        dma("sp", QT[c], val[:, :], r=[vk])
                    elif c < 16:
                        dma("sp", KT[c - 8], val[:, :], r=[vk])
                    if c >= 8:
                        tok, tkk = toks[c % 2], ("tok", c % 2)
                        for b0 in range(0, 34, 4):
                            nbk = min(4, 34 - b0)
                            pb = 4 + (b0 // 4) % 4
                            for bb in range(nbk):
                                blk = b0 + bb
                                op("pe", lambda bb=bb, blk=blk, pb=pb: PE.transpose(out=PS[pb][:, bb * 128:(bb + 1) * 128],
                                                                                     in_=val[:, blk * 128:(blk + 1) * 128], identity=ident[:]),
                                   r=[vk, "ident"], w=[pk[pb]])
                            src_ap = PS[pb][:, 0:nbk * 128].rearrange("p (b f) -> p b f", f=128)
                            if (b0 // 4) % 2 == 0:
                                op("act", lambda: A.copy(out=tok[:, b0:b0 + nbk, :], in_=src_ap), r=[pk[pb]], w=[tkk])
                            else:
                                op("dve", lambda: V.tensor_copy(out=tok[:, b0:b0 + nbk, :], in_=src_ap), r=[pk[pb]], w=[tkk])
                        dst = (KTOK[c - 8] if c < 16 else VTOK[c - 16]).rearrange("(b p) f -> p b f", p=128)
                        dma("sp", dst, tok[:, :, :], r=[tkk])
                k.barrier()
            with contextlib.ExitStack() as ph:
                W3 = [64, 8, 64]
                offd_t = sb(ph, "offd", [128, 64])
                mi_t = sb(ph, "mi", [128, 64]); ntm_t = sb(ph, "ntm", [128, 64]); ncm_t = sb(ph, "ncm", [128, 64])
                MI, NT_, NC_, OFFD, I64 = [], [], [], [], []
                for d in range(2):
                    po = d * 64
                    v64 = iot[po:po + 64, po:po + 64]
                    mi, ntm, ncm, offd = mi_t[po:po + 64, :], ntm_t[po:po + 64, :], ncm_t[po:po + 64, :], offd_t[po:po + 64, :]
                    op("dve", lambda: V.tensor_scalar(out=offd, in0=v64, scalar1=0.0, scalar2=None, op0=ALU.not_equal), r=["iot"], w=["msk"])
                    op("dve", lambda: V.tensor_scalar(out=mi, in0=v64, scalar1=0.0, scalar2=None,
                                                      op0=(ALU.is_ge if d == 0 else ALU.is_le)), r=["iot"], w=["msk"])
                    op("dve", lambda: V.tensor_scalar(out=ntm, in0=mi, scalar1=-1.0, scalar2=-NEG, op0=ALU.add, op1=ALU.mult),
                       r=["msk"], w=["msk"])
                    op("dve", lambda: V.tensor_scalar(out=ncm, in0=v64, scalar1=0.0, scalar2=None,
                                                      op0=(ALU.is_lt if d == 0 else ALU.is_gt)), r=["iot"], w=["msk"])
                    op("dve", lambda: V.tensor_scalar(out=ncm, in0=ncm, scalar1=-1.0, scalar2=-NEG, op0=ALU.add, op1=ALU.mult),
                       r=["msk"], w=["msk"])
                    MI.append(mi); NT_.append(ntm); NC_.append(ncm); OFFD.append(offd); I64.append(ident[po:po + 64, po:po + 64])
                bcm = lambda m: m.unsqueeze(1).broadcast_to(W3)
                bc = lambda ap2, n: ap2.unsqueeze(2).broadcast_to([ap2.shape[0], 8, n])
                TL = [[None, None], [None, None]]
                for par in range(2):
                    big = {}
                    for nm in ("ktok", "vtok", "kbg", "vb", "kd", "vn"):
                        big[nm] = sb(ph, f"{nm}B{par}", [128, 8, 128])
                    for nm in ("gM", "bdg", "E1", "E2", "DTi", "DCs", "DTs", "Q0", "P0", "Q1", "P1", "QKm", "R", "Rr", "tmpw"):
                        big[nm] = sb(ph, f"{nm}B{par}", [128, 8, 64])
                    for nm in ("g16", "b16"):
                        big[nm] = sb(ph, f"{nm}B{par}", [128, 16])
                    for nm in ("gcc", "egc", "bco"):
                        big[nm] = sb(ph, f"{nm}B{par}", [128, 8])
                    for d in range(2):
                        t = {}
                        for nm in ("kT", "qT", "qdT", "wTn", "osb", "egr"):
                            t[nm] = sb(ph, f"{nm}{d}{par}", [128, 8, 64])
                        for nm in big:
                            t[nm] = big[nm][d * 64:(d + 1) * 64]
                        TL[d][par] = t
                for d in range(2):
                    S_ = sb(ph, f"S{d}", [128, 8, 128])
                    TL[d][0]["S"] = S_
                    TL[d][1]["S"] = S_
                    t = TL[d][0]
                    op("pool", lambda t=t: PO.memset(t["egr"][:, :, :], 0.0), w=[("egr", d, 0)])
                    for h in range(8):
                        op("dve", lambda t=t, h=h: V.tensor_copy(out=t["S"][:, h, :].bitcast(F32R), in_=t["egr"][:, 0:2, :].rearrange("p a b -> p (a b)")),
                           r=[("egr", d, 0)], w=[("S", d, h)])
                w3 = lambda ps_, po: ps_[po:po + 64, :].rearrange("p (h j) -> p h j", h=8)
                fl = lambda tl: tl[:, :, :].rearrange("p h j -> p (h j)")

                def dn_A(d, t0, is_lat, par):
                    t = TL[d][par]
                    po = d * 64
                    i64, offd = I64[d], OFFD[d]
                    rr = lambda ap: ap.bitcast(F32R)
                    rq = rr if d == 0 else (lambda ap: ap)
                    K_ = lambda nm: (nm, d, par)
                    T1a, T1b, T2a, T2b = PS[d * 4], PS[d * 4 + 1], PS[d * 4 + 2], PS[d * 4 + 3]
                    k1a, k1b, k2a, k2b = pk[d * 4], pk[d * 4 + 1], pk[d * 4 + 2], pk[d * 4 + 3]
                    last = 63 if d == 0 else 0
                    gd = t["g16"][:, d * 8:(d + 1) * 8]
                    bd = t["b16"][:, d * 8:(d + 1) * 8]
                    dma("sp", t["kT"][:, :, :], KT[:, :, t0:t0 + 64].rearrange("h p t -> p h t"), w=[K_("kT")])
                    dma("sp", t["qT"][:, :, :], QT[:, :, t0:t0 + 64].rearrange("h p t -> p h t"), w=[K_("qT")])
                    dma("sp", t["ktok"][:, :, :], KTOK[:, t0:t0 + 64, :].rearrange("h t f -> t h f"), w=[K_("ktok")])
                    dma("sp", t["vtok"][:, :, :], VTOK[:, t0:t0 + 64, :].rearrange("h t f -> t h f"), w=[K_("vtok")])
                    dma("sp", t["g16"][:, :], GD[t0:t0 + 64, :], w=[K_("g16")])
                    dma("sp", t["b16"][:, :], BD[t0:t0 + 64, :], w=[K_("b16")])
                    yield
                    op("dve", lambda: V.tensor_tensor(out=t["gM"][:, :, :], in0=bcm(MI[d]), in1=bc(gd, 64), op=ALU.mult),
                       r=["msk", K_("g16")], w=[K_("gM")])
                    op("pe", lambda: PE.matmul(T1a[:, :], lhsT=ones[po:po + 64, :], rhs=fl(t["gM"]), start=True, stop=True),
                       r=["ones", K_("gM")], w=[k1a])
                    op("pe", lambda: PE.matmul(T2b[po:po + 64, 0:8], lhsT=MI[d], rhs=gd, start=True, stop=True),
                       r=["msk", K_("g16")], w=[k2b])
                    op("dve", lambda: V.tensor_tensor(out=t["bdg"][:, :, :], in0=bcm(i64), in1=bc(bd, 64), op=ALU.mult),
                       r=["ident", K_("b16")], w=[K_("bdg")])
                    op("pe", lambda: PE.matmul(T2a[po:po + 64, :], lhsT=ones[po:po + 64, 0:64], rhs=fl(t["bdg"]), start=True, stop=True),
                       r=["ones", K_("bdg")], w=[k2a])
                    op("act", lambda: A.copy(out=t["gcc"][:, :], in_=T2b[po:po + 64, 0:8]), r=[k2b], w=[K_("gcc")])
                    op("dve", lambda: V.tensor_tensor(out=t["E1"][:, :, :], in0=w3(T1a, po), in1=bc(t["gcc"][:, :], 64), op=ALU.subtract),
                       r=[k1a, K_("gcc")], w=[K_("E1")])
                    op("dve", lambda: V.tensor_tensor(out=t["E2"][:, :, :], in0=bc(t["gcc"][:, :], 64), in1=w3(T1a, po), op=ALU.subtract),
                       r=[k1a, K_("gcc")], w=[K_("E2")])
                    op("dve", lambda: V.scalar_tensor_tensor(out=t["E1"][:, :, :], in0=t["E1"][:, :, :], scalar=0.0, in1=bcm(NT_[d]),
                                                             op0=ALU.min, op1=ALU.add), r=[K_("E1"), "msk"], w=[K_("E1")])
                    op("dve", lambda: V.scalar_tensor_tensor(out=t["E2"][:, :, :], in0=t["E2"][:, :, :], scalar=0.0, in1=bcm(NC_[d]),
                                                             op0=ALU.min, op1=ALU.add), r=[K_("E2"), "msk"], w=[K_("E2")])
                    op("act", lambda: A.activation(out=t["DTi"][:, :, :], in_=t["E1"][:, :, :], func=AF.Exp), r=[K_("E1")], w=[K_("DTi")])
                    op("act", lambda: A.activation(out=t["DCs"][:, :, :], in_=t["E2"][:, :, :], func=AF.Exp), r=[K_("E2")], w=[K_("DCs")])
                    op("act", lambda: A.activation(out=fl(t["egr"]), in_=T1a[:, :], func=AF.Exp), r=[k1a], w=[K_("egr")])
                    op("act", lambda: A.activation(out=t["egc"][:, :], in_=t["gcc"][:, :], func=AF.Exp), r=[K_("gcc")], w=[K_("egc")])
                    yield
                    for h in range(8):
                        op("pe", lambda h=h: PE.matmul(T2b[po:po + 64, h * 64:(h + 1) * 64], lhsT=t["kT"][:, h, :], rhs=t["kT"][:, h, :],
                                                       start=True, stop=True), r=[K_("kT")], w=[k2b])
                    for h in range(8):
                        op("pe", lambda h=h: PE.matmul(T1a[po:po + 64, h * 64:(h + 1) * 64], lhsT=t["kT"][:, h, :], rhs=t["qT"][:, h, :],
                                                       start=True, stop=True), r=[K_("kT"), K_("qT")], w=[k1a])
                    yield
                    op("pool", lambda: PO.tensor_tensor(out=t["DTs"][:, :, :], in0=t["DTi"][:, :, :], in1=bcm(offd), op=ALU.mult),
                       r=[K_("DTi"), "msk"], w=[K_("DTs")])
                    op("dve", lambda: V.tensor_tensor(out=t["tmpw"][:, :, :], in0=w3(T2b, po), in1=t["DTs"][:, :, :], op=ALU.mult),
                       r=[k2b, K_("DTs")], w=[K_("tmpw")])
                    op("dve", lambda: V.tensor_tensor(out=t["Q0"][:, :, :], in0=w3(T2a, po), in1=t["tmpw"][:, :, :], op=ALU.mult),
                       r=[k2a, K_("tmpw")], w=[K_("Q0")])
                    op("dve", lambda: V.tensor_tensor(out=t["tmpw"][:, :, :], in0=w3(T2b, po), in1=t["DCs"][:, :, :], op=ALU.mult),
                       r=[k2b, K_("DCs"), K_("tmpw")], w=[K_("tmpw")])
                    op("dve", lambda: V.tensor_tensor(out=t["P0"][:, :, :], in0=t["tmpw"][:, :, :], in1=bc(bd, 64), op=ALU.mult),
                       r=[K_("tmpw"), K_("b16")], w=[K_("P0")])
                    op("dve", lambda: V.tensor_tensor(out=rr(t["QKm"][:, :, :]), in0=w3(T1a, po), in1=t["DTi"][:, :, :], op=ALU.mult),
                       r=[k1a, K_("DTi")], w=[K_("QKm")])
                    op("dve", lambda: V.tensor_tensor(out=t["R"][:, :, :], in0=bcm(i64), in1=t["Q0"][:, :, :], op=ALU.subtract),
                       r=["ident", K_("Q0")], w=[K_("R")])
                    yield
                    Qc, Pc, Qn, Pn = "Q0", "P0", "Q1", "P1"
                    for lev in range(1, 6):
                        for h in range(8):
                            op("pe", lambda h=h: PE.matmul(T2a[po:po + 64, h * 64:(h + 1) * 64], lhsT=t[Qc][:, h, :], rhs=t[Pc][:, h, :],
                                                           start=True, stop=True), r=[K_(Qc), K_(Pc)], w=[k2a])
                        if lev < 5:
                            for h in range(8):
                                op("pe", lambda h=h: PE.matmul(T1a[po:po + 64, h * 64:(h + 1) * 64], lhsT=t[Pc][:, h, :], rhs=t[Qc][:, h, :],
                                                               start=True, stop=True), r=[K_(Qc), K_(Pc)], w=[k1a])
                        yield
                        op("act", lambda: A.copy(out=t[Pn][:, :, :], in_=w3(T2a, po)), r=[k2a], w=[K_(Pn)])
                        if lev < 5:
                            op("dve", lambda: V.tensor_copy(out=t[Qn][:, :, :], in_=w3(T1a, po)), r=[k1a], w=[K_(Qn)])
                        for h in range(8):
                            op("pe", lambda h=h: PE.matmul(T2b[po:po + 64, h * 64:(h + 1) * 64], lhsT=t[Pn][:, h, :], rhs=t["R"][:, h, :],
                                                           start=True, stop=True), r=[K_(Pn), K_("R")], w=[k2b])
                        op("dve", lambda: V.tensor_tensor(out=t["R"][:, :, :], in0=w3(T2b, po), in1=t["R"][:, :, :], op=ALU.add),
                           r=[k2b, K_("R")], w=[K_("R")])
                        Qc, Pc, Qn, Pn = Qn, Pn, Qc, Pc
                        yield

                def dn_B(d, t0, is_lat, par):
                    t = TL[d][par]
                    po = d * 64
                    i64, offd = I64[d], OFFD[d]
                    rr = lambda ap: ap.bitcast(F32R)
                    rq = rr if d == 0 else (lambda ap: ap)
                    K_ = lambda nm: (nm, d, par)
                    T1a, T1b, T2a, T2b = PS[d * 4], PS[d * 4 + 1], PS[d * 4 + 2], PS[d * 4 + 3]
                    k1a, k1b, k2a, k2b = pk[d * 4], pk[d * 4 + 1], pk[d * 4 + 2], pk[d * 4 + 3]
                    last = 63 if d == 0 else 0
                    gd = t["g16"][:, d * 8:(d + 1) * 8]
                    bd = t["b16"][:, d * 8:(d + 1) * 8]
                    op("act", lambda: A.copy(out=rr(t["Rr"][:, :, :]), in_=t["R"][:, :, :]), r=[K_("R")], w=[K_("Rr")])
                    op("dve", lambda: V.tensor_tensor(out=t["bco"][:, :], in0=bd, in1=t["egc"][:, :], op=ALU.mult),
                       r=[K_("b16"), K_("egc")], w=[K_("bco")])
                    op("dve", lambda: V.tensor_tensor(out=rr(t["kbg"][:, :, :]), in0=t["ktok"][:, :, :], in1=bc(t["bco"][:, :], 128), op=ALU.mult),
                       r=[K_("ktok"), K_("bco")], w=[K_("kbg")])
                    op("dve", lambda: V.tensor_tensor(out=rr(t["vb"][:, :, :]), in0=t["vtok"][:, :, :], in1=bc(bd, 128), op=ALU.mult),
                       r=[K_("vtok"), K_("b16")], w=[K_("vb")])
                    op("dve", lambda: V.tensor_tensor(out=rr(t["kd"][:, :, :]), in0=t["ktok"][:, :, :],
                                                        in1=bc(t["DTi"][:, :, last], 128), op=ALU.mult),
                       r=[K_("ktok"), K_("DTi")], w=[K_("kd")])
                    op("dve", lambda: V.tensor_tensor(out=rr(t["qdT"][:, :, :]), in0=t["qT"][:, :, :], in1=t["egr"][:, :, :], op=ALU.mult),
                       r=[K_("qT"), K_("egr")], w=[K_("qdT")])
                    yield
                    for h in range(8):
                        op("pe", lambda h=h: PE.matmul(T1b[:, h * 64:(h + 1) * 64], lhsT=rr(t["kbg"][:, h, :]), rhs=rr(t["Rr"][:, h, :]),
                                                       start=True, stop=True), r=[K_("kbg"), K_("Rr")], w=[k1b])
                    yield
                    op("act", lambda: A.activation(out=rr(fl(t["wTn"])), in_=T1b[:, :], func=AF.Copy, scale=-1.0), r=[k1b], w=[K_("wTn")])
                    for half in range(2):
                        for h in range(half * 4, half * 4 + 4):
                            o_ = T1b[po:po + 64, (h % 4) * 128:(h % 4 + 1) * 128]
                            op("pe", lambda h=h, o_=o_: PE.matmul(o_, lhsT=rq(t["Rr"][:, h, :]), rhs=rq(t["vb"][:, h, :]), start=True, stop=False),
                               r=[K_("Rr"), K_("vb")], w=[k1b])
                            op("pe", lambda h=h, o_=o_: PE.matmul(o_, lhsT=rq(t["wTn"][:, h, :]), rhs=rq(t["S"][:, h, :]), start=False, stop=True),
                               r=[K_("wTn"), ("S", d, h)], w=[k1b])
                        src_ = T1b[po:po + 64, :].rearrange("p (h f) -> p h f", h=4)
                        if half == 0:
                            op("act", lambda: A.copy(out=rr(t["vn"][:, 0:4, :]), in_=src_), r=[k1b], w=[K_("vn")])
                        else:
                            op("dve", lambda: V.tensor_copy(out=rr(t["vn"][:, 4:8, :]), in_=src_), r=[k1b], w=[K_("vn")])
                        yield
                    if is_lat:
                        for h in range(8):
                            o_ = T1b[:, h * 64:(h + 1) * 64]
                            op("pe", lambda h=h, o_=o_: PE.matmul(o_, lhsT=rr(t["S"][:, h, :]), rhs=rr(t["qdT"][:, h, :]), start=True, stop=False),
                               r=[("S", d, h), K_("qdT")], w=[k1b])
                            op("pe", lambda h=h, o_=o_: PE.matmul(o_, lhsT=rr(t["vn"][:, h, :]), rhs=rr(t["QKm"][:, h, :]), start=False, stop=True),
                               r=[K_("vn"), K_("QKm")], w=[k1b])
                        op("act", lambda: A.copy(out=fl(t["osb"]), in_=T1b[:, :]), r=[k1b], w=[K_("osb")])
                        s0 = t0 - CTX
                        dma("sp", OFB[d][:, :, s0:s0 + 64].rearrange("h p t -> p h t"), t["osb"][:, :, :], r=[K_("osb")])
                        yield
                    for half in range(2):
                        for h in range(half * 4, half * 4 + 4):
                            o_ = T1b[:, (h % 4) * 128:(h % 4 + 1) * 128]
                            op("pe", lambda h=h, o_=o_: PE.matmul(o_, lhsT=rr(t["kd"][:, h, :]), rhs=rr(t["vn"][:, h, :]), start=True, stop=True),
                               r=[K_("kd"), K_("vn")], w=[k1b])
                        for h in range(half * 4, half * 4 + 4):
                            o_ = T1b[:, (h % 4) * 128:(h % 4 + 1) * 128]
                            op("dve", lambda h=h, o_=o_: V.scalar_tensor_tensor(out=rr(t["S"][:, h, :]), in0=t["S"][:, h, :],
                                                                                scalar=t["egr"][:, h, last:last + 1], in1=o_,
                                                                                op0=ALU.mult, op1=ALU.add),
                               r=[("S", d, h), K_("egr"), k1b], w=[("S", d, h)])
                        yield

                fw = [(64 * j, False) for j in range(4)] + [(CTX + 64 * j, True) for j in range(64)]
                bw = [(64 * j, False) for j in range(3, -1, -1)] + [(CTX + 64 * j, True) for j in range(63, -1, -1)]
                for step in range(69):
                    gens = []
                    if step >= 1:
                        gens += [dn_B(0, *fw[step - 1], (step - 1) % 2), dn_B(1, *bw[step - 1], (step - 1) % 2)]
                    if step < 68:
                        gens += [dn_A(0, *fw[step], step % 2), dn_A(1, *bw[step], step % 2)]
                    while gens:
                        for g in list(gens):
                            try:
                                next(g)
                            except StopIteration:
                                gens.remove(g)
                k.barrier()
            with contextlib.ExitStack() as ph:
                wo = sb(ph, "dwo", [128, 8, D], BF16)
                gb0 = sb(ph, "gb0", [128, D])
                dg = sb(ph, "dg", [128, 128])
                xts = [sb(ph, f"xt{i}", [128, 2, D]) for i in range(2)]
                ofs = [sb(ph, f"of{i}", [128, 8, 256]) for i in range(2)]
                obs = [sb(ph, f"ob{i}", [128, 8, 256]) for i in range(2)]
                zss = [sb(ph, f"zs{i}", [128, 8, 256]) for i in range(2)]
                sq4 = [sb(ph, f"rsq{i}", [128, 2048]) for i in range(2)]
                rr4 = [sb(ph, f"rr{i}", [128, 2048]) for i in range(2)]
                yb4 = [sb(ph, f"yb{i}", [128, 8, 256], BF16) for i in range(2)]
                tmp = [sb(ph, f"tmp{i}", [128, 512]) for i in range(2)]
                for c in range(8):
                    dma("pool", wo[:, c, :], dn_w_out[c * 128:(c + 1) * 128, :], w=[("dwo", c)])
                bcast_row(gb0, "gb0", gate_ap(l, s, 0), 1.0, dg)
                gn = cols[:, C_GN:C_GN + 1]
                n = 0
                for ti in range(16):
                    xt, xk = xts[ti % 2], ("xt", ti % 2)
                    of, ofk = ofs[ti % 2], ("of", ti % 2)
                    ob, obk = obs[ti % 2], ("ob", ti % 2)
                    zs, zk = zss[ti % 2], ("zs", ti % 2)
                    s0 = ti * 256
                    for b in range(2):
                        for cl in range(2):
                            dma("sp", xt[cl * 64:(cl + 1) * 64, b, :], Xlat_cm[4 * ti + 2 * b + cl], w=[xk])
                    dma("sp", of[:, :, :], OFB[0][:, :, s0:s0 + 256].rearrange("h p t -> p h t"), w=[ofk])
                    dma("sp", ob[:, :, :], OFB[1][:, :, s0:s0 + 256].rearrange("h p t -> p h t"), w=[obk])
                    dma("sp", zs[:, :, :], ZD[:, :, CTX + s0:CTX + s0 + 256].rearrange("c p t -> p c t"), w=[zk])
                    sq, rr, yb = sq4[ti % 2], rr4[ti % 2], yb4[ti % 2]
                    off = of[:, :, :].rearrange("p h t -> p (h t)")
                    op("dve", lambda: V.tensor_tensor(out=off, in0=off, in1=ob[:, :, :].rearrange("p h t -> p (h t)"), op=ALU.add),
                       r=[ofk, obk], w=[ofk])
                    op("act", lambda: A.activation(out=sq[:, :], in_=off, func=AF.Square), r=[ofk], w=[("rsq", ti % 2)])
                    for q in range(4):
                        pb = 4 + q
                        op("pe", lambda q=q, pb=pb: PE.matmul(PS[pb][:, :], lhsT=ones[:, :], rhs=sq[:, q * 512:(q + 1) * 512],
                                                              start=True, stop=True), r=[("rsq", ti % 2), "ones"], w=[pk[pb]])
                        op("act", lambda q=q, pb=pb: A.activation(out=rr[:, q * 512:(q + 1) * 512], in_=PS[pb][:, :], func=AF.Ln,
                                                                  bias=EPS, scale=1.0 / 128), r=[pk[pb]], w=[("rr", ti % 2)])
                    op("act", lambda: A.activation(out=rr[:, :], in_=rr[:, :], func=AF.Exp, scale=-0.5), r=[("rr", ti % 2)], w=[("rr", ti % 2)])
                    op("dve", lambda: V.tensor_tensor(out=off, in0=off, in1=rr[:, :], op=ALU.mult), r=[ofk, ("rr", ti % 2)], w=[ofk])
                    op("dve", lambda: V.scalar_tensor_tensor(out=yb[:, :, :].rearrange("p h t -> p (h t)"), in0=off, scalar=gn,
                                                             in1=zs[:, :, :].rearrange("p h t -> p (h t)"), op0=ALU.mult, op1=ALU.mult),
                       r=[ofk, zk, "cols"], w=[("yb", ti % 2)])
                    for tb in range(2):
                        for dh in range(2):
                            po = n % 2
                            tm = tmp[n % 2]
                            n += 1
                            for c in range(8):
                                op("pe", lambda c=c, po=po, tb=tb, dh=dh: PE.matmul(
                                    PS[po][:, :], lhsT=yb[:, c, tb * 128:(tb + 1) * 128], rhs=wo[:, c, dh * 512:(dh + 1) * 512],
                                    start=(c == 0), stop=(c == 7)), r=[("yb", ti % 2), ("dwo", c)], w=[pk[po]])
                            op("dve", lambda po=po, tm=tm, dh=dh: V.tensor_tensor(
                                out=tm[:], in0=PS[po][:, :], in1=gb0[:, dh * 512:(dh + 1) * 512], op=ALU.mult),
                               r=[pk[po], "gb0"], w=[("tmp", po)])
                            op("pool", lambda tm=tm, tb=tb, dh=dh, xt=xt: PO.tensor_tensor(
                                out=xt[:, tb, dh * 512:(dh + 1) * 512], in0=tm[:], in1=xt[:, tb, dh * 512:(dh + 1) * 512], op=ALU.add),
                               r=[("tmp", po), xk], w=[xk])
                    for b in range(2):
                        for cl in range(2):
                            dma("sp", Xlat_cm[4 * ti + 2 * b + cl], xt[cl * 64:(cl + 1) * 64, b, :], r=[xk])
                k.barrier()

        def final_phase(raw):
            with contextlib.ExitStack() as ph:
                xts = [sb(ph, f"fx{i}", [128, 2, D]) for i in range(2)]
                junk = sb(ph, "fj", [128, D])
                sts = [sb(ph, f"fs{i}", [128, 4]) for i in range(2)]
                for ti in range(SEQ // 256):
                    xt = xts[ti % 2]
                    xk = ("fx", ti % 2)
                    dma("sp", xt[:, :, :], X[CTX + ti * 256:CTX + (ti + 1) * 256, :].rearrange("(b p) d -> p b d", p=128), w=[xk])
                    if not raw:
                        for b in range(2):
                            st = sts[b]
                            ks = ("fs", b)
                            op("act", lambda b=b, st=st: A.activation(out=junk[:], in_=xt[:, b, :], func=AF.Square, accum_out=st[:, 0:1]),
                               r=[xk], w=["fj", ks])
                            op("act", lambda st=st: A.activation(out=st[:, 1:2], in_=st[:, 0:1], func=AF.Ln, bias=EPS, scale=1.0 / D),
                               r=[ks], w=[ks])
                            op("act", lambda st=st: A.activation(out=st[:, 2:3], in_=st[:, 1:2], func=AF.Exp, scale=-0.5), r=[ks], w=[ks])
                            op("dve", lambda b=b, st=st: V.scalar_tensor_tensor(out=xt[:, b, :], in0=xt[:, b, :], scalar=st[:, 2:3],
                                                                               in1=rowsb[:, 32:1056], op0=ALU.mult, op1=ALU.mult),
                               r=[xk, ks, "rowsb"], w=[xk])
                    dma("sp", out[ti * 256:(ti + 1) * 256, :].rearrange("(b p) d -> p b d", p=128), xt[:, :, :], r=[xk])
                k.barrier()

        stages = []
        def ffn_first():
            ffn_phase(0, 0, 0, True, True, pre=pre_w)
            pre_stack.close()
        stages.append(ffn_first)
        stages.append(lru_phase)
        stages.append(lambda: ffn_phase(0, 2, 1, False, True))
        stages.append(lambda: ffn_phase(1, 0, 2, False, True))
        stages.append(dn_phase)
        stages.append(lambda: ffn_phase(1, 2, 3, False, False))
        from_lru = len(stages)
        nst = 0
        for f in stages:
            if nst >= stage:
                break
            f()
            nst += 1
        if stage < 1:
            pre_stack.close()
        final_phase(raw=dbg)
    return nc


def host_inputs(inputs, b):
    f = lambda a: np.ascontiguousarray(np.asarray(a, dtype=np.float32))
    cols = np.zeros((NCOLS, 128), np.float32)
    cols[C_C:C_C + 8] = f(inputs["c"])[b].reshape(8, 128)
    cols[C_CC:C_CC + 8] = f(inputs["c_ctx"]).reshape(8, 128)
    cols[C_BADA:C_BADA + 144] = f(inputs["b_ada"]).reshape(144, 128)
    cols[C_GSUB:C_GSUB + 48] = f(inputs["g_sub"]).reshape(48, 128)
    cols[C_LCW:C_LCW + 32] = f(inputs["lru_conv_w"])[0].reshape(32, 128)
    cols[C_LCB:C_LCB + 8] = f(inputs["lru_conv_b"])[0].reshape(8, 128)
    cols[C_LBG:C_LBG + 32] = f(inputs["lru_b_gate"])[0].reshape(32, 128)
    cols[C_LAM:C_LAM + 16] = f(inputs["lru_lambda"])[0].reshape(16, 128)
    cols[C_DCW:C_DCW + 96] = f(inputs["dn_conv_w"])[0].reshape(96, 128)
    cols[C_GN:C_GN + 1] = f(inputs["dn_g_norm"])[0].reshape(1, 128)
    rows = np.concatenate([f(inputs["dn_a_log"])[0].reshape(16), f(inputs["dn_dt_bias"])[0].reshape(16),
                           f(inputs["g_final"]).reshape(1024)]).reshape(1, 1056)
    return {
        "x": f(inputs["x"])[b], "ctx": f(inputs["ctx"])[b], "cols_src": cols, "rows_src": rows,
        "w_ada": f(inputs["w_ada"]), "ffn_w_in": f(inputs["ffn_w_in"]).reshape(4, D, 2 * DFF),
        "ffn_w_out": f(inputs["ffn_w_out"]).reshape(4, DFF, D), "lru_w_in": f(inputs["lru_w_in"])[0],
        "lru_w_gate": f(inputs["lru_w_gate"])[0].reshape(16, 256, 256), "lru_w_out": f(inputs["lru_w_out"])[0],
        "dn_w_in": f(inputs["dn_w_in"])[0], "dn_w_out": f(inputs["dn_w_out"])[0],
    }


def kernel(**inputs):
    nc = build()
    in_maps = [host_inputs(inputs, c % 4) for c in range(8)]
    res = run_bass_kernel_spmd(nc, in_maps, core_ids=list(range(8)))
    return np.stack([np.asarray(res.results[b]["out"], dtype=np.float32) for b in range(4)], axis=0)
```
